# Optimizing a Trainium2 kernel written in Bass

```python
import math
import jax, jax.numpy as jnp
from jax import lax
import numpy as np

D_MODEL = 1024
BATCH = 8
SEQ = 4096
DEPTH = 4

N_MIXERS = 4
D_FF = 4 * D_MODEL
LN_EPS = 1e-5
RMS_EPS = 1e-6
GN_EPS = 1e-5
DN_ALPHA = (2.0 * DEPTH) ** 0.25
DN_BETA = (8.0 * DEPTH) ** -0.25
BLOCK = 128
NEG = -1e30

RET_HEADS = 4
RET_QK_DIM = D_MODEL // RET_HEADS
RET_V_DIM = 2 * RET_QK_DIM
RET_CHUNK = 128
RET_THETA = 10000.0

DIL_PAIRS = ((128, 1), (512, 4), (2048, 16))
DIL_HEADS = 8
DIL_HEAD_DIM = 128
DIL_ROT = DIL_HEAD_DIM // 4
ROPE_THETA = 500000.0

MLA_HEADS = 16
MLA_NOPE = 128
MLA_ROPE = 64
MLA_V = 128
MLA_Q_RANK = 256
MLA_KV_RANK = 128
MLA_THETA = 10000.0

RWKV_HEAD = 64
RWKV_HEADS = D_MODEL // RWKV_HEAD
RWKV_LORA = max(32, int(round(1.8 * D_MODEL ** 0.5 / 32)) * 32)
RWKV_GATE_LORA = max(32, int(round(0.6 * D_MODEL ** 0.8 / 32)) * 32)
RWKV_GN_EPS = 64e-5

kernel_name = "hybrid_interleaved_ret_dilswa_mla_rwkv7"


def _n_occ(m):
    return (DEPTH - m + N_MIXERS - 1) // N_MIXERS


def _layer_norm(x, g, b):
    xf = x.astype(jnp.float32)
    mu = xf.mean(-1, keepdims=True)
    var = jnp.square(xf - mu).mean(-1, keepdims=True)
    return ((xf - mu) * lax.rsqrt(var + LN_EPS) * g + b).astype(x.dtype)


def _rms_norm(x, g):
    xf = x.astype(jnp.float32)
    return (xf * lax.rsqrt(jnp.mean(jnp.square(xf), -1, keepdims=True) + RMS_EPS) * g).astype(x.dtype)


def _head_norm(o, g, b, eps):
    mu = o.mean(-1, keepdims=True)
    var = jnp.square(o - mu).mean(-1, keepdims=True)
    on = (o - mu) * lax.rsqrt(var + eps)
    return on.reshape(o.shape[0], o.shape[1], -1) * g + b


def _rope(x, pos, rot_dim, theta):
    half = rot_dim // 2
    inv_freq = theta ** (-jnp.arange(half, dtype=jnp.float32) / half)
    ang = pos.astype(jnp.float32)[:, None] * inv_freq[None, :]
    cos = jnp.cos(ang)[:, None, :]
    sin = jnp.sin(ang)[:, None, :]
    x1 = x[..., :half].astype(jnp.float32)
    x2 = x[..., half:rot_dim].astype(jnp.float32)
    rot = jnp.concatenate([x1 * cos - x2 * sin, x2 * cos + x1 * sin], axis=-1).astype(x.dtype)
    return jnp.concatenate([rot, x[..., rot_dim:]], axis=-1)


def _retention(x, w_in, gn, w_out):
    B, S, _ = x.shape
    H, dk, dv, C = RET_HEADS, RET_QK_DIM, RET_V_DIM, RET_CHUNK
    n_chunks = S // C
    pos = jnp.arange(S)
    q, k, v, g = jnp.split(x @ w_in, [H * dk, 2 * H * dk, 2 * H * dk + H * dv], axis=-1)
    q = _rope(q.reshape(B, S, H, dk), pos, dk, RET_THETA)
    k = _rope(k.reshape(B, S, H, dk), pos, dk, RET_THETA) * dk ** -0.5
    v = v.reshape(B, S, H, dv)

    def to_chunks(t):
        return t.astype(jnp.float32).reshape(B, n_chunks, C, H, -1).transpose(1, 0, 3, 2, 4)

    log_gamma = jnp.log(1.0 - 2.0 ** (-5.0 - jnp.arange(H, dtype=jnp.float32)))
    idx = jnp.arange(C, dtype=jnp.float32)
    diff = idx[:, None] - idx[None, :]
    intra = jnp.where(diff >= 0, jnp.exp(log_gamma[:, None, None] * jnp.maximum(diff, 0.0)), 0.0)
    q_dec = jnp.exp(log_gamma[:, None] * (idx + 1.0))[:, :, None]
    k_dec = jnp.exp(log_gamma[:, None] * (C - 1.0 - idx))[:, :, None]
    chunk_dec = jnp.exp(log_gamma * C)[:, None, None]

    def step(state, qkv):
        qc, kc, vc = qkv
        scores = jnp.einsum('bhid,bhjd->bhij', qc, kc) * intra
        o = jnp.einsum('bhij,bhjv->bhiv', scores, vc) + jnp.einsum('bhid,bhdv->bhiv', qc, state) * q_dec
        state = state * chunk_dec + jnp.einsum('bhjd,bhjv->bhdv', kc * k_dec, vc)
        return state, o

    state0 = jnp.zeros((B, H, dk, dv), jnp.float32)
    _, o = lax.scan(step, state0, (to_chunks(q), to_chunks(k), to_chunks(v)))
    o = o.transpose(1, 0, 3, 2, 4).reshape(B, S, H, dv)
    o = _head_norm(o, gn[0], gn[1], GN_EPS)
    o = (jax.nn.silu(g.astype(jnp.float32)) * o).astype(x.dtype)
    return o @ w_out


def _strided_band(q, k, v, dil, reach):
    B, S, H, Dh = q.shape
    L = S // dil
    nb = -(-L // BLOCK)
    Lp = nb * BLOCK

    def gather(t):
        t = t.reshape(B, L, dil, H, Dh).transpose(0, 2, 1, 3, 4).reshape(B * dil, L, H, Dh)
        t = jnp.pad(t, ((0, 0), (0, Lp - L), (0, 0), (0, 0)))
        return t.reshape(B * dil, nb, BLOCK, H, Dh)

    def with_prev(t):
        prev = jnp.pad(t[:, :-1], ((0, 0), (1, 0), (0, 0), (0, 0), (0, 0)))
        return jnp.concatenate([prev, t], axis=2)

    qb = gather(q)
    kb = with_prev(gather(k))
    vb = with_prev(gather(v)).astype(jnp.float32)
    s = jnp.einsum('znqhd,znkhd->znhqk', qb, kb, preferred_element_type=jnp.float32)
    qi = jnp.arange(BLOCK)[:, None]
    ki = jnp.arange(2 * BLOCK)[None, :]
    dist = qi + BLOCK - ki
    blk = jnp.arange(nb)[:, None, None]
    valid = (dist >= 0) & (dist <= reach) & ((blk > 0) | (ki >= BLOCK))
    s = jnp.where(valid[None, :, None], s, NEG)
    m = s.max(-1, keepdims=True)
    p = jnp.exp(s - m)
    l = p.sum(-1)
    o = jnp.einsum('znhqk,znkhd->znqhd', p, vb)

    def scatter_back(t):
        t = t.reshape(B, dil, Lp, *t.shape[3:])[:, :, :L]
        return jnp.moveaxis(t, 1, 2).reshape(B, S, *t.shape[3:])

    return (scatter_back(o), scatter_back(jnp.swapaxes(m[..., 0], 2, 3)),
            scatter_back(jnp.swapaxes(l, 2, 3)))


def _dilated(x, w_in, w_out):
    B, S, _ = x.shape
    G, H, Dh = len(DIL_PAIRS), DIL_HEADS, DIL_HEAD_DIM
    proj = (x @ w_in).reshape(B, S, G, 3, H, Dh)
    pos = jnp.arange(S)
    outs, maxes, dens = [], [], []
    for gi, (window, dil) in enumerate(DIL_PAIRS):
        q = _rope(proj[:, :, gi, 0], pos, DIL_ROT, ROPE_THETA) * Dh ** -0.5
        k = _rope(proj[:, :, gi, 1], pos, DIL_ROT, ROPE_THETA)
        o, m, l = _strided_band(q, k, proj[:, :, gi, 2], dil, window // dil)
        outs.append(o); maxes.append(m); dens.append(l)
    o_all = jnp.stack(outs)
    m_all = jnp.stack(maxes)
    l_all = jnp.stack(dens)
    wts = jnp.exp(m_all - m_all.max(0, keepdims=True))
    o = jnp.einsum('gbsh,gbshd->bshd', wts, o_all) / jnp.einsum('gbsh,gbsh->bsh', wts, l_all)[..., None]
    return o.reshape(B, S, H * Dh).astype(x.dtype) @ w_out


def _mla(x, w_down, norm_q, norm_kv, w_uq, w_ukv, w_out):
    B, S, _ = x.shape
    H = MLA_HEADS
    pos = jnp.arange(S)
    c_q, c_kv, k_pe = jnp.split(x @ w_down, [MLA_Q_RANK, MLA_Q_RANK + MLA_KV_RANK], axis=-1)
    q = (_rms_norm(c_q, norm_q) @ w_uq).reshape(B, S, H, MLA_NOPE + MLA_ROPE)
    kv = (_rms_norm(c_kv, norm_kv) @ w_ukv).reshape(B, S, H, MLA_NOPE + MLA_V)
    q_nope = q[..., :MLA_NOPE]
    q_pe = _rope(q[..., MLA_NOPE:], pos, MLA_ROPE, MLA_THETA)
    k_nope, v = kv[..., :MLA_NOPE], kv[..., MLA_NOPE:]
    k_pe = _rope(k_pe[:, :, None, :], pos, MLA_ROPE, MLA_THETA)[:, :, 0]
    scale = (MLA_NOPE + MLA_ROPE) ** -0.5
    n_blocks = S // BLOCK
    key_pos = jnp.arange(S)

    def blocks(t):
        return jnp.moveaxis(t.reshape(B, n_blocks, BLOCK, *t.shape[2:]), 1, 0)

    def attend(args):
        qn, qp, q0 = args
        s = (jnp.einsum('bqhd,bkhd->bhqk', qn, k_nope, preferred_element_type=jnp.float32)
             + jnp.einsum('bqhd,bkd->bhqk', qp, k_pe, preferred_element_type=jnp.float32)) * scale
        causal = (q0 + jnp.arange(BLOCK))[:, None] >= key_pos[None, :]
        p = jax.nn.softmax(jnp.where(causal, s, NEG), axis=-1)
        return jnp.einsum('bhqk,bkhd->bqhd', p.astype(v.dtype), v)

    o = lax.map(attend, (blocks(q_nope), blocks(q_pe), jnp.arange(n_blocks) * BLOCK))
    o = jnp.moveaxis(o, 0, 1).reshape(B, S, H * MLA_V)
    return o @ w_out


def _rwkv7(x, mu, w_rkv, w_out, vec, lora_a, lora_b, gate_a, gate_b, ln_x):
    B, S, D = x.shape
    H, N = RWKV_HEADS, RWKV_HEAD
    xx = jnp.pad(x, ((0, 0), (1, 0), (0, 0)))[:, :-1] - x
    xr, xw, xk, xv, xa, xg = [x + xx * mu[i] for i in range(6)]
    w0, a0, k_k, k_a, r_k = vec[0], vec[1], vec[2], vec[3], vec[4]

    def heads(t):
        return t.astype(jnp.float32).reshape(B, S, H, N)

    r = heads(xr @ w_rkv[0])
    k_raw = xk @ w_rkv[1]
    v = heads(xv @ w_rkv[2])
    log_w = -jax.nn.softplus(-(w0 + jnp.tanh(xw @ lora_a[0]) @ lora_b[0]).astype(jnp.float32)) - 0.5
    decay = heads(jnp.exp(-jnp.exp(log_w)))
    a = heads(jax.nn.sigmoid((a0 + (xa @ lora_a[1]) @ lora_b[1]).astype(jnp.float32)))
    g = jax.nn.sigmoid(xg @ gate_a) @ gate_b
    kk = heads(k_raw * k_k)
    kk = kk / jnp.maximum(jnp.sqrt(jnp.sum(jnp.square(kk), -1, keepdims=True)), 1e-12)
    k = heads(k_raw) * (1.0 + (a - 1.0) * k_a.astype(jnp.float32).reshape(H, N))

    def step(state, inp):
        r_t, w_t, k_t, v_t, kk_t, b_t = inp
        sa = jnp.einsum('bhij,bhj->bhi', state, -kk_t)
        state = (state * w_t[:, :, None, :] + sa[..., None] * b_t[:, :, None, :]
                 + v_t[..., None] * k_t[:, :, None, :])
        return state, jnp.einsum('bhij,bhj->bhi', state, r_t)

    tm = lambda t: jnp.moveaxis(t, 1, 0)
    state0 = jnp.zeros((B, H, N, N), jnp.float32)
    _, y = lax.scan(step, state0, (tm(r), tm(decay), tm(k), tm(v), tm(kk), tm(kk * a)))
    y = jnp.moveaxis(y, 0, 1)
    y = _head_norm(y, ln_x[0], ln_x[1], RWKV_GN_EPS)
    bonus = (jnp.sum(r * k * r_k.astype(jnp.float32).reshape(H, N), -1, keepdims=True) * v).reshape(B, S, D)
    y = ((y + bonus) * g.astype(jnp.float32)).astype(x.dtype)
    return y @ w_out


def _sq_relu_mlp(x, w1, w2):
    return jnp.square(jax.nn.relu(x @ w1)) @ w2


def setup_inputs(seed: int = 0) -> dict:
    key = jax.random.key(seed)
    ks = iter(jax.random.split(key, 64))
    f32 = jnp.float32

    def nrm(shape, fan_in, scale=1.0):
        return jax.random.normal(next(ks), shape, f32) * (scale * fan_in ** -0.5)

    def gain(shape):
        return 1.0 + 0.05 * jax.random.normal(next(ks), shape, f32)

    def small(shape, s=0.02):
        return s * jax.random.normal(next(ks), shape, f32)

    nA, nB, nC, nD = _n_occ(0), _n_occ(1), _n_occ(2), _n_occ(3)
    D = D_MODEL
    rh, dk, dv = RET_HEADS, RET_QK_DIM, RET_V_DIM
    G, dh, dd = len(DIL_PAIRS), DIL_HEADS, DIL_HEAD_DIM
    mh = MLA_HEADS
    inp = {}
    inp['x'] = jax.random.normal(next(ks), (BATCH, SEQ, D), f32)
    inp['ret_w_in'] = nrm((nA, D, 2 * rh * dk + 2 * rh * dv), D)
    inp['ret_gn'] = jnp.stack([gain((nA, rh * dv)), small((nA, rh * dv))], axis=1)
    inp['ret_w_out'] = nrm((nA, rh * dv, D), rh * dv, DN_BETA)
    inp['dil_w_in'] = nrm((nB, D, G * 3 * dh * dd), D)
    inp['dil_w_out'] = nrm((nB, dh * dd, D), dh * dd, DN_BETA)
    inp['mla_w_down'] = nrm((nC, D, MLA_Q_RANK + MLA_KV_RANK + MLA_ROPE), D)
    inp['mla_norm_q'] = gain((nC, MLA_Q_RANK))
    inp['mla_norm_kv'] = gain((nC, MLA_KV_RANK))
    inp['mla_w_uq'] = nrm((nC, MLA_Q_RANK, mh * (MLA_NOPE + MLA_ROPE)), MLA_Q_RANK)
    inp['mla_w_ukv'] = nrm((nC, MLA_KV_RANK, mh * (MLA_NOPE + MLA_V)), MLA_KV_RANK)
    inp['mla_w_out'] = nrm((nC, mh * MLA_V, D), mh * MLA_V, DN_BETA)
    inp['rwkv_mu'] = jax.random.uniform(next(ks), (nD, 6, D), f32, 0.0, 1.0)
    inp['rwkv_w_rkv'] = nrm((nD, 3, D, D), D)
    inp['rwkv_w_out'] = nrm((nD, D, D), D, DN_BETA)
    inp['rwkv_vec'] = jnp.stack([
        jax.random.uniform(next(ks), (nD, D), f32, -5.0, 0.0),
        small((nD, D), 0.1),
        0.85 + small((nD, D), 0.05),
        1.0 + small((nD, D), 0.05),
        small((nD, D), 0.1),
    ], axis=1)
    inp['rwkv_lora_a'] = nrm((nD, 2, D, RWKV_LORA), D)
    inp['rwkv_lora_b'] = nrm((nD, 2, RWKV_LORA, D), RWKV_LORA, 0.1)
    inp['rwkv_gate_a'] = nrm((nD, D, RWKV_GATE_LORA), D)
    inp['rwkv_gate_b'] = nrm((nD, RWKV_GATE_LORA, D), RWKV_GATE_LORA)
    inp['rwkv_ln_x'] = jnp.stack([gain((nD, D)), small((nD, D))], axis=1)
    inp['mlp_w1'] = nrm((DEPTH, D, D_FF), D)
    inp['mlp_w2'] = nrm((DEPTH, D_FF, D), D_FF, DN_BETA)
    inp['ln_g'] = gain((DEPTH, 2, D))
    inp['ln_b'] = small((DEPTH, 2, D))
    return inp


def reference(x, ret_w_in, ret_gn, ret_w_out, dil_w_in, dil_w_out, mla_w_down, mla_norm_q,
              mla_norm_kv, mla_w_uq, mla_w_ukv, mla_w_out, rwkv_mu, rwkv_w_rkv, rwkv_w_out,
              rwkv_vec, rwkv_lora_a, rwkv_lora_b, rwkv_gate_a, rwkv_gate_b, rwkv_ln_x,
              mlp_w1, mlp_w2, ln_g, ln_b):
    for i in range(DEPTH):
        m, j = i % N_MIXERS, i // N_MIXERS
        if m == 0:
            h = _retention(x, ret_w_in[j], ret_gn[j], ret_w_out[j])
        elif m == 1:
            h = _dilated(x, dil_w_in[j], dil_w_out[j])
        elif m == 2:
            h = _mla(x, mla_w_down[j], mla_norm_q[j], mla_norm_kv[j], mla_w_uq[j], mla_w_ukv[j], mla_w_out[j])
        else:
            h = _rwkv7(x, rwkv_mu[j], rwkv_w_rkv[j], rwkv_w_out[j], rwkv_vec[j], rwkv_lora_a[j],
                       rwkv_lora_b[j], rwkv_gate_a[j], rwkv_gate_b[j], rwkv_ln_x[j])
        x = _layer_norm(DN_ALPHA * x + h, ln_g[i, 0], ln_b[i, 0])
        x = _layer_norm(DN_ALPHA * x + _sq_relu_mlp(x, mlp_w1[i], mlp_w2[i]), ln_g[i, 1], ln_b[i, 1])
    return x
```

```python
import numpy as np
import ml_dtypes
from contextlib import ExitStack
import concourse.bass as bass
import concourse.mybir as mybir
from concourse.bass_utils import run_bass_kernel_spmd

F32 = mybir.dt.float32
BF16 = mybir.dt.bfloat16
AF = mybir.ActivationFunctionType
ALU = mybir.AluOpType
AX = mybir.AxisListType

D = 1024
S = 4096
NCORES = 8
DFF = 4096
LN_EPS = 1e-5
DN_ALPHA = (2.0 * 4) ** 0.25
import os
DBG = os.environ.get("DBG", "")


class Buf:
    __slots__ = ("w", "r", "dsem", "multi", "name", "psum")

    def __init__(self, name="", multi=False):
        self.psum = False
        self.w = {}
        self.r = {}
        self.dsem = {}
        self.multi = multi
        self.name = name


class V:
    __slots__ = ("ap", "buf")

    def __init__(self, ap, buf):
        self.ap = ap
        self.buf = buf


class T:
    def __init__(self, h, buf):
        self.h = h
        self.buf = buf

    def __getitem__(self, idx):
        return V(self.h[idx], self.buf)

    def v(self, ap):
        return V(ap, self.buf)


class K:
    def __init__(self, nc, es):
        self.nc = nc
        self.es = es
        self.eng = {"pe": nc.tensor, "act": nc.scalar, "dve": nc.vector, "pool": nc.gpsimd, "sp": nc.sync}
        self.esem = {}
        self.ecnt = {}
        for e in ("pe", "act", "dve", "pool"):
            self.esem[e] = es.enter_context(nc.semaphore("sem_" + e))
            self.ecnt[e] = 0
        self.seen = {e: {} for e in self.eng}
        self.dpool = []
        self.dfree = {"hw": [], "sw": []}
        self.phase_es = None
        self.phase_bufs = []
        self.ndma = 0
        self.uid = 0

    def begin_phase(self):
        self.phase_es = ExitStack()
        self.phase_bufs = []

    def end_phase(self):
        self.barrier()
        for b in self.phase_bufs:
            for kind, di in b.dsem.items():
                self.dfree[kind].append(di)
            b.dsem = {}
        self.phase_es.close()
        self.phase_es = None

    def sb(self, shape, dtype, name=None):
        self.uid += 1
        name = (name or "sb") + "_%d" % self.uid
        h = self.phase_es.enter_context(self.nc.sbuf_tensor(name, list(shape), dtype))
        b = Buf(name)
        self.phase_bufs.append(b)
        return T(h, b)

    def ps(self, shape, dtype=F32, name=None):
        self.uid += 1
        name = (name or "ps") + "_%d" % self.uid
        isz = 4 if dtype == F32 else 2
        n = 1
        for d in shape[1:]:
            n *= d
        per_bank = 2048 // isz
        nflat = ((n + per_bank - 1) // per_bank) * per_bank
        h = self.phase_es.enter_context(self.nc.psum_tensor(name, [shape[0], nflat], dtype))
        ap = h[:, 0:n]
        if len(shape) == 3:
            ap = ap.rearrange("p (a b) -> p a b", b=shape[2])
        elif len(shape) == 4:
            ap = ap.rearrange("p (a b c) -> p a b c", b=shape[2], c=shape[3])
        b = Buf(name)
        b.psum = True
        self.phase_bufs.append(b)
        return T(ap, b)

    def dram(self, name, shape, dtype, kind="Internal"):
        h = self.nc.dram_tensor(name, list(shape), dtype, kind=kind)
        return T(h.ap(), Buf(name, multi=True))

    def _dsem(self, buf, kind):
        if kind not in buf.dsem:
            if self.dfree[kind]:
                buf.dsem[kind] = self.dfree[kind].pop()
            else:
                sem = self.es.enter_context(self.nc.semaphore("dsem_%d" % len(self.dpool)))
                self.dpool.append([sem, 0])
                buf.dsem[kind] = len(self.dpool) - 1
        return buf.dsem[kind]

    def _wait(self, e, deps):
        seen = self.seen[e]
        for key, val in deps.items():
            if key[0] == "d":
                sem, val = self.dpool[key[1]]
            else:
                if e == "pe" and key[1] == "pe":
                    continue
                sem = self.esem[key[1]]
            if seen.get(key, 0) < val:
                self.eng[e].wait_ge(sem, val)
                seen[key] = val

    def _collect(self, reads, writes):
        deps = {}
        for b in reads:
            for kx, vx in b.w.items():
                if deps.get(kx, 0) < vx:
                    deps[kx] = vx
            if b.psum:
                for kx, vx in b.r.items():
                    if deps.get(kx, 0) < vx:
                        deps[kx] = vx
        for b in writes:
            for kx, vx in b.w.items():
                if deps.get(kx, 0) < vx:
                    deps[kx] = vx
            for kx, vx in b.r.items():
                if deps.get(kx, 0) < vx:
                    deps[kx] = vx
        return deps

    def _record(self, key, val, reads, writes):
        for b in reads:
            b.r[key] = val
        for b in writes:
            if b.multi:
                b.w[key] = val
            else:
                b.w = {key: val}
                b.r = {}

    def op(self, e, fn, reads, writes):
        reads = [v.buf for v in reads if v is not None]
        writes = [v.buf for v in writes if v is not None]
        self._wait(e, self._collect(reads, writes))
        ins = fn(self.eng[e])
        self.ecnt[e] += 1
        ins.then_inc(self.esem[e], 1)
        self._record(("e", e), self.ecnt[e], reads, writes)
        return ins

    def dma(self, out, in_, q="sp", sbuf=None):
        if sbuf is None:
            sbuf = out if not out.buf.multi else in_
        reads = [in_.buf]
        writes = [out.buf]
        self._wait(q, self._collect(reads, writes))
        di = self._dsem(sbuf.buf, "sw" if q == "pool" else "hw")
        ins = self.eng[q].dma_start(out=out.ap, in_=in_.ap)
        self.dpool[di][1] += 16
        ins.then_inc(self.dpool[di][0], 16)
        self._record(("d", di), self.dpool[di][1], reads, writes)
        self.ndma += 1
        return ins

    def barrier(self):
        for e in self.eng:
            deps = {}
            for f in self.esem:
                deps[("e", f)] = self.ecnt[f]
            for i in range(len(self.dpool)):
                deps[("d", i)] = self.dpool[i][1]
            seen = self.seen[e]
            for key, val in deps.items():
                if key[0] == "d":
                    sem = self.dpool[key[1]][0]
                else:
                    sem = self.esem[key[1]]
                if val > 0 and seen.get(key, 0) < val:
                    self.eng[e].wait_ge(sem, val)
                    seen[key] = val

    def mm(self, out, lhsT, rhs, start=True, stop=True):
        return self.op("pe", lambda e: e.matmul(out.ap, lhsT.ap, rhs.ap, start=start, stop=stop),
                       [lhsT, rhs], [out])

    def tr(self, out, in_, ident):
        return self.op("pe", lambda e: e.transpose(out.ap, in_.ap, ident.ap), [in_, ident], [out])

    def act(self, out, in_, func, bias=None, scale=1.0, accum=None, e="act"):
        rd = [in_]
        kw = {}
        if isinstance(bias, V):
            rd.append(bias)
            kw["bias"] = bias.ap
        elif bias is not None:
            kw["bias"] = bias
        if isinstance(scale, V):
            rd.append(scale)
            kw["scale"] = scale.ap
        else:
            kw["scale"] = scale
        wr = [out]
        if accum is not None:
            wr.append(accum)
            kw["accum_out"] = accum.ap
        return self.op(e, lambda en: en.activation(out.ap, in_.ap, func, **kw), rd, wr)

    def tt(self, out, a, b, op, e="dve"):
        return self.op(e, lambda en: en.tensor_tensor(out.ap, a.ap, b.ap, op), [a, b], [out])

    def ts(self, out, a, s1, s2=None, op0=ALU.mult, op1=None, e="dve", accum=None):
        rd = [a]
        s1a = s1.ap if isinstance(s1, V) else s1
        s2a = s2.ap if isinstance(s2, V) else s2
        if isinstance(s1, V):
            rd.append(s1)
        if isinstance(s2, V):
            rd.append(s2)
        kw = {}
        if op1 is not None:
            kw["op1"] = op1
        wr = [out]
        if accum is not None:
            kw["accum_out"] = accum.ap
            wr.append(accum)
        return self.op(e, lambda en: en.tensor_scalar(out.ap, a.ap, s1a, s2a, op0, **kw), rd, wr)

    def stt(self, out, in0, scalar, in1, op0, op1, e="dve"):
        rd = [in0, in1]
        sa = scalar.ap if isinstance(scalar, V) else scalar
        if isinstance(scalar, V):
            rd.append(scalar)
        return self.op(e, lambda en: en.scalar_tensor_tensor(out.ap, in0.ap, sa, in1.ap, op0, op1), rd, [out])

    def copy(self, out, in_, e="dve"):
        if e == "act":
            return self.op(e, lambda en: en.copy(out.ap, in_.ap), [in_], [out])
        return self.op(e, lambda en: en.tensor_copy(out.ap, in_.ap), [in_], [out])

    def memset(self, out, val, e="dve"):
        return self.op(e, lambda en: en.memset(out.ap, val), [], [out])

    def reduce(self, out, in_, op, axis=AX.X, e="dve"):
        return self.op(e, lambda en: en.tensor_reduce(out.ap, in_.ap, axis, op), [in_], [out])


def load_bcast(k, dst, src_ap_row, n, q="sp"):
    pass


def ln_tile(k, y, g_bc, b_bc, out32, tmp_stats, tmp_mv, tmp_rs):
    for c in range(2):
        k.op("dve", lambda en, c=c: en.bn_stats(tmp_stats.h[:, c, :], y.h[:, c * 512:(c + 1) * 512]),
             [y[:, :]], [tmp_stats[:, :, :]])
    k.op("dve", lambda en: en.bn_aggr(tmp_mv.h[:, :], tmp_stats.h[:, :, :]), [tmp_stats[:, :, :]], [tmp_mv[:, :]])
    k.act(tmp_rs[:, 0:1], tmp_mv[:, 1:2], AF.Ln, bias=LN_EPS)
    k.act(tmp_rs[:, 0:1], tmp_rs[:, 0:1], AF.Exp, scale=-0.5)
    k.stt(tmp_mv[:, 1:2], tmp_mv[:, 0:1], -1.0, tmp_rs[:, 0:1], ALU.mult, ALU.mult)
    k.act(out32[:, :], y[:, :], AF.Identity, bias=tmp_mv[:, 1:2], scale=tmp_rs[:, 0:1])
    k.tt(out32[:, :], out32[:, :], g_bc[:, :], ALU.mult)
    k.tt(out32[:, :], out32[:, :], b_bc[:, :], ALU.add)


class LNStage:
    def __init__(self, k, ident32, g_row, b_row, x_in, x_out, xT_out):
        self.k = k
        self.ident = ident32
        self.x_in, self.x_out, self.xT_out = x_in, x_out, xT_out
        self.g = k.sb([128, D], F32, "ln_g")
        self.b = k.sb([128, D], F32, "ln_b")
        k.dma(self.g[:, :], V(g_row.ap.partition_broadcast(128), g_row.buf))
        k.dma(self.b[:, :], V(b_row.ap.partition_broadcast(128), b_row.buf))
        self.xin = [k.sb([128, D], F32, "ln_xin") for _ in range(2)]
        self.st = [k.sb([128, 2, 6], F32, "ln_st") for _ in range(2)]
        self.mv = [k.sb([128, 2], F32, "ln_mv") for _ in range(2)]
        self.rs = [k.sb([128, 1], F32, "ln_rs") for _ in range(2)]
        self.xT = [k.sb([128, 8, 128], BF16, "ln_xT") for _ in range(2)]
        self.pt = [k.ps([128, 4, 128], F32, "ln_pt") for _ in range(2)]

    def prefetch(self, ti):
        k = self.k
        i = ti % 2
        k.dma(self.xin[i][:, :], self.x_in[ti * 128:(ti + 1) * 128, :])

    def run(self, ti, po):
        k = self.k
        self.flush()
        i = ti % 2
        y = self.xin[i]
        for hh in range(2):
            k.stt(y[:, hh * 512:(hh + 1) * 512], y[:, hh * 512:(hh + 1) * 512], DN_ALPHA, po[hh],
                  ALU.mult, ALU.add)
        ln_tile(k, y, self.g, self.b, y, self.st[i], self.mv[i], self.rs[i])
        if self.x_out is not None:
            k.dma(self.x_out[ti * 128:(ti + 1) * 128, :], y[:, :])
        if self.xT_out is None:
            return
        self.pending = ti

    def flush(self):
        ti = getattr(self, "pending", None)
        if ti is None:
            return
        self.pending = None
        k = self.k
        i = ti % 2
        y = self.xin[i]
        for half in range(2):
            pt = self.pt[half]
            for c in range(4):
                cc = half * 4 + c
                k.tr(pt[:, c, :], y[:, cc * 128:(cc + 1) * 128], self.ident[:, :])
            k.copy(self.xT[i][:, half * 4:(half + 1) * 4, :], pt[:, :, :], e="act")
        k.dma(V(self.xT_out.h.rearrange("(c p) t -> p c t", p=128)[:, :, ti * 128:(ti + 1) * 128], self.xT_out.buf),
              self.xT[i][:, :, :])


def load_w_bf16(k, dst, src_ap, src_buf):
    k.dma(dst, V(src_ap, src_buf), q="pool")


def phase_prep(k, x_in, xT_out, ident_d):
    k.begin_phase()
    ident = k.sb([128, 128], F32, "ident")
    k.dma(ident[:, :], ident_d[:, :])
    xin = [k.sb([128, D], F32, "xin") for _ in range(2)]
    xT = [k.sb([128, 8, 128], BF16, "xT") for _ in range(2)]
    pt = [k.ps([128, 4, 128], F32, "pt") for _ in range(2)]
    nt = S // 128
    k.dma(xin[0][:, :], x_in[0:128, :])
    for ti in range(nt):
        i = ti % 2
        if ti + 1 < nt:
            k.dma(xin[1 - i][:, :], x_in[(ti + 1) * 128:(ti + 2) * 128, :])
        for half in range(2):
            for c in range(4):
                cc = half * 4 + c
                k.tr(pt[half][:, c, :], xin[i][:, cc * 128:(cc + 1) * 128], ident[:, :])
            k.copy(xT[i][:, half * 4:(half + 1) * 4, :], pt[half][:, :, :], e=("act" if half else "dve"))
        k.dma(V(xT_out.h.rearrange("(c p) t -> p c t", p=128)[:, :, ti * 128:(ti + 1) * 128], xT_out.buf),
              xT[i][:, :, :])
    k.end_phase()


def phase_mlp(k, w1_d, w2_d, g_row, b_row, x_in, xT_in, x_out, xT_out, ident_d):
    k.begin_phase()
    ident = k.sb([128, 128], F32, "ident")
    k.dma(ident[:, :], ident_d[:, :])
    w1 = k.sb([128, 8, DFF], BF16, "w1")
    w2 = k.sb([128, 32, D], BF16, "w2")
    w1v = w1_d.h.rearrange("(c p) f -> p c f", p=128)
    w2v = w2_d.h.rearrange("(c p) d -> p c d", p=128)
    for c in range(8):
        load_w_bf16(k, w1[:, c, :], w1v[:, c, :], w1_d.buf)
    for c in range(0, 32, 4):
        load_w_bf16(k, w2[:, c:c + 4, :], w2v[:, c:c + 4, :], w2_d.buf)
    ln = LNStage(k, ident, g_row, b_row, x_in, x_out, xT_out)
    xb = [k.sb([128, 8, 512], BF16, "xb") for _ in range(2)]
    hT = [k.sb([128, 32, 512], BF16, "hT") for _ in range(1)]
    r32 = [k.sb([128, 512], F32, "r32") for _ in range(2)]
    ph = [k.ps([128, 512], F32, "ph") for _ in range(2)]
    po = [[k.ps([128, 512], F32, "po") for _ in range(2)] for _ in range(2)]
    xTv = xT_in.h.rearrange("(c p) t -> p c t", p=128)
    nb = S // 512
    k.dma(xb[0][:, :, :], V(xTv[:, :, 0:512], xT_in.buf))
    n_ev = 0
    for bi in range(nb):
        i = bi % 2
        if bi + 1 < nb:
            k.dma(xb[1 - i][:, :, :], V(xTv[:, :, (bi + 1) * 512:(bi + 2) * 512], xT_in.buf))
        h = hT[0]
        for fc in range(32):
            p = ph[fc % 2]
            for dc in range(8):
                k.mm(p[:, :], w1[:, dc, fc * 128:(fc + 1) * 128], xb[i][:, dc, :], start=(dc == 0), stop=(dc == 7))
            r = r32[fc % 2]
            k.act(r[:, :], p[:, :], AF.Relu)
            k.tt(h[:, fc, :], r[:, :], r[:, :], ALU.mult, e="dve")
        for tt in range(4):
            ti = bi * 4 + tt
            if "noln" not in DBG:
                ln.prefetch(ti)
            pp = po[tt % 2]
            for hh in range(2):
                for fc in range(32):
                    k.mm(pp[hh][:, :], h[:, fc, tt * 128:(tt + 1) * 128], w2[:, fc, hh * 512:(hh + 1) * 512],
                         start=(fc == 0), stop=(fc == 31))
            if "noln" in DBG:
                for hh in range(2):
                    k.copy(ln.xin[ti % 2][:, hh * 512:(hh + 1) * 512], pp[hh][:, :])
                k.dma(x_out[ti * 128:(ti + 1) * 128, :], ln.xin[ti % 2][:, :])
            else:
                ln.run(ti, [pp[0][:, :], pp[1][:, :]])
    ln.flush()
    k.end_phase()


def phase_outproj(k, ogT_d, w_out_d, nfc, g_row, b_row, x_in, x_out, xT_out, ident_d):
    k.begin_phase()
    ident = k.sb([128, 128], F32, "ident")
    k.dma(ident[:, :], ident_d[:, :])
    w = k.sb([128, nfc, D], BF16, "wout")
    wv = w_out_d.h.rearrange("(c p) d -> p c d", p=128)
    for c in range(0, nfc, 4):
        load_w_bf16(k, w[:, c:c + 4, :], wv[:, c:c + 4, :], w_out_d.buf)
    ln = LNStage(k, ident, g_row, b_row, x_in, x_out, xT_out)
    ob = [k.sb([128, nfc, 128], BF16, "ob") for _ in range(2)]
    po = [[k.ps([128, 512], F32, "po") for _ in range(2)] for _ in range(2)]
    ov = ogT_d.h.rearrange("(c p) t -> p c t", p=128)
    nt = S // 128
    k.dma(ob[0][:, :, :], V(ov[:, :, 0:128], ogT_d.buf))
    for ti in range(nt):
        i = ti % 2
        if ti + 1 < nt:
            k.dma(ob[1 - i][:, :, :], V(ov[:, :, (ti + 1) * 128:(ti + 2) * 128], ogT_d.buf))
        ln.prefetch(ti)
        pp = po[i]
        for hh in range(2):
            for fc in range(nfc):
                k.mm(pp[hh][:, :], ob[i][:, fc, :], w[:, fc, hh * 512:(hh + 1) * 512],
                     start=(fc == 0), stop=(fc == nfc - 1))
        ln.run(ti, [pp[0][:, :], pp[1][:, :]])
    ln.flush()
    k.end_phase()


RET_H = 4


def phase_ret_proj(k, xT_in, w_in_d, cos_d, sin_d, dec_d, qkT_d, v_d, g_d):
    k.begin_phase()
    w = k.sb([128, 8, 6144], BF16, "w_in")
    wv = w_in_d.h.rearrange("(c p) f -> p c f", p=128)
    for c in range(8):
        for half in range(2):
            load_w_bf16(k, w[:, c, half * 3072:(half + 1) * 3072], wv[:, c, half * 3072:(half + 1) * 3072], w_in_d.buf)
    xb = [k.sb([128, 8, 512], BF16, "xb") for _ in range(2)]
    cs = [k.sb([128, 512], F32, "cos") for _ in range(2)]
    sn = [k.sb([128, 512], F32, "sin") for _ in range(2)]
    dec = [k.sb([128, 8, 512], F32, "dec") for _ in range(2)]
    t1 = k.sb([128, 512], F32, "t1")
    t2 = k.sb([128, 512], F32, "t2")
    r0 = [k.sb([128, 2, 512], BF16, "r0") for _ in range(2)]
    vo = [k.sb([128, 512], BF16, "vo") for _ in range(2)]
    go = [k.sb([128, 512], F32, "go") for _ in range(2)]
    pq = [k.ps([128, 512], F32, "pq") for _ in range(4)]
    pv = [k.ps([128, 512], F32, "pv") for _ in range(2)]
    xTv = xT_in.h.rearrange("(c p) t -> p c t", p=128)
    nb = S // 512

    def load_blk(bi):
        i = bi % 2
        sl = slice(bi * 512, (bi + 1) * 512)
        k.dma(xb[i][:, :, :], V(xTv[:, :, sl], xT_in.buf))
        k.dma(cs[i][:, :], cos_d[:, sl])
        k.dma(sn[i][:, :], sin_d[:, sl])
        for j in range(8):
            k.dma(dec[i][:, j, :], V(dec_d.h[j, sl].partition_broadcast(128), dec_d.buf))

    load_blk(0)
    n = 0
    for bi in range(nb):
        i = bi % 2
        sl = slice(bi * 512, (bi + 1) * 512)
        if bi + 1 < nb:
            load_blk(bi + 1)
        for j in range(8):
            p0, p1 = pq[(n % 2) * 2], pq[(n % 2) * 2 + 1]
            ro = r0[n % 2]
            n += 1
            f0 = j * 256
            for dc in range(8):
                k.mm(p0[:, :], w[:, dc, f0:f0 + 128], xb[i][:, dc, :], start=(dc == 0), stop=(dc == 7))
            for dc in range(8):
                k.mm(p1[:, :], w[:, dc, f0 + 128:f0 + 256], xb[i][:, dc, :], start=(dc == 0), stop=(dc == 7))
            k.tt(t1[:, :], p0[:, :], cs[i][:, :], ALU.mult)
            k.tt(t2[:, :], p1[:, :], sn[i][:, :], ALU.mult)
            k.tt(t1[:, :], t1[:, :], t2[:, :], ALU.subtract)
            k.tt(ro[:, 0, :], t1[:, :], dec[i][:, j, :], ALU.mult)
            k.tt(t1[:, :], p1[:, :], cs[i][:, :], ALU.mult)
            k.tt(t2[:, :], p0[:, :], sn[i][:, :], ALU.mult)
            k.tt(t1[:, :], t1[:, :], t2[:, :], ALU.add)
            k.tt(ro[:, 1, :], t1[:, :], dec[i][:, j, :], ALU.mult)
            k.dma(V(qkT_d.h[2 * j:2 * j + 2, :, sl].rearrange("c p t -> p c t"), qkT_d.buf), ro[:, :, :])
        m = 0
        for tt in range(4):
            tsl = slice(tt * 128, (tt + 1) * 128)
            rows = slice(bi * 512 + tt * 128, bi * 512 + (tt + 1) * 128)
            for fb in range(8):
                f0 = 2048 + fb * 512
                p = pv[m % 2]
                for dc in range(8):
                    k.mm(p[:, :], xb[i][:, dc, tsl], w[:, dc, f0:f0 + 512], start=(dc == 0), stop=(dc == 7))
                if fb < 4:
                    o = vo[m % 2]
                    k.copy(o[:, :], p[:, :], e="act")
                    k.dma(v_d[rows, fb * 512:(fb + 1) * 512], o[:, :])
                else:
                    o = go[m % 2]
                    k.act(o[:, :], p[:, :], AF.Silu)
                    k.dma(g_d[rows, (fb - 4) * 512:(fb - 3) * 512], o[:, :])
                m += 1
    k.end_phase()


def phase_ret_attn(k, qkT_d, v_d, g_d, gn_d, mask_d, ident_d, ogT_d):
    k.begin_phase()
    identb = k.sb([128, 128], BF16, "identb")
    k.dma(identb[:, :], ident_d[:, :], q="pool")
    mask = k.sb([128, 128], F32, "mask")
    k.dma(mask[:, :], mask_d[:, :])
    gng = k.sb([128, 2048], F32, "gng")
    gnb = k.sb([128, 2048], F32, "gnb")
    k.dma(gng[:, :], V(gn_d.h[0, :].partition_broadcast(128), gn_d.buf))
    k.dma(gnb[:, :], V(gn_d.h[1, :].partition_broadcast(128), gn_d.buf))
    qT = [k.sb([128, 2, S], BF16, "qT") for _ in range(1)]
    kT = [k.sb([128, 2, S], BF16, "kT") for _ in range(1)]
    vv = [k.sb([128, 32, 512], BF16, "vv") for _ in range(1)]
    pb = [k.sb([128, 128], BF16, "pb") for _ in range(3)]
    gt = [k.sb([128, 512], F32, "gt") for _ in range(2)]
    on = [k.sb([128, 512], F32, "on") for _ in range(2)]
    ob = [k.sb([128, 512], BF16, "ob") for _ in range(2)]
    oT = [k.sb([128, 4, 128], BF16, "oT") for _ in range(2)]
    st = [k.sb([128, 1, 6], F32, "st") for _ in range(2)]
    mv = [k.sb([128, 2], F32, "mv") for _ in range(2)]
    rs = [k.sb([128, 1], F32, "rs") for _ in range(2)]
    psT = [k.ps([128, 128], F32, "psT") for _ in range(3)]
    po = [k.ps([128, 512], F32, "po") for _ in range(2)]
    pt = [k.ps([128, 4, 128], BF16, "pt") for _ in range(2)]
    np_ = 0
    for h in range(RET_H):
        gamma = 1.0 - 2.0 ** (-5.0 - h)
        k.dma(qT[0][:, :, :], V(qkT_d.h[2 * h:2 * h + 2, :, :].rearrange("c p t -> p c t"), qkT_d.buf))
        k.dma(kT[0][:, :, :], V(qkT_d.h[8 + 2 * h:8 + 2 * h + 2, :, :].rearrange("c p t -> p c t"), qkT_d.buf))
        k.dma(vv[0][:, :, :], V(v_d.h[:, h * 512:(h + 1) * 512].rearrange("(n p) f -> p n f", p=128), v_d.buf))
        items = [(qt, kt) for qt in range(32) for kt in range(qt + 1)]
        slots = {}
        LA = 2

        def emit_qk(it, h=h, gamma=gamma, slots=slots):
            nonlocal np_
            qt, kt = it
            i = qt % 2
            qs = slice(qt * 128, (qt + 1) * 128)
            ks = slice(kt * 128, (kt + 1) * 128)
            if kt == 0:
                k.dma(gt[i][:, :], g_d[qs, h * 512:(h + 1) * 512])
            ps = psT[np_ % 3]
            p = pb[np_ % 3]
            np_ += 1
            slots[it] = p
            for dc in range(2):
                k.mm(ps[:, :], kT[0][:, dc, ks], qT[0][:, dc, qs], start=(dc == 0), stop=(dc == 1))
            if kt == qt:
                k.tt(p[:, :], ps[:, :], mask[:, :], ALU.mult)
            else:
                k.act(p[:, :], ps[:, :], AF.Copy, scale=float(gamma ** (128 * (qt - kt))))

        def emit_pv(it, h=h, slots=slots):
            qt, kt = it
            i = qt % 2
            qs = slice(qt * 128, (qt + 1) * 128)
            p = slots.pop(it)
            k.mm(po[i][:, :], p[:, :], vv[0][:, kt, :], start=(kt == 0), stop=(kt == qt))
            if kt != qt:
                return
            k.op("dve", lambda en, i=i: en.bn_stats(st[i].h[:, 0, :], po[i].h[:, :]), [po[i][:, :]], [st[i][:, :, :]])
            k.op("dve", lambda en, i=i: en.bn_aggr(mv[i].h[:, :], st[i].h[:, :, :]), [st[i][:, :, :]], [mv[i][:, :]])
            k.act(rs[i][:, 0:1], mv[i][:, 1:2], AF.Ln, bias=1e-5)
            k.act(rs[i][:, 0:1], rs[i][:, 0:1], AF.Exp, scale=-0.5)
            k.stt(mv[i][:, 1:2], mv[i][:, 0:1], -1.0, rs[i][:, 0:1], ALU.mult, ALU.mult)
            k.act(on[i][:, :], po[i][:, :], AF.Identity, bias=mv[i][:, 1:2], scale=rs[i][:, 0:1])
            k.tt(on[i][:, :], on[i][:, :], gng[:, h * 512:(h + 1) * 512], ALU.mult)
            k.tt(on[i][:, :], on[i][:, :], gnb[:, h * 512:(h + 1) * 512], ALU.add)
            k.tt(ob[i][:, :], on[i][:, :], gt[i][:, :], ALU.mult)
            for c in range(4):
                k.tr(pt[i][:, c, :], ob[i][:, c * 128:(c + 1) * 128], identb[:, :])
            k.copy(oT[i][:, :, :], pt[i][:, :, :], e="act")
            k.dma(V(ogT_d.h[h * 512:(h + 1) * 512, qs].rearrange("(c p) t -> p c t", p=128), ogT_d.buf), oT[i][:, :, :])

        for idx, it in enumerate(items):
            emit_qk(it)
            if idx >= LA:
                emit_pv(items[idx - LA])
        for it in items[len(items) - LA:]:
            emit_pv(it)
    k.end_phase()


def host_consts():
    c = {}
    c["ident"] = np.eye(128, dtype=np.float32)
    t = np.arange(S, dtype=np.float64)
    inv = 10000.0 ** (-np.arange(128, dtype=np.float64) / 128)
    ang = inv[:, None] * t[None, :]
    c["ret_cos"] = np.cos(ang).astype(np.float32)
    c["ret_sin"] = np.sin(ang).astype(np.float32)
    dec = np.zeros((8, S), np.float64)
    for h in range(4):
        g = 1.0 - 2.0 ** (-5.0 - h)
        dec[h] = g ** (t % 128)
        dec[4 + h] = 256.0 ** -0.5 * g ** (-(t % 128))
    c["ret_dec"] = dec.astype(np.float32)
    kk = np.arange(128)
    c["mask_le"] = (kk[:, None] <= kk[None, :]).astype(np.float32)
    c["mask_ge"] = (kk[:, None] >= kk[None, :]).astype(np.float32)
    inv = 10000.0 ** (-np.arange(32, dtype=np.float64) / 32)
    ang = t[:, None] * inv[None, :]
    c["mla_cosT"] = np.cos(ang).astype(np.float32)
    c["mla_sinT"] = np.sin(ang).astype(np.float32)
    cT, sT = np.cos(ang).T, np.sin(ang).T
    c["mla_c2"] = (MLA_SCALE * np.concatenate([cT, cT], 0)).astype(np.float32)
    c["mla_s2"] = (MLA_SCALE * np.concatenate([-sT, sT], 0)).astype(np.float32)
    kq = np.arange(128)[:, None]
    qq = np.arange(512)[None, :]
    c["mla_mask4"] = np.stack([((m * 128 + kq) <= qq) for m in range(4)]).astype(np.float32)
    inv = 500000.0 ** (-np.arange(16, dtype=np.float64) / 16)
    ang = inv[:, None] * t[None, :]
    cT, sT = np.cos(ang), np.sin(ang)
    c["dil_c2"] = np.concatenate([cT, cT], 0).astype(np.float32)
    c["dil_s2"] = np.concatenate([-sT, sT], 0).astype(np.float32)
    c["dil_c2q"] = (128.0 ** -0.5 * np.concatenate([cT, cT], 0)).astype(np.float32)
    c["dil_s2q"] = (128.0 ** -0.5 * np.concatenate([-sT, sT], 0)).astype(np.float32)
    ii = np.arange(128)
    same = (ii[:, None] // 64) == (ii[None, :] // 64)
    c["rw_ustr"] = (same & (ii[:, None] < ii[None, :])).astype(np.float32)
    c["rw_uincl"] = (same & (ii[:, None] <= ii[None, :])).astype(np.float32)
    c["rw_lstr"] = (same & (ii[:, None] > ii[None, :])).astype(np.float32)
    c["rw_chsel"] = np.stack([(ii < 64), (ii >= 64)], 1).astype(np.float32)
    return c


LASTK = None


def build_all(inputs, layers=(0, 1, 2, 3), ncores=NCORES):
    global LASTK
    consts = host_consts()
    nc = bass.Bass("TRN2", target_bir_lowering=False)
    shared = {}
    with ExitStack() as es:
        k = K(nc, es)
        LASTK = k

        tcache = {}

        def inp(name, arr, dtype=F32):
            if name not in tcache:
                shared[name] = np.ascontiguousarray(arr)
                tcache[name] = k.dram(name, list(arr.shape), dtype, kind="ExternalInput")
            return tcache[name]

        x = k.dram("x", [S, D], F32, kind="ExternalInput")
        out = k.dram("out", [S, D], F32, kind="ExternalOutput")
        ident = inp("ident", consts["ident"])
        ln_g = inp("ln_g", inputs["ln_g"])
        ln_b = inp("ln_b", inputs["ln_b"])
        mlp_w1 = inp("mlp_w1", inputs["mlp_w1"])
        mlp_w2 = inp("mlp_w2", inputs["mlp_w2"])
        xa = k.dram("xa", [S, D], F32)
        xb_ = k.dram("xb", [S, D], F32)
        xT0 = k.dram("xT0", [D, S], BF16)
        xT1 = k.dram("xT1", [D, S], BF16)
        ogT = k.dram("ogT", [2048, S], BF16)

        def lnrow(t, i, j):
            return V(t.h[i, j, :], t.buf)

        phase_prep(k, x, xT0, ident)
        cur = x
        for li, L in enumerate(layers):
            last = (li == len(layers) - 1)
            if L == 0:
                w_in = inp("ret_w_in", inputs["ret_w_in"][0])
                w_out = inp("ret_w_out", inputs["ret_w_out"][0])
                gn = inp("ret_gn", inputs["ret_gn"][0])
                cos_d = inp("ret_cos", consts["ret_cos"])
                sin_d = inp("ret_sin", consts["ret_sin"])
                dec_d = inp("ret_dec", consts["ret_dec"])
                mask_le = inp("mask_le", consts["mask_le"])
                qkT = k.dram("ret_qkT", [16, 128, S], BF16)
                v_d = k.dram("ret_v", [S, 2048], BF16)
                g_d = k.dram("ret_g", [S, 2048], F32)
                phase_ret_proj(k, xT0, w_in, cos_d, sin_d, dec_d, qkT, v_d, g_d)
                phase_ret_attn(k, qkT, v_d, g_d, gn, mask_le, ident, ogT)
                phase_outproj(k, ogT, w_out, 16, lnrow(ln_g, L, 0), lnrow(ln_b, L, 0), cur, xa, xT1, ident)
            elif L == 1:
                wi = inputs["dil_w_in"][0]
                w_in = inp("dil_w_in", wi)
                swidx = np.concatenate([((g * 3 + j) * 8 + h) * 128 + (np.arange(32) + 16) % 32
                                        for g in range(3) for j in range(2) for h in range(8)])
                w_sw = inp("dil_w_sw", wi[:, swidx])
                w_out = inp("dil_w_out", inputs["dil_w_out"][0])
                dc2 = inp("dil_c2", consts["dil_c2"]); ds2 = inp("dil_s2", consts["dil_s2"])
                mle = inp("mask_le", consts["mask_le"])
                mge = inp("mask_ge", consts["mask_ge"])
                U_d = k.dram("dil_U", [3, S, 1024], F32)
                Den_d = k.dram("dil_Den", [3, S, 8], F32)
                phase_dil(k, xT0, w_in, w_sw, dc2, ds2, mle, mge, U_d, Den_d)
                phase_dil_out(k, U_d, Den_d, w_out, lnrow(ln_g, L, 0), lnrow(ln_b, L, 0), cur, xa, xT1, ident)
            elif L == 2:
                w_down = inp("mla_w_down", inputs["mla_w_down"][0])
                nq_ = inp("mla_norm_q", inputs["mla_norm_q"][0])
                nkv_ = inp("mla_norm_kv", inputs["mla_norm_kv"][0])
                wuq = inputs["mla_w_uq"][0]
                w_uq = inp("mla_w_uq", wuq)
                swidx = np.concatenate([h * 192 + 128 + (np.arange(64) + 32) % 64 for h in range(16)])
                w_uqsw = inp("mla_w_uqsw", wuq[:, swidx])
                w_ukv = inp("mla_w_ukv", inputs["mla_w_ukv"][0])
                w_out = inp("mla_w_out", inputs["mla_w_out"][0])
                cosT = inp("mla_cosT", consts["mla_cosT"])
                sinT = inp("mla_sinT", consts["mla_sinT"])
                c2 = inp("mla_c2", consts["mla_c2"])
                s2 = inp("mla_s2", consts["mla_s2"])
                mask4 = inp("mla_mask4", consts["mla_mask4"])
                cT = k.dram("mla_cT", [4, 128, S], BF16)
                qnT = k.dram("mla_qnT", [16, 128, S], BF16)
                qpT = k.dram("mla_qpT", [16, 64, S], BF16)
                knT = k.dram("mla_knT", [16, 128, S], BF16)
                v_d = k.dram("mla_v", [S, 2048], BF16)
                phase_mla_down(k, xT0, w_down, V(nq_.h, nq_.buf), V(nkv_.h, nkv_.buf), cosT, sinT, ident, cT)
                phase_mla_up(k, cT, w_uq, w_uqsw, w_ukv, c2, s2, qnT, qpT, knT, v_d)
                phase_mla_attn(k, qnT, qpT, knT, cT, v_d, mask4, ogT)
                phase_outproj(k, ogT, w_out, 16, lnrow(ln_g, L, 0), lnrow(ln_b, L, 0), cur, xa, xT1, ident)
            elif L == 3:
                mu_d = inp("rwkv_mu", inputs["rwkv_mu"][0])
                w_rkv = inp("rwkv_w_rkv", inputs["rwkv_w_rkv"][0])
                w_out = inp("rwkv_w_out", inputs["rwkv_w_out"][0])
                vec_d = inp("rwkv_vec", inputs["rwkv_vec"][0])
                la_d = inp("rwkv_lora_a", inputs["rwkv_lora_a"][0])
                lb_d = inp("rwkv_lora_b", inputs["rwkv_lora_b"][0])
                ga_d = inp("rwkv_gate_a", inputs["rwkv_gate_a"][0])
                gb_d = inp("rwkv_gate_b", inputs["rwkv_gate_b"][0])
                lnx_d = inp("rwkv_ln_x", inputs["rwkv_ln_x"][0])
                fm_d = k.dram("rw_fm", [5, 1024, S], F32)
                rv_d = k.dram("rw_v", [S, 1024], F32)
                rg_d = k.dram("rw_g", [S, 1024], F32)
                rbo_d = k.dram("rw_bonus", [S, 1024], F32)
                ry_d = k.dram("rw_y", [S, 1024], F32)
                if "rwseq" not in DBG:
                    tm_d = k.dram("rw_tm", [2, S, 1024], F32)
                    wc_d = k.dram("rw_wc", [1024, 64], F32)
                    ustr = inp("rw_ustr", consts["rw_ustr"]); uincl = inp("rw_uincl", consts["rw_uincl"])
                    lstr = inp("rw_lstr", consts["rw_lstr"]); chsel = inp("rw_chsel", consts["rw_chsel"])
                    phase_rwkv_proj(k, xT0, mu_d, w_rkv, vec_d, la_d, lb_d, ga_d, gb_d, ident, fm_d, rv_d, rg_d, rbo_d,
                                    tm_d=tm_d, wc_d=wc_d, uincl_d=uincl, chsel_d=chsel)
                    if "rwA" not in DBG:
                        phase_rwkv_chunk(k, fm_d, tm_d, rv_d, wc_d, ustr, uincl, lstr, ident, ry_d)
                else:
                    phase_rwkv_proj(k, xT0, mu_d, w_rkv, vec_d, la_d, lb_d, ga_d, gb_d, ident, fm_d, rv_d, rg_d, rbo_d)
                if "rwseq" in DBG and "rwA" not in DBG and "rwC" not in DBG:
                    phase_rwkv_scan(k, fm_d, rv_d, ry_d, nsteps=(int(os.environ.get("RWN", S))))
                if "rwA" not in DBG and "rwB" not in DBG:
                    phase_rwkv_out(k, ry_d, rbo_d, rg_d, lnx_d, w_out, lnrow(ln_g, L, 0), lnrow(ln_b, L, 0), cur, xa, xT1, ident)
                else:
                    phase_outproj(k, T(ogT.h[0:1024, :], ogT.buf), w_out, 8, lnrow(ln_g, L, 0), lnrow(ln_b, L, 0), cur, xa, xT1, ident)
            else:
                raise NotImplementedError
            dst = out if last else xb_
            phase_mlp(k, T(mlp_w1.h[L], mlp_w1.buf), T(mlp_w2.h[L], mlp_w2.buf), lnrow(ln_g, L, 1), lnrow(ln_b, L, 1),
                      xa, xT1, dst, (None if last else xT0), ident)
            cur = xb_
        k.barrier()
    in_maps = []
    for c in range(ncores):
        m = dict(shared)
        m["x"] = np.ascontiguousarray(inputs["x"][c])
        in_maps.append(m)
    return nc, in_maps


MLA_H = 16
MLA_SCALE = 192.0 ** -0.5


def phase_mla_down(k, xT_in, w_down_d, nq_d, nkv_d, cosT_d, sinT_d, ident_d, cT_d):
    k.begin_phase()
    identb = k.sb([128, 128], BF16, "identb")
    k.dma(identb[:, :], ident_d[:, :], q="pool")
    w = k.sb([128, 8, 448], BF16, "wdown")
    load_w_bf16(k, w[:, :, :], w_down_d.h.rearrange("(c p) f -> p c f", p=128), w_down_d.buf)
    gq = k.sb([128, 384], F32, "gq")
    k.dma(gq[:, 0:256], V(nq_d.ap.partition_broadcast(128), nq_d.buf))
    k.dma(gq[:, 256:384], V(nkv_d.ap.partition_broadcast(128), nkv_d.buf))
    xb = [k.sb([128, 8, 128], BF16, "xb") for _ in range(2)]
    cs = [k.sb([128, 32], F32, "cs") for _ in range(2)]
    sn = [k.sb([128, 32], F32, "sn") for _ in range(2)]
    sq = k.sb([128, 384], F32, "sq")
    ss = [k.sb([128, 2], F32, "ss") for _ in range(2)]
    cn = [k.sb([128, 384], F32, "cn") for _ in range(2)]
    cc = [k.sb([128, 448], BF16, "cc") for _ in range(2)]
    t1 = k.sb([128, 64], F32, "t1")
    t2 = k.sb([128, 64], F32, "t2")
    cTs = [k.sb([128, 4, 128], BF16, "cTs") for _ in range(2)]
    pc = [k.ps([128, 448], F32, "pc") for _ in range(2)]
    pt = [k.ps([128, 4, 128], BF16, "pt") for _ in range(2)]
    xTv = xT_in.h.rearrange("(c p) t -> p c t", p=128)
    nt = S // 128

    def load(ti):
        i = ti % 2
        sl = slice(ti * 128, (ti + 1) * 128)
        k.dma(xb[i][:, :, :], V(xTv[:, :, sl], xT_in.buf))
        k.dma(cs[i][:, :], cosT_d[sl, :])
        k.dma(sn[i][:, :], sinT_d[sl, :])

    load(0)
    for ti in range(nt):
        i = ti % 2
        sl = slice(ti * 128, (ti + 1) * 128)
        if ti + 1 < nt:
            load(ti + 1)
        p = pc[i]
        for dc in range(8):
            k.mm(p[:, :], xb[i][:, dc, :], w[:, dc, :], start=(dc == 0), stop=(dc == 7))
        k.act(sq[:, :], p[:, 0:384], AF.Square)
        k.reduce(ss[i][:, 0:1], sq[:, 0:256], ALU.add)
        k.reduce(ss[i][:, 1:2], sq[:, 256:384], ALU.add)
        k.act(ss[i][:, 0:1], ss[i][:, 0:1], AF.Ln, bias=1e-6, scale=1.0 / 256)
        k.act(ss[i][:, 1:2], ss[i][:, 1:2], AF.Ln, bias=1e-6, scale=1.0 / 128)
        k.act(ss[i][:, :], ss[i][:, :], AF.Exp, scale=-0.5)
        k.ts(cn[i][:, 0:256], p[:, 0:256], ss[i][:, 0:1], None, op0=ALU.mult)
        k.ts(cn[i][:, 256:384], p[:, 256:384], ss[i][:, 1:2], None, op0=ALU.mult)
        k.tt(cc[i][:, 0:384], cn[i][:, :], gq[:, :], ALU.mult)
        k.tt(t1[:, 0:32], p[:, 384:416], cs[i][:, :], ALU.mult)
        k.tt(t2[:, 0:32], p[:, 416:448], sn[i][:, :], ALU.mult)
        k.tt(cc[i][:, 384:416], t1[:, 0:32], t2[:, 0:32], ALU.subtract)
        k.tt(t1[:, 32:64], p[:, 416:448], cs[i][:, :], ALU.mult)
        k.tt(t2[:, 32:64], p[:, 384:416], sn[i][:, :], ALU.mult)
        k.tt(cc[i][:, 416:448], t1[:, 32:64], t2[:, 32:64], ALU.add)
        for c in range(3):
            k.tr(pt[i][:, c, :], cc[i][:, c * 128:(c + 1) * 128], identb[:, :])
        k.tr(pt[i][0:64, 3, :], cc[i][:, 384:448], identb[:, :])
        k.copy(cTs[i][:, 0:3, :], pt[i][:, 0:3, :], e="act")
        k.copy(cTs[i][0:64, 3, :], pt[i][0:64, 3, :], e="act")
        k.dma(V(cT_d.h[0:3, :, sl].rearrange("c p t -> p c t"), cT_d.buf), cTs[i][:, 0:3, :])
        k.dma(V(cT_d.h[3, 0:64, sl], cT_d.buf), cTs[i][0:64, 3, :])
    k.end_phase()


def phase_mla_up(k, cT_d, w_uq_d, w_uqsw_d, w_ukv_d, c2_d, s2_d, qnT_d, qpT_d, knT_d, v_d):
    k.begin_phase()
    wq = k.sb([128, 2, 3072], BF16, "wq")
    wqs = k.sb([128, 2, 1024], BF16, "wqs")
    wkv = k.sb([128, 4096], BF16, "wkv")
    for rc in range(2):
        load_w_bf16(k, wq[:, rc, :], w_uq_d.h[rc * 128:(rc + 1) * 128, :], w_uq_d.buf)
        load_w_bf16(k, wqs[:, rc, :], w_uqsw_d.h[rc * 128:(rc + 1) * 128, :], w_uqsw_d.buf)
    load_w_bf16(k, wkv[:, :], w_ukv_d.h[:, :], w_ukv_d.buf)
    cb = [k.sb([128, 3, 512], BF16, "cb") for _ in range(2)]
    c2 = [k.sb([64, 512], F32, "c2") for _ in range(2)]
    s2 = [k.sb([64, 512], F32, "s2") for _ in range(2)]
    ob = [k.sb([128, 512], BF16, "ob") for _ in range(3)]
    t1 = k.sb([64, 512], F32, "t1")
    t2 = k.sb([64, 512], F32, "t2")
    pp = [k.ps([128, 512], F32, "pp") for _ in range(6)]
    nb = S // 512
    cTv = cT_d.h[0:3, :, :].rearrange("c p t -> p c t")
    wkv4 = wkv.h[:, :].rearrange("p (h two d) -> p h two d", two=2, d=128)

    def load(bi):
        i = bi % 2
        sl = slice(bi * 512, (bi + 1) * 512)
        k.dma(cb[i][:, :, :], V(cTv[:, :, sl], cT_d.buf))
        k.dma(c2[i][:, :], c2_d[:, sl])
        k.dma(s2[i][:, :], s2_d[:, sl])

    load(0)
    n = 0
    m = 0
    for bi in range(nb):
        i = bi % 2
        sl = slice(bi * 512, (bi + 1) * 512)
        if bi + 1 < nb:
            load(bi + 1)
        for h in range(MLA_H):
            p = pp[n % 6]; n += 1
            o = ob[m % 3]; m += 1
            for rc in range(2):
                k.mm(p[:, :], wq[:, rc, h * 192:h * 192 + 128], cb[i][:, rc, :], start=(rc == 0), stop=(rc == 1))
            k.act(o[:, :], p[:, :], AF.Copy, scale=MLA_SCALE)
            k.dma(qnT_d[h, :, sl], o[:, :])
            pa = pp[n % 6]; n += 1
            pb = pp[n % 6]; n += 1
            o = ob[m % 3]; m += 1
            for rc in range(2):
                k.mm(pa[0:64, :], wq[:, rc, h * 192 + 128:h * 192 + 192], cb[i][:, rc, :], start=(rc == 0), stop=(rc == 1))
            for rc in range(2):
                k.mm(pb[0:64, :], wqs[:, rc, h * 64:(h + 1) * 64], cb[i][:, rc, :], start=(rc == 0), stop=(rc == 1))
            k.tt(t1[:, :], pa[0:64, :], c2[i][:, :], ALU.mult)
            k.tt(t2[:, :], pb[0:64, :], s2[i][:, :], ALU.mult)
            k.tt(o[0:64, :], t1[:, :], t2[:, :], ALU.add)
            k.dma(qpT_d[h, :, sl], o[0:64, :])
            p = pp[n % 6]; n += 1
            o = ob[m % 3]; m += 1
            k.mm(p[:, :], wkv[:, h * 256:h * 256 + 128], cb[i][:, 2, :], start=True, stop=True)
            k.copy(o[:, :], p[:, :], e="act")
            k.dma(knT_d[h, :, sl], o[:, :])
        for tt in range(4):
            rows = slice(bi * 512 + tt * 128, bi * 512 + (tt + 1) * 128)
            for hg in range(4):
                p = pp[n % 6]; n += 1
                o = ob[m % 3]; m += 1
                k.mm(V(p.h[:, :].rearrange("p (h d) -> p h d", d=128), p.buf), cb[i][:, 2, tt * 128:(tt + 1) * 128],
                     V(wkv4[:, hg * 4:hg * 4 + 4, 1, :], wkv.buf), start=True, stop=True)
                k.copy(o[:, :], p[:, :], e="act")
                k.dma(v_d[rows, hg * 512:(hg + 1) * 512], o[:, :])
    k.end_phase()


def phase_mla_attn(k, qnT_d, qpT_d, knT_d, cT_d, v_d, mask4_d, ogT_d):
    k.begin_phase()
    masks = k.sb([128, 4, 512], BF16, "masks")
    k.dma(masks[:, :, :], V(mask4_d.h.rearrange("m p q -> p m q"), mask4_d.buf), q="pool")
    ones = k.sb([128, 128], BF16, "ones")
    k.memset(ones[:, :], 1.0)
    kp = k.sb([64, S], BF16, "kp")
    k.dma(kp[:, :], cT_d[3, 0:64, :])
    qn = [k.sb([128, S], BF16, "qn") for _ in range(2)]
    qp = [k.sb([64, S], BF16, "qp") for _ in range(2)]
    kn = [k.sb([128, S], BF16, "kn") for _ in range(2)]
    vv = [k.sb([128, 32, 128], BF16, "vv") for _ in range(2)]
    pb = [k.sb([128, 512], BF16, "pb") for _ in range(4)]
    rd = [k.sb([128, 512], F32, "rd") for _ in range(2)]
    oo = [k.sb([128, 512], BF16, "oo") for _ in range(2)]
    psT = [k.ps([128, 512], F32, "psT") for _ in range(4)]
    po = [k.ps([128, 512], F32, "po") for _ in range(2)]
    pd = [k.ps([128, 512], F32, "pd") for _ in range(2)]

    def load(h):
        i = h % 2
        k.dma(qn[i][:, :], qnT_d[h, :, :])
        k.dma(qp[i][:, :], qpT_d[h, :, :])
        k.dma(kn[i][:, :], knT_d[h, :, :])
        k.dma(vv[i][:, :, :], V(v_d.h[:, h * 128:(h + 1) * 128].rearrange("(n p) f -> p n f", p=128), v_d.buf))

    load(0)
    LA = 3
    state = {"n": 0}
    for h in range(MLA_H):
        i = h % 2
        if h + 1 < MLA_H:
            load(h + 1)
        items = [(Q, kt) for Q in range(8) for kt in range(4 * Q + 4)]
        slots = {}

        def emit_qk(it, i=i, slots=slots):
            Q, kt = it
            qs = slice(Q * 512, (Q + 1) * 512)
            ks = slice(kt * 128, (kt + 1) * 128)
            n = state["n"]
            state["n"] += 1
            ps = psT[n % 4]
            p = pb[n % 4]
            slots[it] = p
            k.mm(ps[:, :], kn[i][:, ks], qn[i][:, qs], start=True, stop=False)
            k.mm(ps[:, :], kp[:, ks], qp[i][:, qs], start=False, stop=True)
            k.act(p[:, :], ps[:, :], AF.Exp)
            if kt >= 4 * Q:
                k.tt(p[:, :], p[:, :], masks[:, kt - 4 * Q, :], ALU.mult)

        def emit_pv(it, i=i, h=h, slots=slots):
            Q, kt = it
            qs = slice(Q * 512, (Q + 1) * 512)
            j = Q % 2
            nk = 4 * Q + 4
            p = slots.pop(it)
            k.mm(po[j][:, :], vv[i][:, kt, :], p[:, :], start=(kt == 0), stop=(kt == nk - 1))
            k.mm(pd[j][:, :], ones[:, :], p[:, :], start=(kt == 0), stop=(kt == nk - 1))
            if kt == nk - 1:
                k.op("dve", lambda en, j=j: en.reciprocal(rd[j].h[:, :], pd[j].h[:, :]), [pd[j][:, :]], [rd[j][:, :]])
                k.tt(oo[j][:, :], po[j][:, :], rd[j][:, :], ALU.mult)
                k.dma(ogT_d[h * 128:(h + 1) * 128, qs], oo[j][:, :])

        for idx, it in enumerate(items):
            emit_qk(it)
            if idx >= LA:
                emit_pv(items[idx - LA])
        for it in items[len(items) - LA:]:
            emit_pv(it)
    k.end_phase()


DIL_PAIRS = ((128, 1), (512, 4), (2048, 16))


def phase_dil(k, xT_in, w_in_d, w_sw_d, c2_d, s2_d, mle_d, mge_d, U_d, Den_d):
    k.begin_phase()
    xs = k.sb([128, 8, S], BF16, "xs")
    xTv = xT_in.h.rearrange("(c p) t -> p c t", p=128)
    for c in range(8):
        k.dma(xs[:, c, :], V(xTv[:, c, :], xT_in.buf))
    c2 = k.sb([32, S], F32, "c2"); s2 = k.sb([32, S], F32, "s2")
    for t_, d_ in ((c2, c2_d), (s2, s2_d)):
        k.dma(t_[:, :], d_[:, :])
    mle = k.sb([128, 128], BF16, "mle"); mge = k.sb([128, 128], BF16, "mge")
    k.dma(mle[:, :], mle_d[:, :], q="pool")
    k.dma(mge[:, :], mge_d[:, :], q="pool")
    ones = k.sb([128, 8], BF16, "ones")
    k.memset(ones[:, :], 1.0)
    wg = k.sb([128, 8, 3072], BF16, "wg")
    wsw = k.sb([128, 8, 512], BF16, "wsw")
    qS = k.sb([128, 8, 512], BF16, "qS")
    kS = [k.sb([128, 8, 512], BF16, "kS") for _ in range(2)]
    vb = [k.sb([128, 1024], BF16, "vb") for _ in range(2)]
    t1 = k.sb([32, 2, 512], F32, "t1"); t2 = k.sb([32, 2, 512], F32, "t2")
    pc = [k.sb([128, 4, 128], BF16, "pc") for _ in range(2)]
    ppv = [k.sb([128, 4, 128], BF16, "ppv") for _ in range(2)]
    uo = [k.sb([128, 1024], F32, "uo") for _ in range(2)]
    dn = [k.sb([128, 8], F32, "dn") for _ in range(2)]
    A2 = k.ps([128, 1024], F32, "A2")
    B2 = k.ps([128, 1024], F32, "B2")
    Vp = k.ps([128, 1024], F32, "Vp")
    X = [k.ps([128, 4, 128], F32, "X") for _ in range(2)]
    A2p = T(A2.h[:, :].rearrange("p (a b) -> p a b", b=512), A2.buf)
    B2p = T(B2.h[:, :].rearrange("p (a b) -> p a b", b=512), B2.buf)
    A = T(A2.h[:, :].rearrange("p (a b) -> p a b", b=128), A2.buf)
    wv = w_in_d.h.rearrange("(c p) f -> p c f", p=128)
    wsv = w_sw_d.h.rearrange("(c p) f -> p c f", p=128)
    nblk = 0
    nsb = 0
    for g, (window, dil) in enumerate(DIL_PAIRS):
        if "dilg" in DBG and ("dilg%d" % g) not in DBG:
            continue
        for c in range(8):
            load_w_bf16(k, wg[:, c, :], wv[:, c, g * 3072:(g + 1) * 3072], w_in_d.buf)
        load_w_bf16(k, wsw[:, :, :], wsv[:, :, g * 512:(g + 1) * 512], w_sw_d.buf)
        L = S // dil
        nblk_r = L // 128
        SBk = min(4, nblk_r)
        ntok = SBk * 128
        for r in range(dil):
            for sb0 in range(0, nblk_r, SBk):
                cs_ = nsb % 2
                nsb += 1
                st_ = r + dil * sb0 * 128
                idxs = slice(st_, st_ + dil * (ntok - 1) + 1, dil)
                for j, dstS in enumerate((qS, kS[cs_])):
                    qsc = 128.0 ** -0.5 if j == 0 else 1.0
                    for hp in range(4):
                        for hh in range(2):
                            h = hp * 2 + hh
                            f0 = (j * 8 + h) * 128
                            for dc in range(8):
                                k.mm(A2p[:, hh, 0:ntok], wg[:, dc, f0:f0 + 128], xs[:, dc, idxs], start=(dc == 0), stop=(dc == 7))
                            f0 = (j * 8 + h) * 32
                            for dc in range(8):
                                k.mm(B2p[0:32, hh, 0:ntok], wsw[:, dc, f0:f0 + 32], xs[:, dc, idxs], start=(dc == 0), stop=(dc == 7))
                        hs = slice(hp * 2, hp * 2 + 2)
                        k.ts(dstS[:, hs, 0:ntok], A2p[:, :, 0:ntok], qsc, None, op0=ALU.mult)
                        cb_ = V(c2.h[0:32, idxs].unsqueeze(1).broadcast_to([32, 2, ntok]), c2.buf)
                        sb_ = V(s2.h[0:32, idxs].unsqueeze(1).broadcast_to([32, 2, ntok]), s2.buf)
                        k.stt(t1[:, :, 0:ntok], A2p[0:32, :, 0:ntok], qsc, cb_, ALU.mult, ALU.mult)
                        k.stt(t2[:, :, 0:ntok], B2p[0:32, :, 0:ntok], qsc, sb_, ALU.mult, ALU.mult)
                        k.tt(dstS[0:32, hs, 0:ntok], t1[:, :, 0:ntok], t2[:, :, 0:ntok], ALU.add)
                for bl in range(SBk):
                    blk = sb0 + bl
                    start = r + dil * blk * 128
                    idx = slice(start, start + dil * 127 + 1, dil)
                    bs_ = slice(bl * 128, (bl + 1) * 128)
                    cur = nblk % 2
                    nblk += 1
                    for half in range(2):
                        for dc in range(8):
                            k.mm(Vp[:, half * 512:(half + 1) * 512], xs[:, dc, idx],
                                 wg[:, dc, 2048 + half * 512:2048 + (half + 1) * 512], start=(dc == 0), stop=(dc == 7))
                    k.copy(vb[cur][:, :], Vp[:, :], e="act")
                    has_prev = blk > 0
                    if bl > 0:
                        kprev, ps_ = kS[cs_], slice((bl - 1) * 128, bl * 128)
                    else:
                        kprev, ps_ = kS[1 - cs_], slice((SBk - 1) * 128, SBk * 128)
                    o_ = uo[cur]
                    d_ = dn[cur]
                    for hg in range(2):
                        sc, sp = X[0], X[1]
                        for hh in range(4):
                            h = hg * 4 + hh
                            k.mm(sc[:, hh, :], kS[cs_][:, h, bs_], qS[:, h, bs_], start=True, stop=True)
                        if has_prev:
                            for hh in range(4):
                                h = hg * 4 + hh
                                k.mm(sp[:, hh, :], kprev[:, h, ps_], qS[:, h, bs_], start=True, stop=True)
                        pcur, pprev = pc[hg], ppv[hg]
                        k.act(pcur[:, :, :], sc[:, :, :], AF.Exp)
                        k.tt(pcur[:, :, :], pcur[:, :, :], V(mle.h[:, :].unsqueeze(1).broadcast_to([128, 4, 128]), mle.buf), ALU.mult)
                        if has_prev:
                            k.act(pprev[:, :, :], sp[:, :, :], AF.Exp)
                            k.tt(pprev[:, :, :], pprev[:, :, :], V(mge.h[:, :].unsqueeze(1).broadcast_to([128, 4, 128]), mge.buf), ALU.mult)
                        for hh in range(4):
                            h = hg * 4 + hh
                            k.mm(A[:, h, :], pcur[:, hh, :], vb[cur][:, h * 128:(h + 1) * 128], start=True, stop=not has_prev)
                            if has_prev:
                                k.mm(A[:, h, :], pprev[:, hh, :], vb[1 - cur][:, h * 128:(h + 1) * 128], start=False, stop=True)
                            k.mm(B2[:, h:h + 1], pcur[:, hh, :], ones[:, 0:1], start=True, stop=not has_prev)
                            if has_prev:
                                k.mm(B2[:, h:h + 1], pprev[:, hh, :], ones[:, 0:1], start=False, stop=True)
                    k.copy(V(o_.h[:, :].rearrange("p (h d) -> p h d", d=128), o_.buf), A[:, :, :], e="dve")
                    k.copy(d_[:, :], B2[:, 0:8], e="dve")
                    k.dma(U_d[g, idx, :], o_[:, :])
                    k.dma(Den_d[g, idx, :], d_[:, :])
    k.end_phase()


def phase_dil_out(k, U_d, Den_d, w_out_d, g_row, b_row, x_in, x_out, xT_out, ident_d):
    k.begin_phase()
    ident = k.sb([128, 128], F32, "ident")
    k.dma(ident[:, :], ident_d[:, :])
    identb = k.sb([128, 128], BF16, "identb")
    k.dma(identb[:, :], ident_d[:, :], q="pool")
    w = k.sb([128, 8, D], BF16, "wout")
    load_w_bf16(k, w[:, :, :], w_out_d.h.rearrange("(c p) d -> p c d", p=128), w_out_d.buf)
    ln = LNStage(k, ident, g_row, b_row, x_in, x_out, xT_out)
    U = [[k.sb([128, 1024], F32, "U") for _ in range(3)] for _ in range(2)]
    Dn = [[k.sb([128, 8], F32, "Dn") for _ in range(3)] for _ in range(2)]
    ob = [k.sb([128, 1024], BF16, "ob") for _ in range(2)]
    oT = [k.sb([128, 8, 128], BF16, "oT") for _ in range(2)]
    po = [k.ps([128, 512], F32, "po") for _ in range(2)]
    pt = [k.ps([128, 8, 128], BF16, "pt") for _ in range(1)]
    nt = S // 128

    def load(ti):
        i = ti % 2
        sl = slice(ti * 128, (ti + 1) * 128)
        for g in range(3):
            k.dma(U[i][g][:, :], U_d[g, sl, :])
            k.dma(Dn[i][g][:, :], Den_d[g, sl, :])

    load(0)
    for ti in range(nt):
        i = ti % 2
        if ti + 1 < nt:
            load(ti + 1)
        ln.prefetch(ti)
        k.tt(U[i][0][:, :], U[i][0][:, :], U[i][1][:, :], ALU.add)
        k.tt(U[i][0][:, :], U[i][0][:, :], U[i][2][:, :], ALU.add)
        k.tt(Dn[i][0][:, :], Dn[i][0][:, :], Dn[i][1][:, :], ALU.add)
        k.tt(Dn[i][0][:, :], Dn[i][0][:, :], Dn[i][2][:, :], ALU.add)
        k.op("dve", lambda en, i=i: en.reciprocal(Dn[i][1].h[:, :], Dn[i][0].h[:, :]), [Dn[i][0][:, :]], [Dn[i][1][:, :]])
        k.tt(V(ob[i].h[:, :].rearrange("p (h d) -> p h d", d=128), ob[i].buf),
             V(U[i][0].h[:, :].rearrange("p (h d) -> p h d", d=128), U[i][0].buf),
             V(Dn[i][1].h[:, :].unsqueeze(2).broadcast_to([128, 8, 128]), Dn[i][1].buf), ALU.mult)
        for c in range(8):
            k.tr(pt[0][:, c, :], ob[i][:, c * 128:(c + 1) * 128], identb[:, :])
        k.copy(oT[i][:, :, :], pt[0][:, :, :], e="act")
        for hh in range(2):
            for fc in range(8):
                k.mm(po[hh][:, :], oT[i][:, fc, :], w[:, fc, hh * 512:(hh + 1) * 512], start=(fc == 0), stop=(fc == 7))
        ln.run(ti, [po[0][:, :], po[1][:, :]])
    ln.flush()
    k.end_phase()


RW_H = 16
SCAN_AUX = "dve"
RWKV_GN_EPS = 64e-5
EXP_M05 = float(np.exp(-0.5))


def bc3(t, lo, hi, n):
    return V(t.h[:, lo:hi].unsqueeze(2).broadcast_to([t.h.shape[0], hi - lo, n]), t.buf)


def v3(t, d=64):
    return V(t.h[:, :].rearrange("p (h d) -> p h d", d=d), t.buf)


def phase_rwkv_proj(k, xT_in, mu_d, w_rkv_d, vec_d, la_d, lb_d, ga_d, gb_d, ident_d,
                    fm_d, v_d, g_d, bonus_d, tm_d=None, wc_d=None, uincl_d=None, chsel_d=None):
    k.begin_phase()
    ident = k.sb([128, 128], F32, "ident")
    k.dma(ident[:, :], ident_d[:, :])
    wr = k.sb([128, 3, 8, 1024], BF16, "w_rkv")
    for m in range(3):
        for c in range(0, 8, 4):
            load_w_bf16(k, wr[:, m, c:c + 4, :], w_rkv_d.h[m].rearrange("(c p) f -> p c f", p=128)[:, c:c + 4, :], w_rkv_d.buf)
    la = k.sb([128, 2, 8, 64], BF16, "la")
    for m in range(2):
        load_w_bf16(k, la[:, m, :, :], la_d.h[m].rearrange("(c p) f -> p c f", p=128), la_d.buf)
    lb = k.sb([64, 2, 1024], BF16, "lb")
    load_w_bf16(k, lb[:, :, :], lb_d.h.rearrange("m r f -> r m f"), lb_d.buf)
    ga = k.sb([128, 8, 160], BF16, "ga")
    load_w_bf16(k, ga[:, :, :], ga_d.h.rearrange("(c p) f -> p c f", p=128), ga_d.buf)
    gb1 = k.sb([128, 1024], BF16, "gb1")
    gb2 = k.sb([32, 1024], BF16, "gb2")
    load_w_bf16(k, gb1[:, :], gb_d.h[0:128, :], gb_d.buf)
    load_w_bf16(k, gb2[:, :], gb_d.h[128:160, :], gb_d.buf)
    vec = k.sb([128, 5, 1024], F32, "vec")
    for m in range(5):
        k.dma(vec[:, m, :], V(vec_d.h[m, :].partition_broadcast(128), vec_d.buf))
    mu = k.sb([128, 6, 8], F32, "mu")
    mur = k.sb([48, 128], F32, "mur")
    k.dma(mur[:, :], V(mu_d.h.rearrange("m (c p) -> (m c) p", p=128), mu_d.buf))
    xb = [k.sb([128, 8, 129], BF16, "xb") for _ in range(2)]
    xx = k.sb([128, 8, 128], F32, "xx")
    xm = [k.sb([128, 8, 128], BF16, "xm") for _ in range(6)]
    tmpx = k.sb([128, 8, 128], F32, "tmpx")
    hw = k.sb([64, 2, 128], BF16, "hw")
    hg1 = k.sb([128, 128], BF16, "hg1")
    hg2 = k.sb([32, 128], BF16, "hg2")
    R = k.sb([128, 1024], F32, "R"); KR = k.sb([128, 1024], F32, "KR"); Vv = k.sb([128, 1024], F32, "Vv")
    Wd = k.sb([128, 1024], F32, "Wd"); Aa = k.sb([128, 1024], F32, "Aa"); KK = k.sb([128, 1024], F32, "KK")
    Kk = k.sb([128, 1024], F32, "Kk"); Bb = k.sb([128, 1024], F32, "Bb"); Gg = k.sb([128, 1024], F32, "Gg")
    Bo = k.sb([128, 1024], F32, "Bo"); T1 = k.sb([128, 1024], F32, "T1")
    ssum = k.sb([128, 16], F32, "ssum"); rn = k.sb([128, 16], F32, "rn"); rk = k.sb([128, 16], F32, "rk")
    chunked = tm_d is not None
    fmT = [k.sb([128, 16 if chunked else 8, 128], F32, "fmT") for _ in range(2)]
    if chunked:
        uincl = k.sb([128, 128], F32, "uincl")
        k.dma(uincl[:, :], uincl_d[:, :])
        chsel = k.sb([128, 2], F32, "chsel")
        k.dma(chsel[:, :], chsel_d[:, :])
        LWt = k.sb([128, 1024], F32, "LWt")
        E1 = k.sb([128, 1024], F32, "E1")
        wct = [k.sb([128, 8, 2], F32, "wct") for _ in range(2)]
    pA = [k.ps([128, 1024], F32, "pA") for _ in range(2)]
    pS = k.ps([128, 512], F32, "pS")
    pT = [k.ps([128, 4, 128], F32, "pT") for _ in range(2)]
    xTv = xT_in.h.rearrange("(c p) t -> p c t", p=128)
    nt = S // 128

    def load(ti):
        i = ti % 2
        if ti == 0:
            k.memset(xb[i][:, :, 0:1], 0.0)
            k.dma(xb[i][:, :, 1:129], V(xTv[:, :, 0:128], xT_in.buf))
        else:
            k.dma(xb[i][:, :, :], V(xTv[:, :, ti * 128 - 1:ti * 128 + 128], xT_in.buf))

    def proj_tm(dst_ps, xmix, wsel):
        for half in range(2):
            for dc in range(8):
                k.mm(dst_ps[:, half * 512:(half + 1) * 512], xmix[:, dc, :], wsel[:, dc, half * 512:(half + 1) * 512],
                     start=(dc == 0), stop=(dc == 7))

    k.tr(pS[:, 0:48], mur[:, :], ident[0:48, 0:48])
    k.copy(V(mu.h[:, :, :].rearrange("p m c -> p (m c)"), mu.buf), pS[:, 0:48])
    nfm = 0
    load(0)
    for ti in range(nt):
        i = ti % 2
        sl = slice(ti * 128, (ti + 1) * 128)
        if ti + 1 < nt:
            load(ti + 1)
        cur = xb[i]
        k.tt(xx[:, :, :], cur[:, :, 0:128], cur[:, :, 1:129], ALU.subtract)
        for m in range(6):
            k.tt(tmpx[:, :, :], xx[:, :, :], bc3(T(mu.h[:, m, :], mu.buf), 0, 8, 128), ALU.mult)
            k.tt(xm[m][:, :, :], tmpx[:, :, :], cur[:, :, 1:129], ALU.add)
        xr, xw, xk, xv, xa, xg = xm
        proj_tm(pA[0], xr, T(wr.h[:, 0], wr.buf)); k.copy(R[:, :], pA[0][:, :], e="act")
        proj_tm(pA[1], xk, T(wr.h[:, 1], wr.buf)); k.copy(KR[:, :], pA[1][:, :], e="act")
        proj_tm(pA[0], xv, T(wr.h[:, 2], wr.buf)); k.copy(Vv[:, :], pA[0][:, :], e="act")
        k.dma(v_d[sl, :], Vv[:, :])
        for dc in range(8):
            k.mm(pS[0:64, 0:128], la[:, 0, dc, :], xw[:, dc, :], start=(dc == 0), stop=(dc == 7))
        k.act(hw[:, 0, :], pS[0:64, 0:128], AF.Tanh)
        for half in range(2):
            k.mm(pA[1][:, half * 512:(half + 1) * 512], hw[:, 0, :], lb[:, 0, half * 512:(half + 1) * 512], start=True, stop=True)
        k.tt(T1[:, :], pA[1][:, :], vec[:, 0, :], ALU.add)
        k.act(T1[:, :], T1[:, :], AF.Sigmoid)
        if chunked:
            k.ts(Wd[:, :], T1[:, :], -EXP_M05, None, op0=ALU.mult)
        else:
            k.act(Wd[:, :], T1[:, :], AF.Exp, scale=-EXP_M05)
        for dc in range(8):
            k.mm(pS[0:64, 128:256], la[:, 1, dc, :], xa[:, dc, :], start=(dc == 0), stop=(dc == 7))
        k.copy(hw[:, 1, :], pS[0:64, 128:256], e="act")
        for half in range(2):
            k.mm(pA[0][:, half * 512:(half + 1) * 512], hw[:, 1, :], lb[:, 1, half * 512:(half + 1) * 512], start=True, stop=True)
        k.tt(T1[:, :], pA[0][:, :], vec[:, 1, :], ALU.add)
        k.act(Aa[:, :], T1[:, :], AF.Sigmoid)
        for dc in range(8):
            k.mm(pS[:, 256:384], ga[:, dc, 0:128], xg[:, dc, :], start=(dc == 0), stop=(dc == 7))
        for dc in range(8):
            k.mm(pS[0:32, 384:512], ga[:, dc, 128:160], xg[:, dc, :], start=(dc == 0), stop=(dc == 7))
        k.act(hg1[:, :], pS[:, 256:384], AF.Sigmoid)
        k.act(hg2[:, :], pS[0:32, 384:512], AF.Sigmoid)
        for half in range(2):
            hs = slice(half * 512, (half + 1) * 512)
            k.mm(pA[1][:, hs], hg1[:, :], gb1[:, hs], start=True, stop=False)
            k.mm(pA[1][:, hs], hg2[:, :], gb2[:, hs], start=False, stop=True)
        k.copy(Gg[:, :], pA[1][:, :], e="act")
        k.dma(g_d[sl, :], Gg[:, :])
        k.tt(KK[:, :], KR[:, :], vec[:, 2, :], ALU.mult)
        k.act(T1[:, :], KK[:, :], AF.Square)
        k.reduce(ssum[:, :], v3(T1), ALU.add)
        k.ts(ssum[:, :], ssum[:, :], 1e-24, None, op0=ALU.max)
        k.act(rn[:, :], ssum[:, :], AF.Ln)
        k.act(rn[:, :], rn[:, :], AF.Exp, scale=-0.5)
        k.tt(v3(KK), v3(KK), bc3(rn, 0, 16, 64), ALU.mult)
        k.stt(T1[:, :], Aa[:, :], -1.0, vec[:, 3, :], ALU.add, ALU.mult)
        k.stt(Kk[:, :], T1[:, :], 1.0, KR[:, :], ALU.add, ALU.mult)
        k.tt(Bb[:, :], KK[:, :], Aa[:, :], ALU.mult)
        k.tt(T1[:, :], R[:, :], Kk[:, :], ALU.mult)
        k.tt(T1[:, :], T1[:, :], vec[:, 4, :], ALU.mult)
        k.reduce(rk[:, :], v3(T1), ALU.add)
        k.tt(v3(Bo), v3(Vv), bc3(rk, 0, 16, 64), ALU.mult)
        k.dma(bonus_d[sl, :], Bo[:, :])
        if chunked:
            for half in range(2):
                hs = slice(half * 512, (half + 1) * 512)
                k.mm(pA[0][:, hs], uincl[:, :], Wd[:, hs], start=True, stop=True)
            k.copy(LWt[:, :], pA[0][:, :], e="act")
            for c in range(8):
                k.mm(pS[:, 48 + 2 * c:50 + 2 * c], Wd[:, c * 128:(c + 1) * 128], chsel[:, :], start=True, stop=True)
            wc = wct[ti % 2]
            k.act(V(wc.h[:, :, :].rearrange("p a b -> p (a b)"), wc.buf), pS[:, 48:64], AF.Exp)
            k.dma(V(wc_d.h.rearrange("(c p) n -> p c n", p=128)[:, :, 2 * ti:2 * ti + 2], wc_d.buf), wc[:, :, :])
            k.tt(T1[:, :], LWt[:, :], Wd[:, :], ALU.subtract)
            k.act(E1[:, :], T1[:, :], AF.Exp)
            k.tt(KK[:, :], KK[:, :], E1[:, :], ALU.mult)
            k.act(E1[:, :], LWt[:, :], AF.Exp)
            k.tt(R[:, :], R[:, :], E1[:, :], ALU.mult)
            k.act(E1[:, :], LWt[:, :], AF.Exp, scale=-1.0)
            k.tt(Bb[:, :], Bb[:, :], E1[:, :], ALU.mult)
            k.tt(Kk[:, :], Kk[:, :], E1[:, :], ALU.mult)
            k.dma(tm_d[0, sl, :], Bb[:, :])
            k.dma(tm_d[1, sl, :], Kk[:, :])
            fm_list = (KK, R, Bb, Kk)
        else:
            fm_list = (KK, Wd, Bb, Kk, R)
        if chunked:
            for m, src in enumerate(fm_list):
                ft = fmT[nfm % 2]
                nfm += 1
                for grp in range(4):
                    for c in range(4):
                        h = grp * 4 + c
                        k.tr(pT[grp % 2][0:64, c, :], src[:, h * 64:(h + 1) * 64], ident[:, :])
                    k.copy(V(ft.h[0:64, :, :].rearrange("p (a b) t -> p a b t", b=4)[:, grp, :, :], ft.buf),
                           pT[grp % 2][0:64, :, :], e=("act" if grp % 2 else "dve"))
                k.dma(V(fm_d.h[m].rearrange("(h j) t -> j h t", j=64)[:, :, sl], fm_d.buf),
                      V(ft.h[0:64, :, :], ft.buf) if False else V(ft.h[0:64, :, :].rearrange("p (a b) t -> p (a b) t", b=4), ft.buf))
        else:
            for m, src in enumerate(fm_list):
                ft = fmT[nfm % 2]
                nfm += 1
                for half in range(2):
                    for c in range(4):
                        cc = half * 4 + c
                        k.tr(pT[half][:, c, :], src[:, cc * 128:(cc + 1) * 128], ident[:, :])
                    k.copy(ft[:, half * 4:(half + 1) * 4, :], pT[half][:, :, :], e="act")
                k.dma(V(fm_d.h[m].rearrange("(c p) t -> p c t", p=128)[:, :, sl], fm_d.buf), ft[:, :, :])
    k.end_phase()


def phase_rwkv_scan(k, fm_d, v_d, y_d, nsteps=S):
    k.begin_phase()
    TB = 128
    ST = k.sb([128, 8, 64], F32, "ST")
    k.memset(ST[:, :, :], 0.0)
    negblk = k.sb([128, 128], F32, "negblk")
    k.memset(negblk[:, :], 0.0)
    k.memset(negblk[0:64, 0:64], -1.0)
    k.memset(negblk[64:128, 64:128], -1.0)
    sel2 = k.sb([2, 128], F32, "sel2")
    selT = k.sb([128, 2], F32, "selT")
    k.memset(selT[:, :], 0.0)
    k.memset(selT[0:64, 0:1], 1.0)
    k.memset(selT[64:128, 1:2], 1.0)
    k.ts(negblk[:, :], negblk[:, :], 1.0, None, op0=ALU.mult)
    tmpsel = k.sb([128, 128], F32, "tmpsel")
    k.ts(tmpsel[:, :], negblk[:, :], -1.0, None, op0=ALU.mult)
    k.dma(sel2[0:1, :], tmpsel[0:1, :])
    k.dma(sel2[1:2, :], tmpsel[64:65, :])
    ops = [[k.sb([128, 8, TB], F32, "fm%d" % m) for m in range(5)] for _ in range(2)]
    SB = 8
    v2 = [k.sb([2, SB, 512], F32, "v2") for _ in range(3)]
    yall = [k.sb([2, SB, 512], F32, "yall") for _ in range(2)]
    tmp = [k.sb([128, 8, 64], F32, "tmp") for _ in range(2)]
    tmp2 = k.sb([128, 8, 64], F32, "tmp2")
    tmp3 = [k.sb([128, 8, 64], F32, "tmp3") for _ in range(2)]
    tmp4 = [k.sb([128, 8, 64], F32, "tmp4") for _ in range(2)]
    vsb = [k.sb([128, 8, 64], F32, "vsb") for _ in range(2)]
    pv = [k.ps([128, 8, 64], F32, "pv") for _ in range(2)]
    psa = [k.ps([128, 8, 64], F32, "psa") for _ in range(2)]
    py = [k.ps([2, 512], F32, "py") for _ in range(2)]
    nb = nsteps // TB

    def load(bi):
        i = bi % 2
        sl = slice(bi * TB, (bi + 1) * TB)
        for m in range(5):
            k.dma(ops[i][m][:, :, :], V(fm_d.h[m].rearrange("(c p) t -> p c t", p=128)[:, :, sl], fm_d.buf))

    def load_v(si):
        sl = slice(si * SB, (si + 1) * SB)
        k.dma(V(v2[si % 3].h[:, :, :].rearrange("c t (hh i) -> c t hh i", i=64), v2[si % 3].buf),
              V(v_d.h[sl, :].rearrange("t (hh c i) -> c t hh i", c=2, i=64), v_d.buf))

    def sc(t_, tl):
        return V(t_.h[:, :, tl:tl + 1].broadcast_to([128, 8, 64]), t_.buf)

    load(0)
    load_v(0)
    load_v(1)
    nsb = nsteps // SB
    for bi in range(nb):
        i = bi % 2
        if bi + 1 < nb:
            load(bi + 1)
        kkT, wT, bT, kT_, rT = ops[i]
        for tl in range(TB):
            s = tl % 2
            tg = bi * TB + tl
            si, sl_ = tg // SB, tg % SB
            if sl_ == 0 and si + 2 < nsb:
                load_v(si + 2)
            k.mm(pv[s][:, :, :], sel2[:, :], V(v2[si % 3].h[:, sl_, :].rearrange("c (hh i) -> c hh i", i=64), v2[si % 3].buf),
                 start=True, stop=True)
            k.tt(tmp3[s][:, :, :], pv[s][:, :, :], sc(kT_, tl), ALU.mult, e=SCAN_AUX)
            k.tt(tmp[s][:, :, :], ST[:, :, :], sc(kkT, tl), ALU.mult)
            k.mm(psa[s][:, :, :], negblk[:, :], tmp[s][:, :, :], start=True, stop=True)
            k.tt(ST[:, :, :], ST[:, :, :], sc(wT, tl), ALU.mult)
            k.tt(ST[:, :, :], ST[:, :, :], tmp3[s][:, :, :], ALU.add)
            k.tt(tmp2[:, :, :], psa[s][:, :, :], sc(bT, tl), ALU.mult)
            k.tt(ST[:, :, :], ST[:, :, :], tmp2[:, :, :], ALU.add)
            k.tt(tmp4[s][:, :, :], ST[:, :, :], sc(rT, tl), ALU.mult, e=SCAN_AUX)
            k.mm(py[s][:, :], selT[:, :], V(tmp4[s].h[:, :, :].rearrange("p a b -> p (a b)"), tmp4[s].buf), start=True, stop=True)
            k.copy(yall[si % 2][:, sl_, :], py[s][:, :], e="act")
            if sl_ == SB - 1:
                sl = slice(si * SB, (si + 1) * SB)
                k.dma(V(y_d.h[sl, :].rearrange("t (hh c i) -> c t hh i", c=2, i=64), y_d.buf),
                      V(yall[si % 2].h[:, :, :].rearrange("c t (hh i) -> c t hh i", i=64), yall[si % 2].buf))
    k.end_phase()


def phase_rwkv_out(k, y_d, bonus_d, g_d, lnx_d, w_out_d, g_row, b_row, x_in, x_out, xT_out, ident_d):
    k.begin_phase()
    ident = k.sb([128, 128], F32, "ident")
    k.dma(ident[:, :], ident_d[:, :])
    identb = k.sb([128, 128], BF16, "identb")
    k.dma(identb[:, :], ident_d[:, :], q="pool")
    w = k.sb([128, 8, D], BF16, "wout")
    load_w_bf16(k, w[:, :, :], w_out_d.h.rearrange("(c p) d -> p c d", p=128), w_out_d.buf)
    lg = k.sb([128, 1024], F32, "lg"); lbb = k.sb([128, 1024], F32, "lbb")
    k.dma(lg[:, :], V(lnx_d.h[0, :].partition_broadcast(128), lnx_d.buf))
    k.dma(lbb[:, :], V(lnx_d.h[1, :].partition_broadcast(128), lnx_d.buf))
    ln = LNStage(k, ident, g_row, b_row, x_in, x_out, xT_out)
    Y = [k.sb([128, 1024], F32, "Y") for _ in range(2)]
    Bo = [k.sb([128, 1024], F32, "Bo") for _ in range(2)]
    Gg = [k.sb([128, 1024], F32, "Gg") for _ in range(2)]
    T1 = k.sb([128, 1024], F32, "T1")
    s1 = k.sb([128, 16], F32, "s1"); s2 = k.sb([128, 16], F32, "s2"); mean = k.sb([128, 16], F32, "mean")
    var = k.sb([128, 16], F32, "var"); rstd = k.sb([128, 16], F32, "rstd")
    ob = [k.sb([128, 1024], BF16, "ob") for _ in range(2)]
    oT = [k.sb([128, 8, 128], BF16, "oT") for _ in range(2)]
    po = [k.ps([128, 512], F32, "po") for _ in range(2)]
    pt = k.ps([128, 8, 128], BF16, "pt")
    nt = S // 128

    def load(ti):
        i = ti % 2
        sl = slice(ti * 128, (ti + 1) * 128)
        k.dma(Y[i][:, :], y_d[sl, :])
        k.dma(Bo[i][:, :], bonus_d[sl, :])
        k.dma(Gg[i][:, :], g_d[sl, :])

    load(0)
    for ti in range(nt):
        i = ti % 2
        if ti + 1 < nt:
            load(ti + 1)
        ln.prefetch(ti)
        y = Y[i]
        k.reduce(s1[:, :], v3(y), ALU.add)
        k.act(T1[:, :], y[:, :], AF.Square)
        k.reduce(s2[:, :], v3(T1), ALU.add)
        k.ts(mean[:, :], s1[:, :], 1.0 / 64, None, op0=ALU.mult)
        k.tt(var[:, :], mean[:, :], mean[:, :], ALU.mult)
        k.stt(var[:, :], s2[:, :], 1.0 / 64, var[:, :], ALU.mult, ALU.subtract)
        k.act(rstd[:, :], var[:, :], AF.Ln, bias=RWKV_GN_EPS)
        k.act(rstd[:, :], rstd[:, :], AF.Exp, scale=-0.5)
        k.tt(v3(y), v3(y), bc3(mean, 0, 16, 64), ALU.subtract)
        k.tt(v3(y), v3(y), bc3(rstd, 0, 16, 64), ALU.mult)
        k.tt(y[:, :], y[:, :], lg[:, :], ALU.mult)
        k.tt(y[:, :], y[:, :], lbb[:, :], ALU.add)
        k.tt(y[:, :], y[:, :], Bo[i][:, :], ALU.add)
        k.tt(ob[i][:, :], y[:, :], Gg[i][:, :], ALU.mult)
        for c in range(8):
            k.tr(pt[:, c, :], ob[i][:, c * 128:(c + 1) * 128], identb[:, :])
        k.copy(oT[i][:, :, :], pt[:, :, :], e="act")
        for hh in range(2):
            for fc in range(8):
                k.mm(po[hh][:, :], oT[i][:, fc, :], w[:, fc, hh * 512:(hh + 1) * 512], start=(fc == 0), stop=(fc == 7))
        ln.run(ti, [po[0][:, :], po[1][:, :]])
    ln.flush()
    k.end_phase()


def kernel(**inputs):
    inputs = {k_: np.asarray(v_) for k_, v_ in inputs.items()}
    nc, in_maps = build_all(inputs, layers=(0, 1, 2, 3), ncores=NCORES)
    res = run_bass_kernel_spmd(nc, in_maps, core_ids=list(range(NCORES)))
    out = np.stack([np.asarray(res.results[c]["out"], dtype=np.float32) for c in range(NCORES)], axis=0)
    return out


def phase_rwkv_chunk(k, fm_d, tm_d, v_d, wc_d, ustr_d, uincl_d, lstr_d, ident_d, y_d):
    k.begin_phase()
    ident = k.sb([64, 64], F32, "ident"); k.dma(ident[:, :], ident_d[0:64, 0:64])
    ustr = k.sb([64, 64], F32, "ustr"); k.dma(ustr[:, :], ustr_d[0:64, 0:64])
    uinc = k.sb([64, 64], F32, "uinc"); k.dma(uinc[:, :], uincl_d[0:64, 0:64])
    lstr = k.sb([64, 64], F32, "lstr"); k.dma(lstr[:, :], lstr_d[0:64, 0:64])
    wc = k.sb([64, 16, 64], F32, "wc")
    k.dma(wc[:, :, :], V(wc_d.h.rearrange("(h j) n -> j h n", j=64), wc_d.buf))
    fm = [[k.sb([64, 16, 128], F32, "fm%d" % m) for m in range(4)] for _ in range(2)]
    tmb = [[k.sb([64, 2, 1024], F32, "tm%d" % m) for m in range(3)] for _ in range(2)]
    NB = 2
    TT = [k.sb([64, 16, 64], F32, "TT") for _ in range(NB)]
    ArT = [k.sb([64, 16, 64], F32, "ArT") for _ in range(NB)]
    BV = [k.sb([64, 16, 64], F32, "BV") for _ in range(NB)]
    BrV = [k.sb([64, 16, 64], F32, "BrV") for _ in range(NB)]
    KtV = [k.sb([64, 16, 64], F32, "KtV") for _ in range(NB)]
    BTs = k.sb([64, 8, 64], F32, "BTs"); BrTs = k.sb([64, 8, 64], F32, "BrTs")
    Xs = [k.sb([64, 8, 64], F32, "Xs") for _ in range(2)]
    XTs = [k.sb([64, 8, 64], F32, "XTs") for _ in range(2)]
    G = k.sb([64, 16, 64], F32, "G")
    k.memset(G[:, :, :], 0.0)
    Zs = [k.sb([64, 8, 64], F32, "Zs") for _ in range(2)]
    Ps = [k.sb([64, 8, 64], F32, "Ps") for _ in range(2)]
    Yt = [k.sb([64, 2, 1024], F32, "Yt") for _ in range(2)]
    g = [k.ps([64, 8, 64], F32, "g%d" % i) for i in range(5)]
    s0 = k.ps([64, 8, 64], F32, "s0"); s1 = k.ps([64, 8, 64], F32, "s1"); s2 = k.ps([64, 8, 64], F32, "s2")
    nt = S // 128

    def load(n):
        i = n % 2
        sl = slice(n * 128, (n + 1) * 128)
        for m in range(4):
            k.dma(fm[i][m][:, :, :], V(fm_d.h[m].rearrange("(h j) t -> j h t", j=64)[:, :, sl], fm_d.buf))
        k.dma(tmb[i][0][:, :, :], V(tm_d.h[0, sl, :].rearrange("(c t) f -> t c f", t=64), tm_d.buf))
        k.dma(tmb[i][1][:, :, :], V(tm_d.h[1, sl, :].rearrange("(c t) f -> t c f", t=64), tm_d.buf))
        k.dma(tmb[i][2][:, :, :], V(v_d.h[sl, :].rearrange("(c t) f -> t c f", t=64), v_d.buf))

    def mb(m_):
        return V(m_.h[:, :].unsqueeze(1).broadcast_to([64, 8, 64]), m_.buf)

    def gram(cidx):
        n, ch = cidx // 2, cidx % 2
        i = n % 2
        j3 = cidx % NB
        cs = slice(ch * 64, ch * 64 + 64)
        kaT, rtT, beT, ktT = fm[i]
        be, kt, vv = tmb[i]
        for hg in range(2):
            for q in range(8):
                h = hg * 8 + q
                k.mm(g[3][:, q, :], kt[:, ch, h * 64:(h + 1) * 64], vv[:, ch, h * 64:(h + 1) * 64])
            k.copy(KtV[j3][:, hg * 8:hg * 8 + 8, :], g[3][:, :, :], e="act")
        for hg in range(2):
            for q in range(8):
                h = hg * 8 + q
                k.mm(g[0][:, q, :], beT[:, h, cs], kaT[:, h, cs])
                k.mm(g[1][:, q, :], beT[:, h, cs], rtT[:, h, cs])
                k.mm(g[2][:, q, :], ktT[:, h, cs], kaT[:, h, cs])
                k.mm(g[3][:, q, :], ktT[:, h, cs], rtT[:, h, cs])
                k.mm(g[4][:, q, :], kaT[:, h, cs], beT[:, h, cs])
            yield
            hs = slice(hg * 8, hg * 8 + 8)
            X, XT = Xs[0], XTs[0]
            k.tt(XT[:, :, :], g[0][:, :, :], mb(ustr), ALU.mult)
            k.tt(ArT[j3][:, hs, :], g[1][:, :, :], mb(uinc), ALU.mult)
            k.tt(BTs[:, :, :], g[2][:, :, :], mb(ustr), ALU.mult)
            k.tt(BrTs[:, :, :], g[3][:, :, :], mb(uinc), ALU.mult)
            k.tt(X[:, :, :], g[4][:, :, :], mb(lstr), ALU.mult)
            Q = V(TT[j3].h[:, hs, :], TT[j3].buf)
            k.tt(Q, mb(ident), XT[:, :, :], ALU.subtract)
            yield
            for q in range(8):
                h = hg * 8 + q
                hc = slice(h * 64, (h + 1) * 64)
                k.mm(g[1][:, q, :], BTs[:, q, :], vv[:, ch, hc])
                k.mm(g[2][:, q, :], BrTs[:, q, :], vv[:, ch, hc])
                if "ckG3" in DBG:
                    k.mm(g[3][:, q, :], BrTs[:, q, :], vv[:, ch, hc])
                elif "ckG4" in DBG:
                    k.mm(g[3][:, q, :], kt[:, ch, hc], BrTs[:, q, :])
                elif "ckG5" in DBG:
                    k.mm(g[3][:, q, :], vv[:, 0, q * 64:(q + 1) * 64], BrTs[:, q, :])
                elif "ckG7" in DBG:
                    k.mm(g[3][:, q, :], BrTs[:, q, :], ktc[cidx % 2][:, h, :])
                elif "ckG6" in DBG:
                    k.mm(g[3][:, q, :], be[:, ch, hc], BrTs[:, q, :])
                else:
                    pass
            k.copy(BV[j3][:, hs, :], g[1][:, :, :], e="act")
            k.copy(BrV[j3][:, hs, :], g[2][:, :, :], e="act")
            yield
            cur = 0
            for lvl in range(1, 6):
                Xn, XTn = Xs[1 - cur], XTs[1 - cur]
                for q in range(8):
                    k.mm(g[0][:, q, :], XTs[cur][:, q, :], Xs[cur][:, q, :])
                if lvl < 5:
                    for q in range(8):
                        k.mm(g[4][:, q, :], Xs[cur][:, q, :], XTs[cur][:, q, :])
                yield
                k.copy(Xn[:, :, :], g[0][:, :, :], e="act")
                if lvl < 5:
                    k.copy(XTn[:, :, :], g[4][:, :, :], e="act")
                for q in range(8):
                    k.mm(g[1][:, q, :], Xn[:, q, :], V(TT[j3].h[:, hg * 8 + q, :], TT[j3].buf))
                yield
                k.tt(Q, Q, g[1][:, :, :], ALU.add)
                cur = 1 - cur

    def seq(cidx):
        n, ch = cidx // 2, cidx % 2
        i = n % 2
        j3 = cidx % NB
        cs = slice(ch * 64, ch * 64 + 64)
        kaT, rtT, beT, ktT = fm[i]
        be, kt, vv = tmb[i]
        y = Yt[i]
        for half in range(2):
            hs = slice(half * 8, half * 8 + 8)
            for q in range(8):
                h = half * 8 + q
                k.mm(s0[:, q, :], kaT[:, h, cs], G[:, h, :])
            yield
            k.stt(Zs[half][:, :, :], s0[:, :, :], -1.0, V(BV[j3].h[:, hs, :], BV[j3].buf), ALU.mult, ALU.subtract)
            for q in range(8):
                h = half * 8 + q
                k.mm(s0[:, q, :], V(TT[j3].h[:, h, :], TT[j3].buf), Zs[half][:, q, :])
            yield
            k.copy(Ps[half][:, :, :], s0[:, :, :], e="act")
            for q in range(8):
                h = half * 8 + q
                k.mm(s1[:, q, :], rtT[:, h, cs], G[:, h, :], start=True, stop=False)
                k.mm(s1[:, q, :], V(ArT[j3].h[:, h, :], ArT[j3].buf), Ps[half][:, q, :], start=False, stop=True)
            k.tt(V(y.h[:, ch, half * 512:(half + 1) * 512].rearrange("p (a b) -> p a b", b=64), y.buf),
                 s1[:, :, :], V(BrV[j3].h[:, hs, :], BrV[j3].buf), ALU.add)
            for q in range(8):
                h = half * 8 + q
                k.mm(s2[:, q, :], be[:, ch, h * 64:(h + 1) * 64], Ps[half][:, q, :])
            yield
            Gh = V(G.h[:, hs, :], G.buf)
            k.tt(Gh, Gh, s2[:, :, :], ALU.add)
            k.tt(Gh, Gh, V(KtV[j3].h[:, hs, :], KtV[j3].buf), ALU.add)
            k.tt(Gh, Gh, V(wc.h[:, hs, cidx:cidx + 1].broadcast_to([64, 8, 64]), wc.buf), ALU.mult)
        if ch == 1:
            k.dma(V(y_d.h[n * 128:(n + 1) * 128, :].rearrange("(c t) f -> t c f", t=64), y_d.buf), y[:, :, :])

    def interleave(ga, gb):
        live_a, live_b = ga is not None, gb is not None
        while live_a or live_b:
            for _ in range(3):
                if live_a:
                    try:
                        next(ga)
                    except StopIteration:
                        live_a = False
            if live_b:
                try:
                    next(gb)
                except StopIteration:
                    live_b = False

    nchunks = 2 * nt
    load(0)
    interleave(gram(0), None)
    for cidx in range(nchunks):
        ga = None
        if cidx + 1 < nchunks:
            if (cidx + 1) % 2 == 0:
                load((cidx + 1) // 2)
            ga = gram(cidx + 1)
        interleave(ga, seq(cidx))
    k.end_phase()
```

```python
import numpy as np
import ml_dtypes
from contextlib import ExitStack
import concourse.bass as bass
import concourse.mybir as mybir
from concourse.bass_utils import run_bass_kernel_spmd

F32 = mybir.dt.float32
BF16 = mybir.dt.bfloat16
AF = mybir.ActivationFunctionType
ALU = mybir.AluOpType
AX = mybir.AxisListType

D = 1024
S = 4096
NCORES = 8
DFF = 4096
LN_EPS = 1e-5
DN_ALPHA = (2.0 * 4) ** 0.25
import os
DBG = os.environ.get("DBG", "")


class Buf:
    __slots__ = ("w", "r", "dsem", "multi", "name", "psum")

    def __init__(self, name="", multi=False):
        self.psum = False
        self.w = {}
        self.r = {}
        self.dsem = {}
        self.multi = multi
        self.name = name


class V:
    __slots__ = ("ap", "buf")

    def __init__(self, ap, buf):
        self.ap = ap
        self.buf = buf


class T:
    def __init__(self, h, buf):
        self.h = h
        self.buf = buf

    def __getitem__(self, idx):
        return V(self.h[idx], self.buf)

    def v(self, ap):
        return V(ap, self.buf)


class K:
    def __init__(self, nc, es):
        self.nc = nc
        self.es = es
        self.eng = {"pe": nc.tensor, "act": nc.scalar, "dve": nc.vector, "pool": nc.gpsimd, "sp": nc.sync}
        self.esem = {}
        self.ecnt = {}
        for e in ("pe", "act", "dve", "pool"):
            self.esem[e] = es.enter_context(nc.semaphore("sem_" + e))
            self.ecnt[e] = 0
        self.seen = {e: {} for e in self.eng}
        self.dpool = []
        self.dfree = {"hw": [], "sw": []}
        self.phase_es = None
        self.phase_bufs = []
        self.ndma = 0
        self.uid = 0

    def begin_phase(self):
        self.phase_es = ExitStack()
        self.phase_bufs = []

    def end_phase(self):
        self.barrier()
        for b in self.phase_bufs:
            for kind, di in b.dsem.items():
                self.dfree[kind].append(di)
            b.dsem = {}
        self.phase_es.close()
        self.phase_es = None

    def sb(self, shape, dtype, name=None):
        self.uid += 1
        name = (name or "sb") + "_%d" % self.uid
        h = self.phase_es.enter_context(self.nc.sbuf_tensor(name, list(shape), dtype))
        b = Buf(name)
        self.phase_bufs.append(b)
        return T(h, b)

    def ps(self, shape, dtype=F32, name=None):
        self.uid += 1
        name = (name or "ps") + "_%d" % self.uid
        isz = 4 if dtype == F32 else 2
        n = 1
        for d in shape[1:]:
            n *= d
        per_bank = 2048 // isz
        nflat = ((n + per_bank - 1) // per_bank) * per_bank
        h = self.phase_es.enter_context(self.nc.psum_tensor(name, [shape[0], nflat], dtype))
        ap = h[:, 0:n]
        if len(shape) == 3:
            ap = ap.rearrange("p (a b) -> p a b", b=shape[2])
        elif len(shape) == 4:
            ap = ap.rearrange("p (a b c) -> p a b c", b=shape[2], c=shape[3])
        b = Buf(name)
        b.psum = True
        self.phase_bufs.append(b)
        return T(ap, b)

    def dram(self, name, shape, dtype, kind="Internal"):
        h = self.nc.dram_tensor(name, list(shape), dtype, kind=kind)
        return T(h.ap(), Buf(name, multi=True))

    def _dsem(self, buf, kind):
        if kind not in buf.dsem:
            if self.dfree[kind]:
                buf.dsem[kind] = self.dfree[kind].pop()
            else:
                sem = self.es.enter_context(self.nc.semaphore("dsem_%d" % len(self.dpool)))
                self.dpool.append([sem, 0])
                buf.dsem[kind] = len(self.dpool) - 1
        return buf.dsem[kind]

    def _wait(self, e, deps):
        seen = self.seen[e]
        for key, val in deps.items():
            if key[0] == "d":
                sem, val = self.dpool[key[1]]
            else:
                if e == "pe" and key[1] == "pe":
                    continue
                sem = self.esem[key[1]]
            if seen.get(key, 0) < val:
                self.eng[e].wait_ge(sem, val)
                seen[key] = val

    def _collect(self, reads, writes):
        deps = {}
        for b in reads:
            for kx, vx in b.w.items():
                if deps.get(kx, 0) < vx:
                    deps[kx] = vx
            if b.psum:
                for kx, vx in b.r.items():
                    if deps.get(kx, 0) < vx:
                        deps[kx] = vx
        for b in writes:
            for kx, vx in b.w.items():
                if deps.get(kx, 0) < vx:
                    deps[kx] = vx
            for kx, vx in b.r.items():
                if deps.get(kx, 0) < vx:
                    deps[kx] = vx
        return deps

    def _record(self, key, val, reads, writes):
        for b in reads:
            b.r[key] = val
        for b in writes:
            if b.multi:
                b.w[key] = val
            else:
                b.w = {key: val}
                b.r = {}

    def op(self, e, fn, reads, writes):
        reads = [v.buf for v in reads if v is not None]
        writes = [v.buf for v in writes if v is not None]
        self._wait(e, self._collect(reads, writes))
        ins = fn(self.eng[e])
        self.ecnt[e] += 1
        ins.then_inc(self.esem[e], 1)
        self._record(("e", e), self.ecnt[e], reads, writes)
        return ins

    def dma(self, out, in_, q="sp", sbuf=None):
        if sbuf is None:
            sbuf = out if not out.buf.multi else in_
        reads = [in_.buf]
        writes = [out.buf]
        self._wait(q, self._collect(reads, writes))
        di = self._dsem(sbuf.buf, "sw" if q == "pool" else "hw")
        ins = self.eng[q].dma_start(out=out.ap, in_=in_.ap)
        self.dpool[di][1] += 16
        ins.then_inc(self.dpool[di][0], 16)
        self._record(("d", di), self.dpool[di][1], reads, writes)
        self.ndma += 1
        return ins

    def barrier(self):
        for e in self.eng:
            deps = {}
            for f in self.esem:
                deps[("e", f)] = self.ecnt[f]
            for i in range(len(self.dpool)):
                deps[("d", i)] = self.dpool[i][1]
            seen = self.seen[e]
            for key, val in deps.items():
                if key[0] == "d":
                    sem = self.dpool[key[1]][0]
                else:
                    sem = self.esem[key[1]]
                if val > 0 and seen.get(key, 0) < val:
                    self.eng[e].wait_ge(sem, val)
                    seen[key] = val

    def mm(self, out, lhsT, rhs, start=True, stop=True):
        return self.op("pe", lambda e: e.matmul(out.ap, lhsT.ap, rhs.ap, start=start, stop=stop),
                       [lhsT, rhs], [out])

    def tr(self, out, in_, ident):
        return self.op("pe", lambda e: e.transpose(out.ap, in_.ap, ident.ap), [in_, ident], [out])

    def act(self, out, in_, func, bias=None, scale=1.0, accum=None, e="act"):
        rd = [in_]
        kw = {}
        if isinstance(bias, V):
            rd.append(bias)
            kw["bias"] = bias.ap
        elif bias is not None:
            kw["bias"] = bias
        if isinstance(scale, V):
            rd.append(scale)
            kw["scale"] = scale.ap
        else:
            kw["scale"] = scale
        wr = [out]
        if accum is not None:
            wr.append(accum)
            kw["accum_out"] = accum.ap
        return self.op(e, lambda en: en.activation(out.ap, in_.ap, func, **kw), rd, wr)

    def tt(self, out, a, b, op, e="dve"):
        return self.op(e, lambda en: en.tensor_tensor(out.ap, a.ap, b.ap, op), [a, b], [out])

    def ts(self, out, a, s1, s2=None, op0=ALU.mult, op1=None, e="dve", accum=None):
        rd = [a]
        s1a = s1.ap if isinstance(s1, V) else s1
        s2a = s2.ap if isinstance(s2, V) else s2
        if isinstance(s1, V):
            rd.append(s1)
        if isinstance(s2, V):
            rd.append(s2)
        kw = {}
        if op1 is not None:
            kw["op1"] = op1
        wr = [out]
        if accum is not None:
            kw["accum_out"] = accum.ap
            wr.append(accum)
        return self.op(e, lambda en: en.tensor_scalar(out.ap, a.ap, s1a, s2a, op0, **kw), rd, wr)

    def stt(self, out, in0, scalar, in1, op0, op1, e="dve"):
        rd = [in0, in1]
        sa = scalar.ap if isinstance(scalar, V) else scalar
        if isinstance(scalar, V):
            rd.append(scalar)
        return self.op(e, lambda en: en.scalar_tensor_tensor(out.ap, in0.ap, sa, in1.ap, op0, op1), rd, [out])

    def copy(self, out, in_, e="dve"):
        if e == "act":
            return self.op(e, lambda en: en.copy(out.ap, in_.ap), [in_], [out])
        return self.op(e, lambda en: en.tensor_copy(out.ap, in_.ap), [in_], [out])

    def memset(self, out, val, e="dve"):
        return self.op(e, lambda en: en.memset(out.ap, val), [], [out])

    def reduce(self, out, in_, op, axis=AX.X, e="dve"):
        return self.op(e, lambda en: en.tensor_reduce(out.ap, in_.ap, axis, op), [in_], [out])


def load_bcast(k, dst, src_ap_row, n, q="sp"):
    pass


def ln_tile(k, y, g_bc, b_bc, out32, tmp_stats, tmp_mv, tmp_rs):
    for c in range(2):
        k.op("dve", lambda en, c=c: en.bn_stats(tmp_stats.h[:, c, :], y.h[:, c * 512:(c + 1) * 512]),
             [y[:, :]], [tmp_stats[:, :, :]])
    k.op("dve", lambda en: en.bn_aggr(tmp_mv.h[:, :], tmp_stats.h[:, :, :]), [tmp_stats[:, :, :]], [tmp_mv[:, :]])
    k.act(tmp_rs[:, 0:1], tmp_mv[:, 1:2], AF.Ln, bias=LN_EPS)
    k.act(tmp_rs[:, 0:1], tmp_rs[:, 0:1], AF.Exp, scale=-0.5)
    k.stt(tmp_mv[:, 1:2], tmp_mv[:, 0:1], -1.0, tmp_rs[:, 0:1], ALU.mult, ALU.mult)
    k.act(out32[:, :], y[:, :], AF.Identity, bias=tmp_mv[:, 1:2], scale=tmp_rs[:, 0:1])
    k.tt(out32[:, :], out32[:, :], g_bc[:, :], ALU.mult)
    k.tt(out32[:, :], out32[:, :], b_bc[:, :], ALU.add)


class LNStage:
    def __init__(self, k, ident32, g_row, b_row, x_in, x_out, xT_out):
        self.k = k
        self.ident = ident32
        self.x_in, self.x_out, self.xT_out = x_in, x_out, xT_out
        self.g = k.sb([128, D], F32, "ln_g")
        self.b = k.sb([128, D], F32, "ln_b")
        k.dma(self.g[:, :], V(g_row.ap.partition_broadcast(128), g_row.buf))
        k.dma(self.b[:, :], V(b_row.ap.partition_broadcast(128), b_row.buf))
        self.xin = [k.sb([128, D], F32, "ln_xin") for _ in range(2)]
        self.st = [k.sb([128, 2, 6], F32, "ln_st") for _ in range(2)]
        self.mv = [k.sb([128, 2], F32, "ln_mv") for _ in range(2)]
        self.rs = [k.sb([128, 1], F32, "ln_rs") for _ in range(2)]
        self.xT = [k.sb([128, 8, 128], BF16, "ln_xT") for _ in range(2)]
        self.pt = [k.ps([128, 4, 128], F32, "ln_pt") for _ in range(2)]

    def prefetch(self, ti):
        k = self.k
        i = ti % 2
        k.dma(self.xin[i][:, :], self.x_in[ti * 128:(ti + 1) * 128, :])

    def run(self, ti, po):
        k = self.k
        self.flush()
        i = ti % 2
        y = self.xin[i]
        for hh in range(2):
            k.stt(y[:, hh * 512:(hh + 1) * 512], y[:, hh * 512:(hh + 1) * 512], DN_ALPHA, po[hh],
                  ALU.mult, ALU.add)
        ln_tile(k, y, self.g, self.b, y, self.st[i], self.mv[i], self.rs[i])
        if self.x_out is not None:
            k.dma(self.x_out[ti * 128:(ti + 1) * 128, :], y[:, :])
        if self.xT_out is None:
            return
        self.pending = ti

    def flush(self):
        ti = getattr(self, "pending", None)
        if ti is None:
            return
        self.pending = None
        k = self.k
        i = ti % 2
        y = self.xin[i]
        for half in range(2):
            pt = self.pt[half]
            for c in range(4):
                cc = half * 4 + c
                k.tr(pt[:, c, :], y[:, cc * 128:(cc + 1) * 128], self.ident[:, :])
            k.copy(self.xT[i][:, half * 4:(half + 1) * 4, :], pt[:, :, :], e="act")
        k.dma(V(self.xT_out.h.rearrange("(c p) t -> p c t", p=128)[:, :, ti * 128:(ti + 1) * 128], self.xT_out.buf),
              self.xT[i][:, :, :])


def load_w_bf16(k, dst, src_ap, src_buf):
    k.dma(dst, V(src_ap, src_buf), q="pool")


def phase_prep(k, x_in, xT_out, ident_d):
    k.begin_phase()
    ident = k.sb([128, 128], F32, "ident")
    k.dma(ident[:, :], ident_d[:, :])
    xin = [k.sb([128, D], F32, "xin") for _ in range(2)]
    xT = [k.sb([128, 8, 128], BF16, "xT") for _ in range(2)]
    pt = [k.ps([128, 4, 128], F32, "pt") for _ in range(2)]
    nt = S // 128
    k.dma(xin[0][:, :], x_in[0:128, :])
    for ti in range(nt):
        i = ti % 2
        if ti + 1 < nt:
            k.dma(xin[1 - i][:, :], x_in[(ti + 1) * 128:(ti + 2) * 128, :])
        for half in range(2):
            for c in range(4):
                cc = half * 4 + c
                k.tr(pt[half][:, c, :], xin[i][:, cc * 128:(cc + 1) * 128], ident[:, :])
            k.copy(xT[i][:, half * 4:(half + 1) * 4, :], pt[half][:, :, :], e=("act" if half else "dve"))
        k.dma(V(xT_out.h.rearrange("(c p) t -> p c t", p=128)[:, :, ti * 128:(ti + 1) * 128], xT_out.buf),
              xT[i][:, :, :])
    k.end_phase()


def phase_mlp(k, w1_d, w2_d, g_row, b_row, x_in, xT_in, x_out, xT_out, ident_d):
    k.begin_phase()
    ident = k.sb([128, 128], F32, "ident")
    k.dma(ident[:, :], ident_d[:, :])
    w1 = k.sb([128, 8, DFF], BF16, "w1")
    w2 = k.sb([128, 32, D], BF16, "w2")
    w1v = w1_d.h.rearrange("(c p) f -> p c f", p=128)
    w2v = w2_d.h.rearrange("(c p) d -> p c d", p=128)
    for c in range(8):
        load_w_bf16(k, w1[:, c, :], w1v[:, c, :], w1_d.buf)
    for c in range(0, 32, 4):
        load_w_bf16(k, w2[:, c:c + 4, :], w2v[:, c:c + 4, :], w2_d.buf)
    ln = LNStage(k, ident, g_row, b_row, x_in, x_out, xT_out)
    xb = [k.sb([128, 8, 512], BF16, "xb") for _ in range(2)]
    hT = [k.sb([128, 32, 512], BF16, "hT") for _ in range(1)]
    r32 = [k.sb([128, 512], F32, "r32") for _ in range(2)]
    ph = [k.ps([128, 512], F32, "ph") for _ in range(2)]
    po = [[k.ps([128, 512], F32, "po") for _ in range(2)] for _ in range(2)]
    xTv = xT_in.h.rearrange("(c p) t -> p c t", p=128)
    nb = S // 512
    k.dma(xb[0][:, :, :], V(xTv[:, :, 0:512], xT_in.buf))
    n_ev = 0
    for bi in range(nb):
        i = bi % 2
        if bi + 1 < nb:
            k.dma(xb[1 - i][:, :, :], V(xTv[:, :, (bi + 1) * 512:(bi + 2) * 512], xT_in.buf))
        h = hT[0]
        for fc in range(32):
            p = ph[fc % 2]
            for dc in range(8):
                k.mm(p[:, :], w1[:, dc, fc * 128:(fc + 1) * 128], xb[i][:, dc, :], start=(dc == 0), stop=(dc == 7))
            r = r32[fc % 2]
            k.act(r[:, :], p[:, :], AF.Relu)
            k.tt(h[:, fc, :], r[:, :], r[:, :], ALU.mult, e="dve")
        for tt in range(4):
            ti = bi * 4 + tt
            if "noln" not in DBG:
                ln.prefetch(ti)
            pp = po[tt % 2]
            for hh in range(2):
                for fc in range(32):
                    k.mm(pp[hh][:, :], h[:, fc, tt * 128:(tt + 1) * 128], w2[:, fc, hh * 512:(hh + 1) * 512],
                         start=(fc == 0), stop=(fc == 31))
            if "noln" in DBG:
                for hh in range(2):
                    k.copy(ln.xin[ti % 2][:, hh * 512:(hh + 1) * 512], pp[hh][:, :])
                k.dma(x_out[ti * 128:(ti + 1) * 128, :], ln.xin[ti % 2][:, :])
            else:
                ln.run(ti, [pp[0][:, :], pp[1][:, :]])
    ln.flush()
    k.end_phase()


def phase_outproj(k, ogT_d, w_out_d, nfc, g_row, b_row, x_in, x_out, xT_out, ident_d):
    k.begin_phase()
    ident = k.sb([128, 128], F32, "ident")
    k.dma(ident[:, :], ident_d[:, :])
    w = k.sb([128, nfc, D], BF16, "wout")
    wv = w_out_d.h.rearrange("(c p) d -> p c d", p=128)
    for c in range(0, nfc, 4):
        load_w_bf16(k, w[:, c:c + 4, :], wv[:, c:c + 4, :], w_out_d.buf)
    ln = LNStage(k, ident, g_row, b_row, x_in, x_out, xT_out)
    ob = [k.sb([128, nfc, 128], BF16, "ob") for _ in range(2)]
    po = [[k.ps([128, 512], F32, "po") for _ in range(2)] for _ in range(2)]
    ov = ogT_d.h.rearrange("(c p) t -> p c t", p=128)
    nt = S // 128
    k.dma(ob[0][:, :, :], V(ov[:, :, 0:128], ogT_d.buf))
    for ti in range(nt):
        i = ti % 2
        if ti + 1 < nt:
            k.dma(ob[1 - i][:, :, :], V(ov[:, :, (ti + 1) * 128:(ti + 2) * 128], ogT_d.buf))
        ln.prefetch(ti)
        pp = po[i]
        for hh in range(2):
            for fc in range(nfc):
                k.mm(pp[hh][:, :], ob[i][:, fc, :], w[:, fc, hh * 512:(hh + 1) * 512],
                     start=(fc == 0), stop=(fc == nfc - 1))
        ln.run(ti, [pp[0][:, :], pp[1][:, :]])
    ln.flush()
    k.end_phase()


RET_H = 4


def phase_ret_proj(k, xT_in, w_in_d, cos_d, sin_d, dec_d, qkT_d, v_d, g_d):
    k.begin_phase()
    w = k.sb([128, 8, 6144], BF16, "w_in")
    wv = w_in_d.h.rearrange("(c p) f -> p c f", p=128)
    for c in range(8):
        for half in range(2):
            load_w_bf16(k, w[:, c, half * 3072:(half + 1) * 3072], wv[:, c, half * 3072:(half + 1) * 3072], w_in_d.buf)
    xb = [k.sb([128, 8, 512], BF16, "xb") for _ in range(2)]
    cs = [k.sb([128, 512], F32, "cos") for _ in range(2)]
    sn = [k.sb([128, 512], F32, "sin") for _ in range(2)]
    dec = [k.sb([128, 8, 512], F32, "dec") for _ in range(2)]
    t1 = k.sb([128, 512], F32, "t1")
    t2 = k.sb([128, 512], F32, "t2")
    r0 = [k.sb([128, 2, 512], BF16, "r0") for _ in range(2)]
    vo = [k.sb([128, 512], BF16, "vo") for _ in range(2)]
    go = [k.sb([128, 512], F32, "go") for _ in range(2)]
    pq = [k.ps([128, 512], F32, "pq") for _ in range(4)]
    pv = [k.ps([128, 512], F32, "pv") for _ in range(2)]
    xTv = xT_in.h.rearrange("(c p) t -> p c t", p=128)
    nb = S // 512

    def load_blk(bi):
        i = bi % 2
        sl = slice(bi * 512, (bi + 1) * 512)
        k.dma(xb[i][:, :, :], V(xTv[:, :, sl], xT_in.buf))
        k.dma(cs[i][:, :], cos_d[:, sl])
        k.dma(sn[i][:, :], sin_d[:, sl])
        for j in range(8):
            k.dma(dec[i][:, j, :], V(dec_d.h[j, sl].partition_broadcast(128), dec_d.buf))

    load_blk(0)
    n = 0
    for bi in range(nb):
        i = bi % 2
        sl = slice(bi * 512, (bi + 1) * 512)
        if bi + 1 < nb:
            load_blk(bi + 1)
        for j in range(8):
            p0, p1 = pq[(n % 2) * 2], pq[(n % 2) * 2 + 1]
            ro = r0[n % 2]
            n += 1
            f0 = j * 256
            for dc in range(8):
                k.mm(p0[:, :], w[:, dc, f0:f0 + 128], xb[i][:, dc, :], start=(dc == 0), stop=(dc == 7))
            for dc in range(8):
                k.mm(p1[:, :], w[:, dc, f0 + 128:f0 + 256], xb[i][:, dc, :], start=(dc == 0), stop=(dc == 7))
            k.tt(t1[:, :], p0[:, :], cs[i][:, :], ALU.mult)
            k.tt(t2[:, :], p1[:, :], sn[i][:, :], ALU.mult)
            k.tt(t1[:, :], t1[:, :], t2[:, :], ALU.subtract)
            k.tt(ro[:, 0, :], t1[:, :], dec[i][:, j, :], ALU.mult)
            k.tt(t1[:, :], p1[:, :], cs[i][:, :], ALU.mult)
            k.tt(t2[:, :], p0[:, :], sn[i][:, :], ALU.mult)
            k.tt(t1[:, :], t1[:, :], t2[:, :], ALU.add)
            k.tt(ro[:, 1, :], t1[:, :], dec[i][:, j, :], ALU.mult)
            k.dma(V(qkT_d.h[2 * j:2 * j + 2, :, sl].rearrange("c p t -> p c t"), qkT_d.buf), ro[:, :, :])
        m = 0
        for tt in range(4):
            tsl = slice(tt * 128, (tt + 1) * 128)
            rows = slice(bi * 512 + tt * 128, bi * 512 + (tt + 1) * 128)
            for fb in range(8):
                f0 = 2048 + fb * 512
                p = pv[m % 2]
                for dc in range(8):
                    k.mm(p[:, :], xb[i][:, dc, tsl], w[:, dc, f0:f0 + 512], start=(dc == 0), stop=(dc == 7))
                if fb < 4:
                    o = vo[m % 2]
                    k.copy(o[:, :], p[:, :], e="act")
                    k.dma(v_d[rows, fb * 512:(fb + 1) * 512], o[:, :])
                else:
                    o = go[m % 2]
                    k.act(o[:, :], p[:, :], AF.Silu)
                    k.dma(g_d[rows, (fb - 4) * 512:(fb - 3) * 512], o[:, :])
                m += 1
    k.end_phase()


def phase_ret_attn(k, qkT_d, v_d, g_d, gn_d, mask_d, ident_d, ogT_d):
    k.begin_phase()
    identb = k.sb([128, 128], BF16, "identb")
    k.dma(identb[:, :], ident_d[:, :], q="pool")
    mask = k.sb([128, 128], F32, "mask")
    k.dma(mask[:, :], mask_d[:, :])
    gng = k.sb([128, 2048], F32, "gng")
    gnb = k.sb([128, 2048], F32, "gnb")
    k.dma(gng[:, :], V(gn_d.h[0, :].partition_broadcast(128), gn_d.buf))
    k.dma(gnb[:, :], V(gn_d.h[1, :].partition_broadcast(128), gn_d.buf))
    qT = [k.sb([128, 2, S], BF16, "qT") for _ in range(1)]
    kT = [k.sb([128, 2, S], BF16, "kT") for _ in range(1)]
    vv = [k.sb([128, 32, 512], BF16, "vv") for _ in range(1)]
    pb = [k.sb([128, 128], BF16, "pb") for _ in range(3)]
    gt = [k.sb([128, 512], F32, "gt") for _ in range(2)]
    on = [k.sb([128, 512], F32, "on") for _ in range(2)]
    ob = [k.sb([128, 512], BF16, "ob") for _ in range(2)]
    oT = [k.sb([128, 4, 128], BF16, "oT") for _ in range(2)]
    st = [k.sb([128, 1, 6], F32, "st") for _ in range(2)]
    mv = [k.sb([128, 2], F32, "mv") for _ in range(2)]
    rs = [k.sb([128, 1], F32, "rs") for _ in range(2)]
    psT = [k.ps([128, 128], F32, "psT") for _ in range(3)]
    po = [k.ps([128, 512], F32, "po") for _ in range(2)]
    pt = [k.ps([128, 4, 128], BF16, "pt") for _ in range(2)]
    np_ = 0
    for h in range(RET_H):
        gamma = 1.0 - 2.0 ** (-5.0 - h)
        k.dma(qT[0][:, :, :], V(qkT_d.h[2 * h:2 * h + 2, :, :].rearrange("c p t -> p c t"), qkT_d.buf))
        k.dma(kT[0][:, :, :], V(qkT_d.h[8 + 2 * h:8 + 2 * h + 2, :, :].rearrange("c p t -> p c t"), qkT_d.buf))
        k.dma(vv[0][:, :, :], V(v_d.h[:, h * 512:(h + 1) * 512].rearrange("(n p) f -> p n f", p=128), v_d.buf))
        items = [(qt, kt) for qt in range(32) for kt in range(qt + 1)]
        slots = {}
        LA = 2

        def emit_qk(it, h=h, gamma=gamma, slots=slots):
            nonlocal np_
            qt, kt = it
            i = qt % 2
            qs = slice(qt * 128, (qt + 1) * 128)
            ks = slice(kt * 128, (kt + 1) * 128)
            if kt == 0:
                k.dma(gt[i][:, :], g_d[qs, h * 512:(h + 1) * 512])
            ps = psT[np_ % 3]
            p = pb[np_ % 3]
            np_ += 1
            slots[it] = p
            for dc in range(2):
                k.mm(ps[:, :], kT[0][:, dc, ks], qT[0][:, dc, qs], start=(dc == 0), stop=(dc == 1))
            if kt == qt:
                k.tt(p[:, :], ps[:, :], mask[:, :], ALU.mult)
            else:
                k.act(p[:, :], ps[:, :], AF.Copy, scale=float(gamma ** (128 * (qt - kt))))

        def emit_pv(it, h=h, slots=slots):
            qt, kt = it
            i = qt % 2
            qs = slice(qt * 128, (qt + 1) * 128)
            p = slots.pop(it)
            k.mm(po[i][:, :], p[:, :], vv[0][:, kt, :], start=(kt == 0), stop=(kt == qt))
            if kt != qt:
                return
            k.op("dve", lambda en, i=i: en.bn_stats(st[i].h[:, 0, :], po[i].h[:, :]), [po[i][:, :]], [st[i][:, :, :]])
            k.op("dve", lambda en, i=i: en.bn_aggr(mv[i].h[:, :], st[i].h[:, :, :]), [st[i][:, :, :]], [mv[i][:, :]])
            k.act(rs[i][:, 0:1], mv[i][:, 1:2], AF.Ln, bias=1e-5)
            k.act(rs[i][:, 0:1], rs[i][:, 0:1], AF.Exp, scale=-0.5)
            k.stt(mv[i][:, 1:2], mv[i][:, 0:1], -1.0, rs[i][:, 0:1], ALU.mult, ALU.mult)
            k.act(on[i][:, :], po[i][:, :], AF.Identity, bias=mv[i][:, 1:2], scale=rs[i][:, 0:1])
            k.tt(on[i][:, :], on[i][:, :], gng[:, h * 512:(h + 1) * 512], ALU.mult)
            k.tt(on[i][:, :], on[i][:, :], gnb[:, h * 512:(h + 1) * 512], ALU.add)
            k.tt(ob[i][:, :], on[i][:, :], gt[i][:, :], ALU.mult)
            for c in range(4):
                k.tr(pt[i][:, c, :], ob[i][:, c * 128:(c + 1) * 128], identb[:, :])
            k.copy(oT[i][:, :, :], pt[i][:, :, :], e="act")
            k.dma(V(ogT_d.h[h * 512:(h + 1) * 512, qs].rearrange("(c p) t -> p c t", p=128), ogT_d.buf), oT[i][:, :, :])

        for idx, it in enumerate(items):
            emit_qk(it)
            if idx >= LA:
                emit_pv(items[idx - LA])
        for it in items[len(items) - LA:]:
            emit_pv(it)
    k.end_phase()


def host_consts():
    c = {}
    c["ident"] = np.eye(128, dtype=np.float32)
    t = np.arange(S, dtype=np.float64)
    inv = 10000.0 ** (-np.arange(128, dtype=np.float64) / 128)
    ang = inv[:, None] * t[None, :]
    c["ret_cos"] = np.cos(ang).astype(np.float32)
    c["ret_sin"] = np.sin(ang).astype(np.float32)
    dec = np.zeros((8, S), np.float64)
    for h in range(4):
        g = 1.0 - 2.0 ** (-5.0 - h)
        dec[h] = g ** (t % 128)
        dec[4 + h] = 256.0 ** -0.5 * g ** (-(t % 128))
    c["ret_dec"] = dec.astype(np.float32)
    kk = np.arange(128)
    c["mask_le"] = (kk[:, None] <= kk[None, :]).astype(np.float32)
    c["mask_ge"] = (kk[:, None] >= kk[None, :]).astype(np.float32)
    inv = 10000.0 ** (-np.arange(32, dtype=np.float64) / 32)
    ang = t[:, None] * inv[None, :]
    c["mla_cosT"] = np.cos(ang).astype(np.float32)
    c["mla_sinT"] = np.sin(ang).astype(np.float32)
    cT, sT = np.cos(ang).T, np.sin(ang).T
    c["mla_c2"] = (MLA_SCALE * np.concatenate([cT, cT], 0)).astype(np.float32)
    c["mla_s2"] = (MLA_SCALE * np.concatenate([-sT, sT], 0)).astype(np.float32)
    kq = np.arange(128)[:, None]
    qq = np.arange(512)[None, :]
    c["mla_mask4"] = np.stack([((m * 128 + kq) <= qq) for m in range(4)]).astype(np.float32)
    inv = 500000.0 ** (-np.arange(16, dtype=np.float64) / 16)
    ang = inv[:, None] * t[None, :]
    cT, sT = np.cos(ang), np.sin(ang)
    c["dil_c2"] = np.concatenate([cT, cT], 0).astype(np.float32)
    c["dil_s2"] = np.concatenate([-sT, sT], 0).astype(np.float32)
    c["dil_c2q"] = (128.0 ** -0.5 * np.concatenate([cT, cT], 0)).astype(np.float32)
    c["dil_s2q"] = (128.0 ** -0.5 * np.concatenate([-sT, sT], 0)).astype(np.float32)
    ii = np.arange(128)
    same = (ii[:, None] // 64) == (ii[None, :] // 64)
    c["rw_ustr"] = (same & (ii[:, None] < ii[None, :])).astype(np.float32)
    c["rw_uincl"] = (same & (ii[:, None] <= ii[None, :])).astype(np.float32)
    c["rw_lstr"] = (same & (ii[:, None] > ii[None, :])).astype(np.float32)
    c["rw_chsel"] = np.stack([(ii < 64), (ii >= 64)], 1).astype(np.float32)
    return c


LASTK = None


def build_all(inputs, layers=(0, 1, 2, 3), ncores=NCORES):
    global LASTK
    consts = host_consts()
    nc = bass.Bass("TRN2", target_bir_lowering=False)
    shared = {}
    with ExitStack() as es:
        k = K(nc, es)
        LASTK = k

        tcache = {}

        def inp(name, arr, dtype=F32):
            if name not in tcache:
                shared[name] = np.ascontiguousarray(arr)
                tcache[name] = k.dram(name, list(arr.shape), dtype, kind="ExternalInput")
            return tcache[name]

        x = k.dram("x", [S, D], F32, kind="ExternalInput")
        out = k.dram("out", [S, D], F32, kind="ExternalOutput")
        ident = inp("ident", consts["ident"])
        ln_g = inp("ln_g", inputs["ln_g"])
        ln_b = inp("ln_b", inputs["ln_b"])
        mlp_w1 = inp("mlp_w1", inputs["mlp_w1"])
        mlp_w2 = inp("mlp_w2", inputs["mlp_w2"])
        xa = k.dram("xa", [S, D], F32)
        xb_ = k.dram("xb", [S, D], F32)
        xT0 = k.dram("xT0", [D, S], BF16)
        xT1 = k.dram("xT1", [D, S], BF16)
        ogT = k.dram("ogT", [2048, S], BF16)

        def lnrow(t, i, j):
            return V(t.h[i, j, :], t.buf)

        phase_prep(k, x, xT0, ident)
        cur = x
        for li, L in enumerate(layers):
            last = (li == len(layers) - 1)
            if L == 0:
                w_in = inp("ret_w_in", inputs["ret_w_in"][0])
                w_out = inp("ret_w_out", inputs["ret_w_out"][0])
                gn = inp("ret_gn", inputs["ret_gn"][0])
                cos_d = inp("ret_cos", consts["ret_cos"])
                sin_d = inp("ret_sin", consts["ret_sin"])
                dec_d = inp("ret_dec", consts["ret_dec"])
                mask_le = inp("mask_le", consts["mask_le"])
                qkT = k.dram("ret_qkT", [16, 128, S], BF16)
                v_d = k.dram("ret_v", [S, 2048], BF16)
                g_d = k.dram("ret_g", [S, 2048], F32)
                phase_ret_proj(k, xT0, w_in, cos_d, sin_d, dec_d, qkT, v_d, g_d)
                phase_ret_attn(k, qkT, v_d, g_d, gn, mask_le, ident, ogT)
                phase_outproj(k, ogT, w_out, 16, lnrow(ln_g, L, 0), lnrow(ln_b, L, 0), cur, xa, xT1, ident)
            elif L == 1:
                wi = inputs["dil_w_in"][0]
                w_in = inp("dil_w_in", wi)
                swidx = np.concatenate([((g * 3 + j) * 8 + h) * 128 + (np.arange(32) + 16) % 32
                                        for g in range(3) for j in range(2) for h in range(8)])
                w_sw = inp("dil_w_sw", wi[:, swidx])
                w_out = inp("dil_w_out", inputs["dil_w_out"][0])
                dc2 = inp("dil_c2", consts["dil_c2"]); ds2 = inp("dil_s2", consts["dil_s2"])
                mle = inp("mask_le", consts["mask_le"])
                mge = inp("mask_ge", consts["mask_ge"])
                U_d = k.dram("dil_U", [3, S, 1024], F32)
                Den_d = k.dram("dil_Den", [3, S, 8], F32)
                phase_dil(k, xT0, w_in, w_sw, dc2, ds2, mle, mge, U_d, Den_d)
                phase_dil_out(k, U_d, Den_d, w_out, lnrow(ln_g, L, 0), lnrow(ln_b, L, 0), cur, xa, xT1, ident)
            elif L == 2:
                w_down = inp("mla_w_down", inputs["mla_w_down"][0])
                nq_ = inp("mla_norm_q", inputs["mla_norm_q"][0])
                nkv_ = inp("mla_norm_kv", inputs["mla_norm_kv"][0])
                wuq = inputs["mla_w_uq"][0]
                w_uq = inp("mla_w_uq", wuq)
                swidx = np.concatenate([h * 192 + 128 + (np.arange(64) + 32) % 64 for h in range(16)])
                w_uqsw = inp("mla_w_uqsw", wuq[:, swidx])
                w_ukv = inp("mla_w_ukv", inputs["mla_w_ukv"][0])
                w_out = inp("mla_w_out", inputs["mla_w_out"][0])
                cosT = inp("mla_cosT", consts["mla_cosT"])
                sinT = inp("mla_sinT", consts["mla_sinT"])
                c2 = inp("mla_c2", consts["mla_c2"])
                s2 = inp("mla_s2", consts["mla_s2"])
                mask4 = inp("mla_mask4", consts["mla_mask4"])
                cT = k.dram("mla_cT", [4, 128, S], BF16)
                qnT = k.dram("mla_qnT", [16, 128, S], BF16)
                qpT = k.dram("mla_qpT", [16, 64, S], BF16)
                knT = k.dram("mla_knT", [16, 128, S], BF16)
                v_d = k.dram("mla_v", [S, 2048], BF16)
                phase_mla_down(k, xT0, w_down, V(nq_.h, nq_.buf), V(nkv_.h, nkv_.buf), cosT, sinT, ident, cT)
                phase_mla_up(k, cT, w_uq, w_uqsw, w_ukv, c2, s2, qnT, qpT, knT, v_d)
                phase_mla_attn(k, qnT, qpT, knT, cT, v_d, mask4, ogT)
                phase_outproj(k, ogT, w_out, 16, lnrow(ln_g, L, 0), lnrow(ln_b, L, 0), cur, xa, xT1, ident)
            elif L == 3:
                mu_d = inp("rwkv_mu", inputs["rwkv_mu"][0])
                w_rkv = inp("rwkv_w_rkv", inputs["rwkv_w_rkv"][0])
                w_out = inp("rwkv_w_out", inputs["rwkv_w_out"][0])
                vec_d = inp("rwkv_vec", inputs["rwkv_vec"][0])
                la_d = inp("rwkv_lora_a", inputs["rwkv_lora_a"][0])
                lb_d = inp("rwkv_lora_b", inputs["rwkv_lora_b"][0])
                ga_d = inp("rwkv_gate_a", inputs["rwkv_gate_a"][0])
                gb_d = inp("rwkv_gate_b", inputs["rwkv_gate_b"][0])
                lnx_d = inp("rwkv_ln_x", inputs["rwkv_ln_x"][0])
                fm_d = k.dram("rw_fm", [5, 1024, S], F32)
                rv_d = k.dram("rw_v", [S, 1024], F32)
                rg_d = k.dram("rw_g", [S, 1024], F32)
                rbo_d = k.dram("rw_bonus", [S, 1024], F32)
                ry_d = k.dram("rw_y", [S, 1024], F32)
                if "rwseq" not in DBG:
                    tm_d = k.dram("rw_tm", [2, S, 1024], F32)
                    wc_d = k.dram("rw_wc", [1024, 64], F32)
                    ustr = inp("rw_ustr", consts["rw_ustr"]); uincl = inp("rw_uincl", consts["rw_uincl"])
                    lstr = inp("rw_lstr", consts["rw_lstr"]); chsel = inp("rw_chsel", consts["rw_chsel"])
                    phase_rwkv_proj(k, xT0, mu_d, w_rkv, vec_d, la_d, lb_d, ga_d, gb_d, ident, fm_d, rv_d, rg_d, rbo_d,
                                    tm_d=tm_d, wc_d=wc_d, uincl_d=uincl, chsel_d=chsel)
                    if "rwA" not in DBG:
                        phase_rwkv_chunk(k, fm_d, tm_d, rv_d, wc_d, ustr, uincl, lstr, ident, ry_d)
                else:
                    phase_rwkv_proj(k, xT0, mu_d, w_rkv, vec_d, la_d, lb_d, ga_d, gb_d, ident, fm_d, rv_d, rg_d, rbo_d)
                if "rwseq" in DBG and "rwA" not in DBG and "rwC" not in DBG:
                    phase_rwkv_scan(k, fm_d, rv_d, ry_d, nsteps=(int(os.environ.get("RWN", S))))
                if "rwA" not in DBG and "rwB" not in DBG:
                    phase_rwkv_out(k, ry_d, rbo_d, rg_d, lnx_d, w_out, lnrow(ln_g, L, 0), lnrow(ln_b, L, 0), cur, xa, xT1, ident)
                else:
                    phase_outproj(k, T(ogT.h[0:1024, :], ogT.buf), w_out, 8, lnrow(ln_g, L, 0), lnrow(ln_b, L, 0), cur, xa, xT1, ident)
            else:
                raise NotImplementedError
            dst = out if last else xb_
            phase_mlp(k, T(mlp_w1.h[L], mlp_w1.buf), T(mlp_w2.h[L], mlp_w2.buf), lnrow(ln_g, L, 1), lnrow(ln_b, L, 1),
                      xa, xT1, dst, (None if last else xT0), ident)
            cur = xb_
        k.barrier()
    in_maps = []
    for c in range(ncores):
        m = dict(shared)
        m["x"] = np.ascontiguousarray(inputs["x"][c])
        in_maps.append(m)
    return nc, in_maps


MLA_H = 16
MLA_SCALE = 192.0 ** -0.5


def phase_mla_down(k, xT_in, w_down_d, nq_d, nkv_d, cosT_d, sinT_d, ident_d, cT_d):
    k.begin_phase()
    identb = k.sb([128, 128], BF16, "identb")
    k.dma(identb[:, :], ident_d[:, :], q="pool")
    w = k.sb([128, 8, 448], BF16, "wdown")
    load_w_bf16(k, w[:, :, :], w_down_d.h.rearrange("(c p) f -> p c f", p=128), w_down_d.buf)
    gq = k.sb([128, 384], F32, "gq")
    k.dma(gq[:, 0:256], V(nq_d.ap.partition_broadcast(128), nq_d.buf))
    k.dma(gq[:, 256:384], V(nkv_d.ap.partition_broadcast(128), nkv_d.buf))
    xb = [k.sb([128, 8, 128], BF16, "xb") for _ in range(2)]
    cs = [k.sb([128, 32], F32, "cs") for _ in range(2)]
    sn = [k.sb([128, 32], F32, "sn") for _ in range(2)]
    sq = k.sb([128, 384], F32, "sq")
    ss = [k.sb([128, 2], F32, "ss") for _ in range(2)]
    cn = [k.sb([128, 384], F32, "cn") for _ in range(2)]
    cc = [k.sb([128, 448], BF16, "cc") for _ in range(2)]
    t1 = k.sb([128, 64], F32, "t1")
    t2 = k.sb([128, 64], F32, "t2")
    cTs = [k.sb([128, 4, 128], BF16, "cTs") for _ in range(2)]
    pc = [k.ps([128, 448], F32, "pc") for _ in range(2)]
    pt = [k.ps([128, 4, 128], BF16, "pt") for _ in range(2)]
    xTv = xT_in.h.rearrange("(c p) t -> p c t", p=128)
    nt = S // 128

    def load(ti):
        i = ti % 2
        sl = slice(ti * 128, (ti + 1) * 128)
        k.dma(xb[i][:, :, :], V(xTv[:, :, sl], xT_in.buf))
        k.dma(cs[i][:, :], cosT_d[sl, :])
        k.dma(sn[i][:, :], sinT_d[sl, :])

    load(0)
    for ti in range(nt):
        i = ti % 2
        sl = slice(ti * 128, (ti + 1) * 128)
        if ti + 1 < nt:
            load(ti + 1)
        p = pc[i]
        for dc in range(8):
            k.mm(p[:, :], xb[i][:, dc, :], w[:, dc, :], start=(dc == 0), stop=(dc == 7))
        k.act(sq[:, :], p[:, 0:384], AF.Square)
        k.reduce(ss[i][:, 0:1], sq[:, 0:256], ALU.add)
        k.reduce(ss[i][:, 1:2], sq[:, 256:384], ALU.add)
        k.act(ss[i][:, 0:1], ss[i][:, 0:1], AF.Ln, bias=1e-6, scale=1.0 / 256)
        k.act(ss[i][:, 1:2], ss[i][:, 1:2], AF.Ln, bias=1e-6, scale=1.0 / 128)
        k.act(ss[i][:, :], ss[i][:, :], AF.Exp, scale=-0.5)
        k.ts(cn[i][:, 0:256], p[:, 0:256], ss[i][:, 0:1], None, op0=ALU.mult)
        k.ts(cn[i][:, 256:384], p[:, 256:384], ss[i][:, 1:2], None, op0=ALU.mult)
        k.tt(cc[i][:, 0:384], cn[i][:, :], gq[:, :], ALU.mult)
        k.tt(t1[:, 0:32], p[:, 384:416], cs[i][:, :], ALU.mult)
        k.tt(t2[:, 0:32], p[:, 416:448], sn[i][:, :], ALU.mult)
        k.tt(cc[i][:, 384:416], t1[:, 0:32], t2[:, 0:32], ALU.subtract)
        k.tt(t1[:, 32:64], p[:, 416:448], cs[i][:, :], ALU.mult)
        k.tt(t2[:, 32:64], p[:, 384:416], sn[i][:, :], ALU.mult)
        k.tt(cc[i][:, 416:448], t1[:, 32:64], t2[:, 32:64], ALU.add)
        for c in range(3):
            k.tr(pt[i][:, c, :], cc[i][:, c * 128:(c + 1) * 128], identb[:, :])
        k.tr(pt[i][0:64, 3, :], cc[i][:, 384:448], identb[:, :])
        k.copy(cTs[i][:, 0:3, :], pt[i][:, 0:3, :], e="act")
        k.copy(cTs[i][0:64, 3, :], pt[i][0:64, 3, :], e="act")
        k.dma(V(cT_d.h[0:3, :, sl].rearrange("c p t -> p c t"), cT_d.buf), cTs[i][:, 0:3, :])
        k.dma(V(cT_d.h[3, 0:64, sl], cT_d.buf), cTs[i][0:64, 3, :])
    k.end_phase()


def phase_mla_up(k, cT_d, w_uq_d, w_uqsw_d, w_ukv_d, c2_d, s2_d, qnT_d, qpT_d, knT_d, v_d):
    k.begin_phase()
    wq = k.sb([128, 2, 3072], BF16, "wq")
    wqs = k.sb([128, 2, 1024], BF16, "wqs")
    wkv = k.sb([128, 4096], BF16, "wkv")
    for rc in range(2):
        load_w_bf16(k, wq[:, rc, :], w_uq_d.h[rc * 128:(rc + 1) * 128, :], w_uq_d.buf)
        load_w_bf16(k, wqs[:, rc, :], w_uqsw_d.h[rc * 128:(rc + 1) * 128, :], w_uqsw_d.buf)
    load_w_bf16(k, wkv[:, :], w_ukv_d.h[:, :], w_ukv_d.buf)
    cb = [k.sb([128, 3, 512], BF16, "cb") for _ in range(2)]
    c2 = [k.sb([64, 512], F32, "c2") for _ in range(2)]
    s2 = [k.sb([64, 512], F32, "s2") for _ in range(2)]
    ob = [k.sb([128, 512], BF16, "ob") for _ in range(3)]
    t1 = k.sb([64, 512], F32, "t1")
    t2 = k.sb([64, 512], F32, "t2")
    pp = [k.ps([128, 512], F32, "pp") for _ in range(6)]
    nb = S // 512
    cTv = cT_d.h[0:3, :, :].rearrange("c p t -> p c t")
    wkv4 = wkv.h[:, :].rearrange("p (h two d) -> p h two d", two=2, d=128)

    def load(bi):
        i = bi % 2
        sl = slice(bi * 512, (bi + 1) * 512)
        k.dma(cb[i][:, :, :], V(cTv[:, :, sl], cT_d.buf))
        k.dma(c2[i][:, :], c2_d[:, sl])
        k.dma(s2[i][:, :], s2_d[:, sl])

    load(0)
    n = 0
    m = 0
    for bi in range(nb):
        i = bi % 2
        sl = slice(bi * 512, (bi + 1) * 512)
        if bi + 1 < nb:
            load(bi + 1)
        for h in range(MLA_H):
            p = pp[n % 6]; n += 1
            o = ob[m % 3]; m += 1
            for rc in range(2):
                k.mm(p[:, :], wq[:, rc, h * 192:h * 192 + 128], cb[i][:, rc, :], start=(rc == 0), stop=(rc == 1))
            k.act(o[:, :], p[:, :], AF.Copy, scale=MLA_SCALE)
            k.dma(qnT_d[h, :, sl], o[:, :])
            pa = pp[n % 6]; n += 1
            pb = pp[n % 6]; n += 1
            o = ob[m % 3]; m += 1
            for rc in range(2):
                k.mm(pa[0:64, :], wq[:, rc, h * 192 + 128:h * 192 + 192], cb[i][:, rc, :], start=(rc == 0), stop=(rc == 1))
            for rc in range(2):
                k.mm(pb[0:64, :], wqs[:, rc, h * 64:(h + 1) * 64], cb[i][:, rc, :], start=(rc == 0), stop=(rc == 1))
            k.tt(t1[:, :], pa[0:64, :], c2[i][:, :], ALU.mult)
            k.tt(t2[:, :], pb[0:64, :], s2[i][:, :], ALU.mult)
            k.tt(o[0:64, :], t1[:, :], t2[:, :], ALU.add)
            k.dma(qpT_d[h, :, sl], o[0:64, :])
            p = pp[n % 6]; n += 1
            o = ob[m % 3]; m += 1
            k.mm(p[:, :], wkv[:, h * 256:h * 256 + 128], cb[i][:, 2, :], start=True, stop=True)
            k.copy(o[:, :], p[:, :], e="act")
            k.dma(knT_d[h, :, sl], o[:, :])
        for tt in range(4):
            rows = slice(bi * 512 + tt * 128, bi * 512 + (tt + 1) * 128)
            for hg in range(4):
                p = pp[n % 6]; n += 1
                o = ob[m % 3]; m += 1
                k.mm(V(p.h[:, :].rearrange("p (h d) -> p h d", d=128), p.buf), cb[i][:, 2, tt * 128:(tt + 1) * 128],
                     V(wkv4[:, hg * 4:hg * 4 + 4, 1, :], wkv.buf), start=True, stop=True)
                k.copy(o[:, :], p[:, :], e="act")
                k.dma(v_d[rows, hg * 512:(hg + 1) * 512], o[:, :])
    k.end_phase()


def phase_mla_attn(k, qnT_d, qpT_d, knT_d, cT_d, v_d, mask4_d, ogT_d):
    k.begin_phase()
    masks = k.sb([128, 4, 512], BF16, "masks")
    k.dma(masks[:, :, :], V(mask4_d.h.rearrange("m p q -> p m q"), mask4_d.buf), q="pool")
    ones = k.sb([128, 128], BF16, "ones")
    k.memset(ones[:, :], 1.0)
    kp = k.sb([64, S], BF16, "kp")
    k.dma(kp[:, :], cT_d[3, 0:64, :])
    qn = [k.sb([128, S], BF16, "qn") for _ in range(2)]
    qp = [k.sb([64, S], BF16, "qp") for _ in range(2)]
    kn = [k.sb([128, S], BF16, "kn") for _ in range(2)]
    vv = [k.sb([128, 32, 128], BF16, "vv") for _ in range(2)]
    pb = [k.sb([128, 512], BF16, "pb") for _ in range(4)]
    rd = [k.sb([128, 512], F32, "rd") for _ in range(2)]
    oo = [k.sb([128, 512], BF16, "oo") for _ in range(2)]
    psT = [k.ps([128, 512], F32, "psT") for _ in range(4)]
    po = [k.ps([128, 512], F32, "po") for _ in range(2)]
    pd = [k.ps([128, 512], F32, "pd") for _ in range(2)]

    def load(h):
        i = h % 2
        k.dma(qn[i][:, :], qnT_d[h, :, :])
        k.dma(qp[i][:, :], qpT_d[h, :, :])
        k.dma(kn[i][:, :], knT_d[h, :, :])
        k.dma(vv[i][:, :, :], V(v_d.h[:, h * 128:(h + 1) * 128].rearrange("(n p) f -> p n f", p=128), v_d.buf))

    load(0)
    LA = 3
    state = {"n": 0}
    for h in range(MLA_H):
        i = h % 2
        if h + 1 < MLA_H:
            load(h + 1)
        items = [(Q, kt) for Q in range(8) for kt in range(4 * Q + 4)]
        slots = {}

        def emit_qk(it, i=i, slots=slots):
            Q, kt = it
            qs = slice(Q * 512, (Q + 1) * 512)
            ks = slice(kt * 128, (kt + 1) * 128)
            n = state["n"]
            state["n"] += 1
            ps = psT[n % 4]
            p = pb[n % 4]
            slots[it] = p
            k.mm(ps[:, :], kn[i][:, ks], qn[i][:, qs], start=True, stop=False)
            k.mm(ps[:, :], kp[:, ks], qp[i][:, qs], start=False, stop=True)
            k.act(p[:, :], ps[:, :], AF.Exp)
            if kt >= 4 * Q:
                k.tt(p[:, :], p[:, :], masks[:, kt - 4 * Q, :], ALU.mult)

        def emit_pv(it, i=i, h=h, slots=slots):
            Q, kt = it
            qs = slice(Q * 512, (Q + 1) * 512)
            j = Q % 2
            nk = 4 * Q + 4
            p = slots.pop(it)
            k.mm(po[j][:, :], vv[i][:, kt, :], p[:, :], start=(kt == 0), stop=(kt == nk - 1))
            k.mm(pd[j][:, :], ones[:, :], p[:, :], start=(kt == 0), stop=(kt == nk - 1))
            if kt == nk - 1:
                k.op("dve", lambda en, j=j: en.reciprocal(rd[j].h[:, :], pd[j].h[:, :]), [pd[j][:, :]], [rd[j][:, :]])
                k.tt(oo[j][:, :], po[j][:, :], rd[j][:, :], ALU.mult)
                k.dma(ogT_d[h * 128:(h + 1) * 128, qs], oo[j][:, :])

        for idx, it in enumerate(items):
            emit_qk(it)
            if idx >= LA:
                emit_pv(items[idx - LA])
        for it in items[len(items) - LA:]:
            emit_pv(it)
    k.end_phase()


DIL_PAIRS = ((128, 1), (512, 4), (2048, 16))


def phase_dil(k, xT_in, w_in_d, w_sw_d, c2_d, s2_d, mle_d, mge_d, U_d, Den_d):
    k.begin_phase()
    xs = k.sb([128, 8, S], BF16, "xs")
    xTv = xT_in.h.rearrange("(c p) t -> p c t", p=128)
    for c in range(8):
        k.dma(xs[:, c, :], V(xTv[:, c, :], xT_in.buf))
    c2 = k.sb([32, S], F32, "c2"); s2 = k.sb([32, S], F32, "s2")
    for t_, d_ in ((c2, c2_d), (s2, s2_d)):
        k.dma(t_[:, :], d_[:, :])
    mle = k.sb([128, 128], BF16, "mle"); mge = k.sb([128, 128], BF16, "mge")
    k.dma(mle[:, :], mle_d[:, :], q="pool")
    k.dma(mge[:, :], mge_d[:, :], q="pool")
    ones = k.sb([128, 8], BF16, "ones")
    k.memset(ones[:, :], 1.0)
    wg = k.sb([128, 8, 3072], BF16, "wg")
    wsw = k.sb([128, 8, 512], BF16, "wsw")
    qS = k.sb([128, 8, 512], BF16, "qS")
    kS = [k.sb([128, 8, 512], BF16, "kS") for _ in range(2)]
    vb = [k.sb([128, 1024], BF16, "vb") for _ in range(2)]
    t1 = k.sb([32, 2, 512], F32, "t1"); t2 = k.sb([32, 2, 512], F32, "t2")
    pc = [k.sb([128, 4, 128], BF16, "pc") for _ in range(2)]
    ppv = [k.sb([128, 4, 128], BF16, "ppv") for _ in range(2)]
    uo = [k.sb([128, 1024], F32, "uo") for _ in range(2)]
    dn = [k.sb([128, 8], F32, "dn") for _ in range(2)]
    A2 = k.ps([128, 1024], F32, "A2")
    B2 = k.ps([128, 1024], F32, "B2")
    Vp = k.ps([128, 1024], F32, "Vp")
    X = [k.ps([128, 4, 128], F32, "X") for _ in range(2)]
    A2p = T(A2.h[:, :].rearrange("p (a b) -> p a b", b=512), A2.buf)
    B2p = T(B2.h[:, :].rearrange("p (a b) -> p a b", b=512), B2.buf)
    A = T(A2.h[:, :].rearrange("p (a b) -> p a b", b=128), A2.buf)
    wv = w_in_d.h.rearrange("(c p) f -> p c f", p=128)
    wsv = w_sw_d.h.rearrange("(c p) f -> p c f", p=128)
    nblk = 0
    nsb = 0
    for g, (window, dil) in enumerate(DIL_PAIRS):
        if "dilg" in DBG and ("dilg%d" % g) not in DBG:
            continue
        for c in range(8):
            load_w_bf16(k, wg[:, c, :], wv[:, c, g * 3072:(g + 1) * 3072], w_in_d.buf)
        load_w_bf16(k, wsw[:, :, :], wsv[:, :, g * 512:(g + 1) * 512], w_sw_d.buf)
        L = S // dil
        nblk_r = L // 128
        SBk = min(4, nblk_r)
        ntok = SBk * 128
        for r in range(dil):
            for sb0 in range(0, nblk_r, SBk):
                cs_ = nsb % 2
                nsb += 1
                st_ = r + dil * sb0 * 128
                idxs = slice(st_, st_ + dil * (ntok - 1) + 1, dil)
                for j, dstS in enumerate((qS, kS[cs_])):
                    qsc = 128.0 ** -0.5 if j == 0 else 1.0
                    for hp in range(4):
                        for hh in range(2):
                            h = hp * 2 + hh
                            f0 = (j * 8 + h) * 128
                            for dc in range(8):
                                k.mm(A2p[:, hh, 0:ntok], wg[:, dc, f0:f0 + 128], xs[:, dc, idxs], start=(dc == 0), stop=(dc == 7))
                            f0 = (j * 8 + h) * 32
                            for dc in range(8):
                                k.mm(B2p[0:32, hh, 0:ntok], wsw[:, dc, f0:f0 + 32], xs[:, dc, idxs], start=(dc == 0), stop=(dc == 7))
                        hs = slice(hp * 2, hp * 2 + 2)
                        k.ts(dstS[:, hs, 0:ntok], A2p[:, :, 0:ntok], qsc, None, op0=ALU.mult)
                        cb_ = V(c2.h[0:32, idxs].unsqueeze(1).broadcast_to([32, 2, ntok]), c2.buf)
                        sb_ = V(s2.h[0:32, idxs].unsqueeze(1).broadcast_to([32, 2, ntok]), s2.buf)
                        k.stt(t1[:, :, 0:ntok], A2p[0:32, :, 0:ntok], qsc, cb_, ALU.mult, ALU.mult)
                        k.stt(t2[:, :, 0:ntok], B2p[0:32, :, 0:ntok], qsc, sb_, ALU.mult, ALU.mult)
                        k.tt(dstS[0:32, hs, 0:ntok], t1[:, :, 0:ntok], t2[:, :, 0:ntok], ALU.add)
                for bl in range(SBk):
                    blk = sb0 + bl
                    start = r + dil * blk * 128
                    idx = slice(start, start + dil * 127 + 1, dil)
                    bs_ = slice(bl * 128, (bl + 1) * 128)
                    cur = nblk % 2
                    nblk += 1
                    for half in range(2):
                        for dc in range(8):
                            k.mm(Vp[:, half * 512:(half + 1) * 512], xs[:, dc, idx],
                                 wg[:, dc, 2048 + half * 512:2048 + (half + 1) * 512], start=(dc == 0), stop=(dc == 7))
                    k.copy(vb[cur][:, :], Vp[:, :], e="act")
                    has_prev = blk > 0
                    if bl > 0:
                        kprev, ps_ = kS[cs_], slice((bl - 1) * 128, bl * 128)
                    else:
                        kprev, ps_ = kS[1 - cs_], slice((SBk - 1) * 128, SBk * 128)
                    o_ = uo[cur]
                    d_ = dn[cur]
                    for hg in range(2):
                        sc, sp = X[0], X[1]
                        for hh in range(4):
                            h = hg * 4 + hh
                            k.mm(sc[:, hh, :], kS[cs_][:, h, bs_], qS[:, h, bs_], start=True, stop=True)
                        if has_prev:
                            for hh in range(4):
                                h = hg * 4 + hh
                                k.mm(sp[:, hh, :], kprev[:, h, ps_], qS[:, h, bs_], start=True, stop=True)
                        pcur, pprev = pc[hg], ppv[hg]
                        k.act(pcur[:, :, :], sc[:, :, :], AF.Exp)
                        k.tt(pcur[:, :, :], pcur[:, :, :], V(mle.h[:, :].unsqueeze(1).broadcast_to([128, 4, 128]), mle.buf), ALU.mult)
                        if has_prev:
                            k.act(pprev[:, :, :], sp[:, :, :], AF.Exp)
                            k.tt(pprev[:, :, :], pprev[:, :, :], V(mge.h[:, :].unsqueeze(1).broadcast_to([128, 4, 128]), mge.buf), ALU.mult)
                        for hh in range(4):
                            h = hg * 4 + hh
                            k.mm(A[:, h, :], pcur[:, hh, :], vb[cur][:, h * 128:(h + 1) * 128], start=True, stop=not has_prev)
                            if has_prev:
                                k.mm(A[:, h, :], pprev[:, hh, :], vb[1 - cur][:, h * 128:(h + 1) * 128], start=False, stop=True)
                            k.mm(B2[:, h:h + 1], pcur[:, hh, :], ones[:, 0:1], start=True, stop=not has_prev)
                            if has_prev:
                                k.mm(B2[:, h:h + 1], pprev[:, hh, :], ones[:, 0:1], start=False, stop=True)
                    k.copy(V(o_.h[:, :].rearrange("p (h d) -> p h d", d=128), o_.buf), A[:, :, :], e="dve")
                    k.copy(d_[:, :], B2[:, 0:8], e="dve")
                    k.dma(U_d[g, idx, :], o_[:, :])
                    k.dma(Den_d[g, idx, :], d_[:, :])
    k.end_phase()


def phase_dil_out(k, U_d, Den_d, w_out_d, g_row, b_row, x_in, x_out, xT_out, ident_d):
    k.begin_phase()
    ident = k.sb([128, 128], F32, "ident")
    k.dma(ident[:, :], ident_d[:, :])
    identb = k.sb([128, 128], BF16, "identb")
    k.dma(identb[:, :], ident_d[:, :], q="pool")
    w = k.sb([128, 8, D], BF16, "wout")
    load_w_bf16(k, w[:, :, :], w_out_d.h.rearrange("(c p) d -> p c d", p=128), w_out_d.buf)
    ln = LNStage(k, ident, g_row, b_row, x_in, x_out, xT_out)
    U = [[k.sb([128, 1024], F32, "U") for _ in range(3)] for _ in range(2)]
    Dn = [[k.sb([128, 8], F32, "Dn") for _ in range(3)] for _ in range(2)]
    ob = [k.sb([128, 1024], BF16, "ob") for _ in range(2)]
    oT = [k.sb([128, 8, 128], BF16, "oT") for _ in range(2)]
    po = [k.ps([128, 512], F32, "po") for _ in range(2)]
    pt = [k.ps([128, 8, 128], BF16, "pt") for _ in range(1)]
    nt = S // 128

    def load(ti):
        i = ti % 2
        sl = slice(ti * 128, (ti + 1) * 128)
        for g in range(3):
            k.dma(U[i][g][:, :], U_d[g, sl, :])
            k.dma(Dn[i][g][:, :], Den_d[g, sl, :])

    load(0)
    for ti in range(nt):
        i = ti % 2
        if ti + 1 < nt:
            load(ti + 1)
        ln.prefetch(ti)
        k.tt(U[i][0][:, :], U[i][0][:, :], U[i][1][:, :], ALU.add)
        k.tt(U[i][0][:, :], U[i][0][:, :], U[i][2][:, :], ALU.add)
        k.tt(Dn[i][0][:, :], Dn[i][0][:, :], Dn[i][1][:, :], ALU.add)
        k.tt(Dn[i][0][:, :], Dn[i][0][:, :], Dn[i][2][:, :], ALU.add)
        k.op("dve", lambda en, i=i: en.reciprocal(Dn[i][1].h[:, :], Dn[i][0].h[:, :]), [Dn[i][0][:, :]], [Dn[i][1][:, :]])
        k.tt(V(ob[i].h[:, :].rearrange("p (h d) -> p h d", d=128), ob[i].buf),
             V(U[i][0].h[:, :].rearrange("p (h d) -> p h d", d=128), U[i][0].buf),
             V(Dn[i][1].h[:, :].unsqueeze(2).broadcast_to([128, 8, 128]), Dn[i][1].buf), ALU.mult)
        for c in range(8):
            k.tr(pt[0][:, c, :], ob[i][:, c * 128:(c + 1) * 128], identb[:, :])
        k.copy(oT[i][:, :, :], pt[0][:, :, :], e="act")
        for hh in range(2):
            for fc in range(8):
                k.mm(po[hh][:, :], oT[i][:, fc, :], w[:, fc, hh * 512:(hh + 1) * 512], start=(fc == 0), stop=(fc == 7))
        ln.run(ti, [po[0][:, :], po[1][:, :]])
    ln.flush()
    k.end_phase()


RW_H = 16
SCAN_AUX = "dve"
RWKV_GN_EPS = 64e-5
EXP_M05 = float(np.exp(-0.5))


def bc3(t, lo, hi, n):
    return V(t.h[:, lo:hi].unsqueeze(2).broadcast_to([t.h.shape[0], hi - lo, n]), t.buf)


def v3(t, d=64):
    return V(t.h[:, :].rearrange("p (h d) -> p h d", d=d), t.buf)


def phase_rwkv_proj(k, xT_in, mu_d, w_rkv_d, vec_d, la_d, lb_d, ga_d, gb_d, ident_d,
                    fm_d, v_d, g_d, bonus_d, tm_d=None, wc_d=None, uincl_d=None, chsel_d=None):
    k.begin_phase()
    ident = k.sb([128, 128], F32, "ident")
    k.dma(ident[:, :], ident_d[:, :])
    wr = k.sb([128, 3, 8, 1024], BF16, "w_rkv")
    for m in range(3):
        for c in range(0, 8, 4):
            load_w_bf16(k, wr[:, m, c:c + 4, :], w_rkv_d.h[m].rearrange("(c p) f -> p c f", p=128)[:, c:c + 4, :], w_rkv_d.buf)
    la = k.sb([128, 2, 8, 64], BF16, "la")
    for m in range(2):
        load_w_bf16(k, la[:, m, :, :], la_d.h[m].rearrange("(c p) f -> p c f", p=128), la_d.buf)
    lb = k.sb([64, 2, 1024], BF16, "lb")
    load_w_bf16(k, lb[:, :, :], lb_d.h.rearrange("m r f -> r m f"), lb_d.buf)
    ga = k.sb([128, 8, 160], BF16, "ga")
    load_w_bf16(k, ga[:, :, :], ga_d.h.rearrange("(c p) f -> p c f", p=128), ga_d.buf)
    gb1 = k.sb([128, 1024], BF16, "gb1")
    gb2 = k.sb([32, 1024], BF16, "gb2")
    load_w_bf16(k, gb1[:, :], gb_d.h[0:128, :], gb_d.buf)
    load_w_bf16(k, gb2[:, :], gb_d.h[128:160, :], gb_d.buf)
    vec = k.sb([128, 5, 1024], F32, "vec")
    for m in range(5):
        k.dma(vec[:, m, :], V(vec_d.h[m, :].partition_broadcast(128), vec_d.buf))
    mu = k.sb([128, 6, 8], F32, "mu")
    mur = k.sb([48, 128], F32, "mur")
    k.dma(mur[:, :], V(mu_d.h.rearrange("m (c p) -> (m c) p", p=128), mu_d.buf))
    xb = [k.sb([128, 8, 129], BF16, "xb") for _ in range(2)]
    xx = k.sb([128, 8, 128], F32, "xx")
    xm = [k.sb([128, 8, 128], BF16, "xm") for _ in range(6)]
    tmpx = k.sb([128, 8, 128], F32, "tmpx")
    hw = k.sb([64, 2, 128], BF16, "hw")
    hg1 = k.sb([128, 128], BF16, "hg1")
    hg2 = k.sb([32, 128], BF16, "hg2")
    R = k.sb([128, 1024], F32, "R"); KR = k.sb([128, 1024], F32, "KR"); Vv = k.sb([128, 1024], F32, "Vv")
    Wd = k.sb([128, 1024], F32, "Wd"); Aa = k.sb([128, 1024], F32, "Aa"); KK = k.sb([128, 1024], F32, "KK")
    Kk = k.sb([128, 1024], F32, "Kk"); Bb = k.sb([128, 1024], F32, "Bb"); Gg = k.sb([128, 1024], F32, "Gg")
    Bo = k.sb([128, 1024], F32, "Bo"); T1 = k.sb([128, 1024], F32, "T1")
    ssum = k.sb([128, 16], F32, "ssum"); rn = k.sb([128, 16], F32, "rn"); rk = k.sb([128, 16], F32, "rk")
    chunked = tm_d is not None
    fmT = [k.sb([128, 16 if chunked else 8, 128], F32, "fmT") for _ in range(2)]
    if chunked:
        uincl = k.sb([128, 128], F32, "uincl")
        k.dma(uincl[:, :], uincl_d[:, :])
        chsel = k.sb([128, 2], F32, "chsel")
        k.dma(chsel[:, :], chsel_d[:, :])
        LWt = k.sb([128, 1024], F32, "LWt")
        E1 = k.sb([128, 1024], F32, "E1")
        wct = [k.sb([128, 8, 2], F32, "wct") for _ in range(2)]
    pA = [k.ps([128, 1024], F32, "pA") for _ in range(2)]
    pS = k.ps([128, 512], F32, "pS")
    pT = [k.ps([128, 4, 128], F32, "pT") for _ in range(2)]
    xTv = xT_in.h.rearrange("(c p) t -> p c t", p=128)
    nt = S // 128

    def load(ti):
        i = ti % 2
        if ti == 0:
            k.memset(xb[i][:, :, 0:1], 0.0)
            k.dma(xb[i][:, :, 1:129], V(xTv[:, :, 0:128], xT_in.buf))
        else:
            k.dma(xb[i][:, :, :], V(xTv[:, :, ti * 128 - 1:ti * 128 + 128], xT_in.buf))

    def proj_tm(dst_ps, xmix, wsel):
        for half in range(2):
            for dc in range(8):
                k.mm(dst_ps[:, half * 512:(half + 1) * 512], xmix[:, dc, :], wsel[:, dc, half * 512:(half + 1) * 512],
                     start=(dc == 0), stop=(dc == 7))

    k.tr(pS[:, 0:48], mur[:, :], ident[0:48, 0:48])
    k.copy(V(mu.h[:, :, :].rearrange("p m c -> p (m c)"), mu.buf), pS[:, 0:48])
    nfm = 0
    load(0)
    for ti in range(nt):
        i = ti % 2
        sl = slice(ti * 128, (ti + 1) * 128)
        if ti + 1 < nt:
            load(ti + 1)
        cur = xb[i]
        k.tt(xx[:, :, :], cur[:, :, 0:128], cur[:, :, 1:129], ALU.subtract)
        for m in range(6):
            k.tt(tmpx[:, :, :], xx[:, :, :], bc3(T(mu.h[:, m, :], mu.buf), 0, 8, 128), ALU.mult)
            k.tt(xm[m][:, :, :], tmpx[:, :, :], cur[:, :, 1:129], ALU.add)
        xr, xw, xk, xv, xa, xg = xm
        proj_tm(pA[0], xr, T(wr.h[:, 0], wr.buf)); k.copy(R[:, :], pA[0][:, :], e="act")
        proj_tm(pA[1], xk, T(wr.h[:, 1], wr.buf)); k.copy(KR[:, :], pA[1][:, :], e="act")
        proj_tm(pA[0], xv, T(wr.h[:, 2], wr.buf)); k.copy(Vv[:, :], pA[0][:, :], e="act")
        k.dma(v_d[sl, :], Vv[:, :])
        for dc in range(8):
            k.mm(pS[0:64, 0:128], la[:, 0, dc, :], xw[:, dc, :], start=(dc == 0), stop=(dc == 7))
        k.act(hw[:, 0, :], pS[0:64, 0:128], AF.Tanh)
        for half in range(2):
            k.mm(pA[1][:, half * 512:(half + 1) * 512], hw[:, 0, :], lb[:, 0, half * 512:(half + 1) * 512], start=True, stop=True)
        k.tt(T1[:, :], pA[1][:, :], vec[:, 0, :], ALU.add)
        k.act(T1[:, :], T1[:, :], AF.Sigmoid)
        if chunked:
            k.ts(Wd[:, :], T1[:, :], -EXP_M05, None, op0=ALU.mult)
        else:
            k.act(Wd[:, :], T1[:, :], AF.Exp, scale=-EXP_M05)
        for dc in range(8):
            k.mm(pS[0:64, 128:256], la[:, 1, dc, :], xa[:, dc, :], start=(dc == 0), stop=(dc == 7))
        k.copy(hw[:, 1, :], pS[0:64, 128:256], e="act")
        for half in range(2):
            k.mm(pA[0][:, half * 512:(half + 1) * 512], hw[:, 1, :], lb[:, 1, half * 512:(half + 1) * 512], start=True, stop=True)
        k.tt(T1[:, :], pA[0][:, :], vec[:, 1, :], ALU.add)
        k.act(Aa[:, :], T1[:, :], AF.Sigmoid)
        for dc in range(8):
            k.mm(pS[:, 256:384], ga[:, dc, 0:128], xg[:, dc, :], start=(dc == 0), stop=(dc == 7))
        for dc in range(8):
            k.mm(pS[0:32, 384:512], ga[:, dc, 128:160], xg[:, dc, :], start=(dc == 0), stop=(dc == 7))
        k.act(hg1[:, :], pS[:, 256:384], AF.Sigmoid)
        k.act(hg2[:, :], pS[0:32, 384:512], AF.Sigmoid)
        for half in range(2):
            hs = slice(half * 512, (half + 1) * 512)
            k.mm(pA[1][:, hs], hg1[:, :], gb1[:, hs], start=True, stop=False)
            k.mm(pA[1][:, hs], hg2[:, :], gb2[:, hs], start=False, stop=True)
        k.copy(Gg[:, :], pA[1][:, :], e="act")
        k.dma(g_d[sl, :], Gg[:, :])
        k.tt(KK[:, :], KR[:, :], vec[:, 2, :], ALU.mult)
        k.act(T1[:, :], KK[:, :], AF.Square)
        k.reduce(ssum[:, :], v3(T1), ALU.add)
        k.ts(ssum[:, :], ssum[:, :], 1e-24, None, op0=ALU.max)
        k.act(rn[:, :], ssum[:, :], AF.Ln)
        k.act(rn[:, :], rn[:, :], AF.Exp, scale=-0.5)
        k.tt(v3(KK), v3(KK), bc3(rn, 0, 16, 64), ALU.mult)
        k.stt(T1[:, :], Aa[:, :], -1.0, vec[:, 3, :], ALU.add, ALU.mult)
        k.stt(Kk[:, :], T1[:, :], 1.0, KR[:, :], ALU.add, ALU.mult)
        k.tt(Bb[:, :], KK[:, :], Aa[:, :], ALU.mult)
        k.tt(T1[:, :], R[:, :], Kk[:, :], ALU.mult)
        k.tt(T1[:, :], T1[:, :], vec[:, 4, :], ALU.mult)
        k.reduce(rk[:, :], v3(T1), ALU.add)
        k.tt(v3(Bo), v3(Vv), bc3(rk, 0, 16, 64), ALU.mult)
        k.dma(bonus_d[sl, :], Bo[:, :])
        if chunked:
            for half in range(2):
                hs = slice(half * 512, (half + 1) * 512)
                k.mm(pA[0][:, hs], uincl[:, :], Wd[:, hs], start=True, stop=True)
            k.copy(LWt[:, :], pA[0][:, :], e="act")
            for c in range(8):
                k.mm(pS[:, 48 + 2 * c:50 + 2 * c], Wd[:, c * 128:(c + 1) * 128], chsel[:, :], start=True, stop=True)
            wc = wct[ti % 2]
            k.act(V(wc.h[:, :, :].rearrange("p a b -> p (a b)"), wc.buf), pS[:, 48:64], AF.Exp)
            k.dma(V(wc_d.h.rearrange("(c p) n -> p c n", p=128)[:, :, 2 * ti:2 * ti + 2], wc_d.buf), wc[:, :, :])
            k.tt(T1[:, :], LWt[:, :], Wd[:, :], ALU.subtract)
            k.act(E1[:, :], T1[:, :], AF.Exp)
            k.tt(KK[:, :], KK[:, :], E1[:, :], ALU.mult)
            k.act(E1[:, :], LWt[:, :], AF.Exp)
            k.tt(R[:, :], R[:, :], E1[:, :], ALU.mult)
            k.act(E1[:, :], LWt[:, :], AF.Exp, scale=-1.0)
            k.tt(Bb[:, :], Bb[:, :], E1[:, :], ALU.mult)
            k.tt(Kk[:, :], Kk[:, :], E1[:, :], ALU.mult)
            k.dma(tm_d[0, sl, :], Bb[:, :])
            k.dma(tm_d[1, sl, :], Kk[:, :])
            fm_list = (KK, R, Bb, Kk)
        else:
            fm_list = (KK, Wd, Bb, Kk, R)
        if chunked:
            for m, src in enumerate(fm_list):
                ft = fmT[nfm % 2]
                nfm += 1
                for grp in range(4):
                    for c in range(4):
                        h = grp * 4 + c
                        k.tr(pT[grp % 2][0:64, c, :], src[:, h * 64:(h + 1) * 64], ident[:, :])
                    k.copy(V(ft.h[0:64, :, :].rearrange("p (a b) t -> p a b t", b=4)[:, grp, :, :], ft.buf),
                           pT[grp % 2][0:64, :, :], e=("act" if grp % 2 else "dve"))
                k.dma(V(fm_d.h[m].rearrange("(h j) t -> j h t", j=64)[:, :, sl], fm_d.buf),
                      V(ft.h[0:64, :, :], ft.buf) if False else V(ft.h[0:64, :, :].rearrange("p (a b) t -> p (a b) t", b=4), ft.buf))
        else:
            for m, src in enumerate(fm_list):
                ft = fmT[nfm % 2]
                nfm += 1
                for half in range(2):
                    for c in range(4):
                        cc = half * 4 + c
                        k.tr(pT[half][:, c, :], src[:, cc * 128:(cc + 1) * 128], ident[:, :])
                    k.copy(ft[:, half * 4:(half + 1) * 4, :], pT[half][:, :, :], e="act")
                k.dma(V(fm_d.h[m].rearrange("(c p) t -> p c t", p=128)[:, :, sl], fm_d.buf), ft[:, :, :])
    k.end_phase()


def phase_rwkv_scan(k, fm_d, v_d, y_d, nsteps=S):
    k.begin_phase()
    TB = 128
    ST = k.sb([128, 8, 64], F32, "ST")
    k.memset(ST[:, :, :], 0.0)
    negblk = k.sb([128, 128], F32, "negblk")
    k.memset(negblk[:, :], 0.0)
    k.memset(negblk[0:64, 0:64], -1.0)
    k.memset(negblk[64:128, 64:128], -1.0)
    sel2 = k.sb([2, 128], F32, "sel2")
    selT = k.sb([128, 2], F32, "selT")
    k.memset(selT[:, :], 0.0)
    k.memset(selT[0:64, 0:1], 1.0)
    k.memset(selT[64:128, 1:2], 1.0)
    k.ts(negblk[:, :], negblk[:, :], 1.0, None, op0=ALU.mult)
    tmpsel = k.sb([128, 128], F32, "tmpsel")
    k.ts(tmpsel[:, :], negblk[:, :], -1.0, None, op0=ALU.mult)
    k.dma(sel2[0:1, :], tmpsel[0:1, :])
    k.dma(sel2[1:2, :], tmpsel[64:65, :])
    ops = [[k.sb([128, 8, TB], F32, "fm%d" % m) for m in range(5)] for _ in range(2)]
    SB = 8
    v2 = [k.sb([2, SB, 512], F32, "v2") for _ in range(3)]
    yall = [k.sb([2, SB, 512], F32, "yall") for _ in range(2)]
    tmp = [k.sb([128, 8, 64], F32, "tmp") for _ in range(2)]
    tmp2 = k.sb([128, 8, 64], F32, "tmp2")
    tmp3 = [k.sb([128, 8, 64], F32, "tmp3") for _ in range(2)]
    tmp4 = [k.sb([128, 8, 64], F32, "tmp4") for _ in range(2)]
    vsb = [k.sb([128, 8, 64], F32, "vsb") for _ in range(2)]
    pv = [k.ps([128, 8, 64], F32, "pv") for _ in range(2)]
    psa = [k.ps([128, 8, 64], F32, "psa") for _ in range(2)]
    py = [k.ps([2, 512], F32, "py") for _ in range(2)]
    nb = nsteps // TB

    def load(bi):
        i = bi % 2
        sl = slice(bi * TB, (bi + 1) * TB)
        for m in range(5):
            k.dma(ops[i][m][:, :, :], V(fm_d.h[m].rearrange("(c p) t -> p c t", p=128)[:, :, sl], fm_d.buf))

    def load_v(si):
        sl = slice(si * SB, (si + 1) * SB)
        k.dma(V(v2[si % 3].h[:, :, :].rearrange("c t (hh i) -> c t hh i", i=64), v2[si % 3].buf),
              V(v_d.h[sl, :].rearrange("t (hh c i) -> c t hh i", c=2, i=64), v_d.buf))

    def sc(t_, tl):
        return V(t_.h[:, :, tl:tl + 1].broadcast_to([128, 8, 64]), t_.buf)

    load(0)
    load_v(0)
    load_v(1)
    nsb = nsteps // SB
    for bi in range(nb):
        i = bi % 2
        if bi + 1 < nb:
            load(bi + 1)
        kkT, wT, bT, kT_, rT = ops[i]
        for tl in range(TB):
            s = tl % 2
            tg = bi * TB + tl
            si, sl_ = tg // SB, tg % SB
            if sl_ == 0 and si + 2 < nsb:
                load_v(si + 2)
            k.mm(pv[s][:, :, :], sel2[:, :], V(v2[si % 3].h[:, sl_, :].rearrange("c (hh i) -> c hh i", i=64), v2[si % 3].buf),
                 start=True, stop=True)
            k.tt(tmp3[s][:, :, :], pv[s][:, :, :], sc(kT_, tl), ALU.mult, e=SCAN_AUX)
            k.tt(tmp[s][:, :, :], ST[:, :, :], sc(kkT, tl), ALU.mult)
            k.mm(psa[s][:, :, :], negblk[:, :], tmp[s][:, :, :], start=True, stop=True)
            k.tt(ST[:, :, :], ST[:, :, :], sc(wT, tl), ALU.mult)
            k.tt(ST[:, :, :], ST[:, :, :], tmp3[s][:, :, :], ALU.add)
            k.tt(tmp2[:, :, :], psa[s][:, :, :], sc(bT, tl), ALU.mult)
            k.tt(ST[:, :, :], ST[:, :, :], tmp2[:, :, :], ALU.add)
            k.tt(tmp4[s][:, :, :], ST[:, :, :], sc(rT, tl), ALU.mult, e=SCAN_AUX)
            k.mm(py[s][:, :], selT[:, :], V(tmp4[s].h[:, :, :].rearrange("p a b -> p (a b)"), tmp4[s].buf), start=True, stop=True)
            k.copy(yall[si % 2][:, sl_, :], py[s][:, :], e="act")
            if sl_ == SB - 1:
                sl = slice(si * SB, (si + 1) * SB)
                k.dma(V(y_d.h[sl, :].rearrange("t (hh c i) -> c t hh i", c=2, i=64), y_d.buf),
                      V(yall[si % 2].h[:, :, :].rearrange("c t (hh i) -> c t hh i", i=64), yall[si % 2].buf))
    k.end_phase()


def phase_rwkv_out(k, y_d, bonus_d, g_d, lnx_d, w_out_d, g_row, b_row, x_in, x_out, xT_out, ident_d):
    k.begin_phase()
    ident = k.sb([128, 128], F32, "ident")
    k.dma(ident[:, :], ident_d[:, :])
    identb = k.sb([128, 128], BF16, "identb")
    k.dma(identb[:, :], ident_d[:, :], q="pool")
    w = k.sb([128, 8, D], BF16, "wout")
    load_w_bf16(k, w[:, :, :], w_out_d.h.rearrange("(c p) d -> p c d", p=128), w_out_d.buf)
    lg = k.sb([128, 1024], F32, "lg"); lbb = k.sb([128, 1024], F32, "lbb")
    k.dma(lg[:, :], V(lnx_d.h[0, :].partition_broadcast(128), lnx_d.buf))
    k.dma(lbb[:, :], V(lnx_d.h[1, :].partition_broadcast(128), lnx_d.buf))
    ln = LNStage(k, ident, g_row, b_row, x_in, x_out, xT_out)
    Y = [k.sb([128, 1024], F32, "Y") for _ in range(2)]
    Bo = [k.sb([128, 1024], F32, "Bo") for _ in range(2)]
    Gg = [k.sb([128, 1024], F32, "Gg") for _ in range(2)]
    T1 = k.sb([128, 1024], F32, "T1")
    s1 = k.sb([128, 16], F32, "s1"); s2 = k.sb([128, 16], F32, "s2"); mean = k.sb([128, 16], F32, "mean")
    var = k.sb([128, 16], F32, "var"); rstd = k.sb([128, 16], F32, "rstd")
    ob = [k.sb([128, 1024], BF16, "ob") for _ in range(2)]
    oT = [k.sb([128, 8, 128], BF16, "oT") for _ in range(2)]
    po = [k.ps([128, 512], F32, "po") for _ in range(2)]
    pt = k.ps([128, 8, 128], BF16, "pt")
    nt = S // 128

    def load(ti):
        i = ti % 2
        sl = slice(ti * 128, (ti + 1) * 128)
        k.dma(Y[i][:, :], y_d[sl, :])
        k.dma(Bo[i][:, :], bonus_d[sl, :])
        k.dma(Gg[i][:, :], g_d[sl, :])

    load(0)
    for ti in range(nt):
        i = ti % 2
        if ti + 1 < nt:
            load(ti + 1)
        ln.prefetch(ti)
        y = Y[i]
        k.reduce(s1[:, :], v3(y), ALU.add)
        k.act(T1[:, :], y[:, :], AF.Square)
        k.reduce(s2[:, :], v3(T1), ALU.add)
        k.ts(mean[:, :], s1[:, :], 1.0 / 64, None, op0=ALU.mult)
        k.tt(var[:, :], mean[:, :], mean[:, :], ALU.mult)
        k.stt(var[:, :], s2[:, :], 1.0 / 64, var[:, :], ALU.mult, ALU.subtract)
        k.act(rstd[:, :], var[:, :], AF.Ln, bias=RWKV_GN_EPS)
        k.act(rstd[:, :], rstd[:, :], AF.Exp, scale=-0.5)
        k.tt(v3(y), v3(y), bc3(mean, 0, 16, 64), ALU.subtract)
        k.tt(v3(y), v3(y), bc3(rstd, 0, 16, 64), ALU.mult)
        k.tt(y[:, :], y[:, :], lg[:, :], ALU.mult)
        k.tt(y[:, :], y[:, :], lbb[:, :], ALU.add)
        k.tt(y[:, :], y[:, :], Bo[i][:, :], ALU.add)
        k.tt(ob[i][:, :], y[:, :], Gg[i][:, :], ALU.mult)
        for c in range(8):
            k.tr(pt[:, c, :], ob[i][:, c * 128:(c + 1) * 128], identb[:, :])
        k.copy(oT[i][:, :, :], pt[:, :, :], e="act")
        for hh in range(2):
            for fc in range(8):
                k.mm(po[hh][:, :], oT[i][:, fc, :], w[:, fc, hh * 512:(hh + 1) * 512], start=(fc == 0), stop=(fc == 7))
        ln.run(ti, [po[0][:, :], po[1][:, :]])
    ln.flush()
    k.end_phase()


def kernel(**inputs):
    inputs = {k_: np.asarray(v_) for k_, v_ in inputs.items()}
    nc, in_maps = build_all(inputs, layers=(0, 1, 2, 3), ncores=NCORES)
    res = run_bass_kernel_spmd(nc, in_maps, core_ids=list(range(NCORES)))
    out = np.stack([np.asarray(res.results[c]["out"], dtype=np.float32) for c in range(NCORES)], axis=0)
    return out


def phase_rwkv_chunk(k, fm_d, tm_d, v_d, wc_d, ustr_d, uincl_d, lstr_d, ident_d, y_d):
    k.begin_phase()
    ident = k.sb([64, 64], F32, "ident"); k.dma(ident[:, :], ident_d[0:64, 0:64])
    ustr = k.sb([64, 64], F32, "ustr"); k.dma(ustr[:, :], ustr_d[0:64, 0:64])
    uinc = k.sb([64, 64], F32, "uinc"); k.dma(uinc[:, :], uincl_d[0:64, 0:64])
    lstr = k.sb([64, 64], F32, "lstr"); k.dma(lstr[:, :], lstr_d[0:64, 0:64])
    wc = k.sb([64, 16, 64], F32, "wc")
    k.dma(wc[:, :, :], V(wc_d.h.rearrange("(h j) n -> j h n", j=64), wc_d.buf))
    fm = [[k.sb([64, 16, 128], F32, "fm%d" % m) for m in range(4)] for _ in range(2)]
    tmb = [[k.sb([64, 2, 1024], F32, "tm%d" % m) for m in range(3)] for _ in range(2)]
    NB = 2
    TT = [k.sb([64, 16, 64], F32, "TT") for _ in range(NB)]
    ArT = [k.sb([64, 16, 64], F32, "ArT") for _ in range(NB)]
    BV = [k.sb([64, 16, 64], F32, "BV") for _ in range(NB)]
    BrV = [k.sb([64, 16, 64], F32, "BrV") for _ in range(NB)]
    KtV = [k.sb([64, 16, 64], F32, "KtV") for _ in range(NB)]
    BTs = k.sb([64, 8, 64], F32, "BTs"); BrTs = k.sb([64, 8, 64], F32, "BrTs")
    Xs = [k.sb([64, 8, 64], F32, "Xs") for _ in range(2)]
    XTs = [k.sb([64, 8, 64], F32, "XTs") for _ in range(2)]
    G = k.sb([64, 16, 64], F32, "G")
    k.memset(G[:, :, :], 0.0)
    Zs = [k.sb([64, 8, 64], F32, "Zs") for _ in range(2)]
    Ps = [k.sb([64, 8, 64], F32, "Ps") for _ in range(2)]
    Yt = [k.sb([64, 2, 1024], F32, "Yt") for _ in range(2)]
    g = [k.ps([64, 8, 64], F32, "g%d" % i) for i in range(5)]
    s0 = k.ps([64, 8, 64], F32, "s0"); s1 = k.ps([64, 8, 64], F32, "s1"); s2 = k.ps([64, 8, 64], F32, "s2")
    nt = S // 128

    def load(n):
        i = n % 2
        sl = slice(n * 128, (n + 1) * 128)
        for m in range(4):
            k.dma(fm[i][m][:, :, :], V(fm_d.h[m].rearrange("(h j) t -> j h t", j=64)[:, :, sl], fm_d.buf))
        k.dma(tmb[i][0][:, :, :], V(tm_d.h[0, sl, :].rearrange("(c t) f -> t c f", t=64), tm_d.buf))
        k.dma(tmb[i][1][:, :, :], V(tm_d.h[1, sl, :].rearrange("(c t) f -> t c f", t=64), tm_d.buf))
        k.dma(tmb[i][2][:, :, :], V(v_d.h[sl, :].rearrange("(c t) f -> t c f", t=64), v_d.buf))

    def mb(m_):
        return V(m_.h[:, :].unsqueeze(1).broadcast_to([64, 8, 64]), m_.buf)

    def gram(cidx):
        n, ch = cidx // 2, cidx % 2
        i = n % 2
        j3 = cidx % NB
        cs = slice(ch * 64, ch * 64 + 64)
        kaT, rtT, beT, ktT = fm[i]
        be, kt, vv = tmb[i]
        for hg in range(2):
            for q in range(8):
                h = hg * 8 + q
                k.mm(g[3][:, q, :], kt[:, ch, h * 64:(h + 1) * 64], vv[:, ch, h * 64:(h + 1) * 64])
            k.copy(KtV[j3][:, hg * 8:hg * 8 + 8, :], g[3][:, :, :], e="act")
        for hg in range(2):
            for q in range(8):
                h = hg * 8 + q
                k.mm(g[0][:, q, :], beT[:, h, cs], kaT[:, h, cs])
                k.mm(g[1][:, q, :], beT[:, h, cs], rtT[:, h, cs])
                k.mm(g[2][:, q, :], ktT[:, h, cs], kaT[:, h, cs])
                k.mm(g[3][:, q, :], ktT[:, h, cs], rtT[:, h, cs])
                k.mm(g[4][:, q, :], kaT[:, h, cs], beT[:, h, cs])
            yield
            hs = slice(hg * 8, hg * 8 + 8)
            X, XT = Xs[0], XTs[0]
            k.tt(XT[:, :, :], g[0][:, :, :], mb(ustr), ALU.mult)
            k.tt(ArT[j3][:, hs, :], g[1][:, :, :], mb(uinc), ALU.mult)
            k.tt(BTs[:, :, :], g[2][:, :, :], mb(ustr), ALU.mult)
            k.tt(BrTs[:, :, :], g[3][:, :, :], mb(uinc), ALU.mult)
            k.tt(X[:, :, :], g[4][:, :, :], mb(lstr), ALU.mult)
            Q = V(TT[j3].h[:, hs, :], TT[j3].buf)
            k.tt(Q, mb(ident), XT[:, :, :], ALU.subtract)
            yield
            for q in range(8):
                h = hg * 8 + q
                hc = slice(h * 64, (h + 1) * 64)
                k.mm(g[1][:, q, :], BTs[:, q, :], vv[:, ch, hc])
                k.mm(g[2][:, q, :], BrTs[:, q, :], vv[:, ch, hc])
                if "ckG3" in DBG:
                    k.mm(g[3][:, q, :], BrTs[:, q, :], vv[:, ch, hc])
                elif "ckG4" in DBG:
                    k.mm(g[3][:, q, :], kt[:, ch, hc], BrTs[:, q, :])
                elif "ckG5" in DBG:
                    k.mm(g[3][:, q, :], vv[:, 0, q * 64:(q + 1) * 64], BrTs[:, q, :])
                elif "ckG7" in DBG:
                    k.mm(g[3][:, q, :], BrTs[:, q, :], ktc[cidx % 2][:, h, :])
                elif "ckG6" in DBG:
                    k.mm(g[3][:, q, :], be[:, ch, hc], BrTs[:, q, :])
                else:
                    pass
            k.copy(BV[j3][:, hs, :], g[1][:, :, :], e="act")
            k.copy(BrV[j3][:, hs, :], g[2][:, :, :], e="act")
            yield
            cur = 0
            for lvl in range(1, 6):
                Xn, XTn = Xs[1 - cur], XTs[1 - cur]
                for q in range(8):
                    k.mm(g[0][:, q, :], XTs[cur][:, q, :], Xs[cur][:, q, :])
                if lvl < 5:
                    for q in range(8):
                        k.mm(g[4][:, q, :], Xs[cur][:, q, :], XTs[cur][:, q, :])
                yield
                k.copy(Xn[:, :, :], g[0][:, :, :], e="act")
                if lvl < 5:
                    k.copy(XTn[:, :, :], g[4][:, :, :], e="act")
                for q in range(8):
                    k.mm(g[1][:, q, :], Xn[:, q, :], V(TT[j3].h[:, hg * 8 + q, :], TT[j3].buf))
                yield
                k.tt(Q, Q, g[1][:, :, :], ALU.add)
                cur = 1 - cur

    def seq(cidx):
        n, ch = cidx // 2, cidx % 2
        i = n % 2
        j3 = cidx % NB
        cs = slice(ch * 64, ch * 64 + 64)
        kaT, rtT, beT, ktT = fm[i]
        be, kt, vv = tmb[i]
        y = Yt[i]
        for half in range(2):
            hs = slice(half * 8, half * 8 + 8)
            for q in range(8):
                h = half * 8 + q
                k.mm(s0[:, q, :], kaT[:, h, cs], G[:, h, :])
            yield
            k.stt(Zs[half][:, :, :], s0[:, :, :], -1.0, V(BV[j3].h[:, hs, :], BV[j3].buf), ALU.mult, ALU.subtract)
            for q in range(8):
                h = half * 8 + q
                k.mm(s0[:, q, :], V(TT[j3].h[:, h, :], TT[j3].buf), Zs[half][:, q, :])
            yield
            k.copy(Ps[half][:, :, :], s0[:, :, :], e="act")
            for q in range(8):
                h = half * 8 + q
                k.mm(s1[:, q, :], rtT[:, h, cs], G[:, h, :], start=True, stop=False)
                k.mm(s1[:, q, :], V(ArT[j3].h[:, h, :], ArT[j3].buf), Ps[half][:, q, :], start=False, stop=True)
            k.tt(V(y.h[:, ch, half * 512:(half + 1) * 512].rearrange("p (a b) -> p a b", b=64), y.buf),
                 s1[:, :, :], V(BrV[j3].h[:, hs, :], BrV[j3].buf), ALU.add)
            for q in range(8):
                h = half * 8 + q
                k.mm(s2[:, q, :], be[:, ch, h * 64:(h + 1) * 64], Ps[half][:, q, :])
            yield
            Gh = V(G.h[:, hs, :], G.buf)
            k.tt(Gh, Gh, s2[:, :, :], ALU.add)
            k.tt(Gh, Gh, V(KtV[j3].h[:, hs, :], KtV[j3].buf), ALU.add)
            k.tt(Gh, Gh, V(wc.h[:, hs, cidx:cidx + 1].broadcast_to([64, 8, 64]), wc.buf), ALU.mult)
        if ch == 1:
            k.dma(V(y_d.h[n * 128:(n + 1) * 128, :].rearrange("(c t) f -> t c f", t=64), y_d.buf), y[:, :, :])

    def interleave(ga, gb):
        live_a, live_b = ga is not None, gb is not None
        while live_a or live_b:
            for _ in range(4):
                if live_a:
                    try:
                        next(ga)
                    except StopIteration:
                        live_a = False
            if live_b:
                try:
                    next(gb)
                except StopIteration:
                    live_b = False

    nchunks = 2 * nt
    load(0)
    interleave(gram(0), None)
    for cidx in range(nchunks):
        ga = None
        if cidx + 1 < nchunks:
            if (cidx + 1) % 2 == 0:
                load((cidx + 1) // 2)
            ga = gram(cidx + 1)
        interleave(ga, seq(cidx))
    k.end_phase()
```
